# Optimizing a Trainium2 kernel written in Bass

```python
import jax, jax.numpy as jnp
from jax import lax
import numpy as np

D_MODEL = 4096
BATCH = 4
SEQ = 2048
DEPTH = 1
DEC_BATCH = 128
DEC_SEQ = 1
PAST_LEN = 16384
PAGE_SIZE = 128

POOL_WIDTH = D_MODEL // 2
POOL_WINDOWS = (2, 4, 8, 16)
N_POOL_GROUPS = len(POOL_WINDOWS)
POOL_GROUP = POOL_WIDTH // N_POOL_GROUPS
POOL_BUF = max(POOL_WINDOWS) - 1
RWKV_WIDTH = D_MODEL // 2
RWKV_HEAD_DIM = 64
RWKV_HEADS = RWKV_WIDTH // RWKV_HEAD_DIM
DECAY_LORA = max(32, int(round(RWKV_WIDTH ** 0.5 * 1.8 / 32)) * 32)
AAA_LORA = max(32, int(round(RWKV_WIDTH ** 0.5 * 1.8 / 32)) * 32)
SHIFT_WIDTH = 3 * RWKV_WIDTH + DECAY_LORA + AAA_LORA
GN_EPS = 64e-5
N_MEM = 256
MEM_HEADS = 4
MEM_WIDTH = 3 * D_MODEL // 8
MEM_HEAD_DIM = MEM_WIDTH // MEM_HEADS
IN_SPLITS = (POOL_WIDTH, POOL_WIDTH, SHIFT_WIDTH, RWKV_WIDTH, MEM_WIDTH, MEM_WIDTH, 3 * D_MODEL)
IN_COLS = sum(IN_SPLITS)
DEEPNORM_ALPHA = (2.0 * DEPTH) ** 0.25
DEEPNORM_BETA = (8.0 * DEPTH) ** -0.25
LN_EPS = 1e-5

kernel_name = "gated_pool_rwkv7_memattn_decoder_step"


def _split(z, sizes):
    out, start = [], 0
    for s in sizes:
        out.append(z[..., start:start + s])
        start += s
    return out


def _layernorm(x, g, b):
    xf = x.astype(jnp.float32)
    mu = jnp.mean(xf, -1, keepdims=True)
    var = jnp.mean(jnp.square(xf - mu), -1, keepdims=True)
    return (xf - mu) * lax.rsqrt(var + LN_EPS) * g + b


def _pool_branch(u, buf, pos0, pool_w, pool_scale):
    n, t, _ = u.shape
    uf = jnp.concatenate([buf.astype(jnp.float32), u.astype(jnp.float32)], axis=1)
    c = jnp.pad(jnp.cumsum(uf, axis=1), ((0, 0), (1, 0), (0, 0)))
    end = c[:, POOL_BUF + 1:]
    pos = pos0 + jnp.arange(t)
    means = []
    for g, win in enumerate(POOL_WINDOWS):
        sl = slice(g * POOL_GROUP, (g + 1) * POOL_GROUP)
        start = c[:, POOL_BUF + 1 - win:POOL_BUF + 1 - win + t, sl]
        count = jnp.minimum(pos + 1, win).astype(jnp.float32)[None, :, None]
        means.append((end[..., sl] - start) / count)
    pooled = (jnp.concatenate(means, -1) - uf[:, POOL_BUF:]).reshape(n, t, N_POOL_GROUPS, POOL_GROUP)
    mixed = jnp.einsum('btgc,gcd->btgd', pooled, pool_w.astype(jnp.float32)).reshape(n, t, POOL_WIDTH)
    return mixed * pool_scale, uf[:, -POOL_BUF:]


def _rwkv_branch(sh, shift_buf, state0, mu, w0, w2, a0, a2, k_k, k_a, r_k, ln_w, ln_b):
    f32 = jnp.float32
    n, t, _ = sh.shape
    sh = sh.astype(f32)
    prev = jnp.concatenate([shift_buf.astype(f32), sh[:, :-1]], axis=1)
    mixed = sh + (prev - sh) * mu
    r, k, v, wd, ad = _split(mixed, (RWKV_WIDTH, RWKV_WIDTH, RWKV_WIDTH, DECAY_LORA, AAA_LORA))
    w_log = -jax.nn.softplus(-(w0 + jnp.tanh(wd) @ w2)) - 0.5
    decay = jnp.exp(-jnp.exp(w_log))
    a = jax.nn.sigmoid(a0 + ad @ a2)
    heads = lambda z: z.reshape(n, t, RWKV_HEADS, RWKV_HEAD_DIM)
    kk = heads(k * k_k)
    kk = kk / jnp.maximum(jnp.sqrt(jnp.sum(kk * kk, -1, keepdims=True)), 1e-12)
    k = k * (1.0 + (a - 1.0) * k_a)
    r_h, k_h, v_h, w_h, a_h = heads(r), heads(k), heads(v), heads(decay), heads(a)

    def step(S, inp):
        r_t, w_t, k_t, v_t, kk_t, a_t = inp
        sa = jnp.einsum('bhij,bhj->bhi', S, -kk_t)
        S = (S * w_t[:, :, None, :] + sa[..., None] * (kk_t * a_t)[:, :, None, :]
             + v_t[..., None] * k_t[:, :, None, :])
        return S, jnp.einsum('bhij,bhj->bhi', S, r_t)

    xs = tuple(jnp.moveaxis(z, 1, 0) for z in (r_h, w_h, k_h, v_h, kk, a_h))
    S, ys = lax.scan(step, state0.astype(f32), xs)
    y = jnp.moveaxis(ys, 0, 1)
    ym = jnp.mean(y, -1, keepdims=True)
    yv = jnp.mean(jnp.square(y - ym), -1, keepdims=True)
    yn = ((y - ym) * lax.rsqrt(yv + GN_EPS)).reshape(n, t, RWKV_WIDTH) * ln_w + ln_b
    bonus = (jnp.sum(r_h * k_h * r_k, -1, keepdims=True) * v_h).reshape(n, t, RWKV_WIDTH)
    return yn + bonus, sh[:, -1:], S


def _project_memory(mem, w_mem_kv):
    n = mem.shape[0]
    mk, mv = _split(mem @ w_mem_kv, (MEM_WIDTH, MEM_WIDTH))
    return (mk.reshape(n, N_MEM, MEM_HEADS, MEM_HEAD_DIM),
            mv.reshape(n, N_MEM, MEM_HEADS, MEM_HEAD_DIM))


def _memory_attention(q, mem_k, mem_v):
    n, t, _ = q.shape
    qh = q.reshape(n, t, MEM_HEADS, MEM_HEAD_DIM).astype(jnp.float32)
    s = jnp.einsum('bthd,bmhd->bhtm', qh, mem_k.astype(jnp.float32)) * (MEM_HEAD_DIM ** -0.5)
    p = jax.nn.softmax(s, axis=-1)
    o = jnp.einsum('bhtm,bmhd->bthd', p, mem_v.astype(jnp.float32))
    return o.reshape(n, t, MEM_WIDTH)


def _layer(x, mem_k, mem_v, pool_buf, shift_buf, rwkv_state, pos0, lp):
    z = x @ lp['w_in']
    u_pool, z_pool, sh, z_rwkv, q_mem, z_mem, gate_in = _split(z, IN_SPLITS)
    pool_out, new_pool = _pool_branch(u_pool, pool_buf, pos0, lp['pool_w'], lp['pool_scale'])
    pool_out = pool_out * jax.nn.silu(z_pool.astype(jnp.float32))
    rwkv_out, new_shift, new_state = _rwkv_branch(
        sh, shift_buf, rwkv_state, lp['rwkv_mu'], lp['rwkv_w0'], lp['rwkv_w2'], lp['rwkv_a0'],
        lp['rwkv_a2'], lp['rwkv_k_k'], lp['rwkv_k_a'], lp['rwkv_r_k'], lp['rwkv_ln_w'], lp['rwkv_ln_b'])
    rwkv_out = rwkv_out * jax.nn.silu(z_rwkv.astype(jnp.float32))
    mem_out = _memory_attention(q_mem, mem_k, mem_v) * jax.nn.silu(z_mem.astype(jnp.float32))
    g_pool, g_rwkv, g_mem = _split(jax.nn.sigmoid(gate_in.astype(jnp.float32) + lp['b_gate']),
                                   (D_MODEL, D_MODEL, D_MODEL))
    dt = x.dtype
    h = (g_pool * (pool_out.astype(dt) @ lp['w_branch_pool'])
         + g_rwkv * (rwkv_out.astype(dt) @ lp['w_branch_rwkv'])
         + g_mem * (mem_out.astype(dt) @ lp['w_branch_mem']))
    sub = h.astype(dt) @ lp['w_out']
    y = _layernorm(DEEPNORM_ALPHA * x.astype(jnp.float32) + sub, lp['ln_g'], lp['ln_b'])
    return y.astype(dt), new_pool, new_shift, new_state


def setup_inputs(seed: int = 0) -> dict:
    key = jax.random.key(seed)
    ks = jax.random.split(key, 40)
    f32 = jnp.float32
    nrm = lambda k, shape, s: (jax.random.normal(k, shape, f32) * s)
    lin = jnp.linspace(0.0, 1.0, RWKV_WIDTH, dtype=f32)[None, :]
    return {
        'x_prompt': nrm(ks[0], (BATCH, SEQ, D_MODEL), 1.0),
        'x_sample': nrm(ks[1], (DEC_BATCH, DEC_SEQ, D_MODEL), 1.0),
        'cache_mem_k': nrm(ks[2], (DEPTH, DEC_BATCH, N_MEM, MEM_HEADS, MEM_HEAD_DIM), 1.0),
        'cache_mem_v': nrm(ks[3], (DEPTH, DEC_BATCH, N_MEM, MEM_HEADS, MEM_HEAD_DIM), 1.0),
        'state_pool': nrm(ks[4], (DEPTH, DEC_BATCH, POOL_BUF, POOL_WIDTH), 1.0),
        'state_shift': nrm(ks[5], (DEPTH, DEC_BATCH, 1, SHIFT_WIDTH), 1.0),
        'state_rwkv': nrm(ks[6], (DEPTH, DEC_BATCH, RWKV_HEADS, RWKV_HEAD_DIM, RWKV_HEAD_DIM), 0.5),
        'mem_prompt': nrm(ks[7], (BATCH, N_MEM, D_MODEL), 1.0),
        'w_in': nrm(ks[8], (DEPTH, D_MODEL, IN_COLS), D_MODEL ** -0.5),
        'b_gate': nrm(ks[9], (DEPTH, 3 * D_MODEL), 0.01),
        'pool_w': nrm(ks[10], (DEPTH, N_POOL_GROUPS, POOL_GROUP, POOL_GROUP), POOL_GROUP ** -0.5),
        'pool_scale': 1.0 + nrm(ks[11], (DEPTH, POOL_WIDTH), 0.01),
        'rwkv_mu': jax.random.uniform(ks[12], (DEPTH, SHIFT_WIDTH), f32),
        'rwkv_w0': (-7.0 + 5.0 * lin ** 0.85 + 0.5) + nrm(ks[13], (DEPTH, RWKV_WIDTH), 0.01),
        'rwkv_w2': nrm(ks[14], (DEPTH, DECAY_LORA, RWKV_WIDTH), 0.1 * DECAY_LORA ** -0.5),
        'rwkv_a0': nrm(ks[15], (DEPTH, RWKV_WIDTH), 0.01),
        'rwkv_a2': nrm(ks[16], (DEPTH, AAA_LORA, RWKV_WIDTH), 0.1 * AAA_LORA ** -0.5),
        'rwkv_k_k': 0.85 + nrm(ks[17], (DEPTH, RWKV_WIDTH), 0.01),
        'rwkv_k_a': 1.0 + nrm(ks[18], (DEPTH, RWKV_WIDTH), 0.01),
        'rwkv_r_k': nrm(ks[19], (DEPTH, RWKV_HEADS, RWKV_HEAD_DIM), 0.1),
        'rwkv_ln_w': 1.0 + nrm(ks[20], (DEPTH, RWKV_WIDTH), 0.01),
        'rwkv_ln_b': nrm(ks[21], (DEPTH, RWKV_WIDTH), 0.01),
        'w_mem_kv': nrm(ks[22], (DEPTH, D_MODEL, 2 * MEM_WIDTH), D_MODEL ** -0.5),
        'w_branch_pool': nrm(ks[23], (DEPTH, POOL_WIDTH, D_MODEL), DEEPNORM_BETA * POOL_WIDTH ** -0.5),
        'w_branch_rwkv': nrm(ks[24], (DEPTH, RWKV_WIDTH, D_MODEL), DEEPNORM_BETA * RWKV_WIDTH ** -0.5),
        'w_branch_mem': nrm(ks[25], (DEPTH, MEM_WIDTH, D_MODEL), DEEPNORM_BETA * MEM_WIDTH ** -0.5),
        'w_out': nrm(ks[26], (DEPTH, D_MODEL, D_MODEL), DEEPNORM_BETA * D_MODEL ** -0.5),
        'ln_g': 1.0 + nrm(ks[27], (DEPTH, D_MODEL), 0.01),
        'ln_b': nrm(ks[28], (DEPTH, D_MODEL), 0.01),
    }


def reference(x_prompt, x_sample, cache_mem_k, cache_mem_v, state_pool, state_shift, state_rwkv,
              mem_prompt, w_in, b_gate, pool_w, pool_scale, rwkv_mu, rwkv_w0, rwkv_w2, rwkv_a0,
              rwkv_a2, rwkv_k_k, rwkv_k_a, rwkv_r_k, rwkv_ln_w, rwkv_ln_b, w_mem_kv,
              w_branch_pool, w_branch_rwkv, w_branch_mem, w_out, ln_g, ln_b):
    f32 = jnp.float32
    yp, ys = x_prompt, x_sample
    nb = x_prompt.shape[0]
    mk_l, mv_l, pp_l, shp_l, sp_l, ps_l, shs_l, ss_l = [], [], [], [], [], [], [], []
    for l in range(DEPTH):
        lp = dict(w_in=w_in[l], b_gate=b_gate[l], pool_w=pool_w[l], pool_scale=pool_scale[l],
                  rwkv_mu=rwkv_mu[l], rwkv_w0=rwkv_w0[l], rwkv_w2=rwkv_w2[l], rwkv_a0=rwkv_a0[l],
                  rwkv_a2=rwkv_a2[l], rwkv_k_k=rwkv_k_k[l], rwkv_k_a=rwkv_k_a[l],
                  rwkv_r_k=rwkv_r_k[l], rwkv_ln_w=rwkv_ln_w[l], rwkv_ln_b=rwkv_ln_b[l],
                  w_branch_pool=w_branch_pool[l], w_branch_rwkv=w_branch_rwkv[l],
                  w_branch_mem=w_branch_mem[l], w_out=w_out[l], ln_g=ln_g[l], ln_b=ln_b[l])
        mk_p, mv_p = _project_memory(mem_prompt, w_mem_kv[l])
        yp, pool_p, shift_p, st_p = _layer(
            yp, mk_p, mv_p,
            jnp.zeros((nb, POOL_BUF, POOL_WIDTH), f32),
            jnp.zeros((nb, 1, SHIFT_WIDTH), f32),
            jnp.zeros((nb, RWKV_HEADS, RWKV_HEAD_DIM, RWKV_HEAD_DIM), f32),
            0, lp)
        ys, pool_s, shift_s, st_s = _layer(
            ys, cache_mem_k[l], cache_mem_v[l], state_pool[l], state_shift[l], state_rwkv[l],
            PAST_LEN, lp)
        mk_l.append(mk_p); mv_l.append(mv_p); pp_l.append(pool_p); shp_l.append(shift_p)
        sp_l.append(st_p); ps_l.append(pool_s); shs_l.append(shift_s); ss_l.append(st_s)
    return (yp, ys, jnp.stack(mk_l), jnp.stack(mv_l), jnp.stack(pp_l), jnp.stack(shp_l),
            jnp.stack(sp_l), jnp.stack(ps_l), jnp.stack(shs_l), jnp.stack(ss_l))
```

```python
import contextlib
import os
import numpy as np
import concourse.bass as bass
import concourse.mybir as mybir
from concourse.bass_utils import run_bass_kernel_spmd

F32 = mybir.dt.float32
BF16 = mybir.dt.bfloat16
AF = mybir.ActivationFunctionType
ALU = mybir.AluOpType
AX = mybir.AxisListType

ENGS = ['tensor', 'vector', 'scalar', 'gpsimd', 'sync']

D = 4096
PW = 2048
RW = 2048
LORA = 96
SHW = 3 * RW + 2 * LORA
MW = 1536
MHD = 384
NMEM = 256
INC = 27840
C_U, C_ZP, C_R, C_K, C_V, C_WD, C_AD, C_ZR, C_Q, C_ZM, C_G = 0, 2048, 4096, 6144, 8192, 10240, 10336, 10432, 12480, 14016, 15552
GN_EPS = 64e-5
LN_EPS = 1e-5
ALPHA = 2.0 ** 0.25
NS = 16
HALO = 16


class T:
    __slots__ = ('name', 'last_w', 'readers', 'excl')

    def __init__(self, name='', excl=False):
        self.name = name
        self.last_w = None
        self.readers = []
        self.excl = excl


class Sched:
    def __init__(self, nc, n_dma_sems=40, strict=('vector', 'scalar', 'gpsimd')):
        self.nc = nc
        self.prog = {e: [] for e in ENGS}
        self.cnt = {e: 0 for e in ENGS}
        self.waited = {e: {} for e in ENGS}
        self.strict = set(strict)
        self.n_dma_sems = n_dma_sems
        self.dma_cnt = [0] * n_dma_sems
        self.dma_rr = 0
        self.sems = {}
        self.ninstr = 0
        self.relay = None

    def alloc_sems(self, stack):
        for e in ENGS:
            self.sems[e] = stack.enter_context(self.nc.semaphore('s_' + e))
        for i in range(self.n_dma_sems):
            self.sems[('d', i)] = stack.enter_context(self.nc.semaphore('d%d' % i))

    def _wait(self, eng, key, val):
        if val <= 0:
            return
        w = self.waited[eng]
        if w.get(key, 0) >= val:
            return
        w[key] = val
        sem = self.sems[key]
        self.prog[eng].append(lambda e, sem=sem, val=val: e.wait_ge(sem, val))
        self.ninstr += 1

    def _deps(self, eng, reads, writes):
        deps = []
        for t in reads:
            if t.last_w is not None:
                deps.append(t.last_w)
            if t.excl:
                deps.extend(r for r in t.readers if r[0] != eng)
        for t in writes:
            if t.last_w is not None:
                deps.append(t.last_w)
            deps.extend(t.readers)
        for key, val in deps:
            if key == eng and eng not in self.strict:
                continue
            if eng == 'gpsimd' and key == 'tensor' and self.relay is not None:
                if self.waited[eng].get(key, 0) >= val:
                    continue
                self.waited[eng][key] = val
                self._wait('vector', key, val)
                self.cnt['vector'] += 1
                v2 = self.cnt['vector']
                rl = self.relay
                sem = self.sems['vector']
                self.prog['vector'].append(lambda e, rl=rl, sem=sem: e.memset(rl, 0.0).then_inc(sem, 1))
                self.ninstr += 1
                self._wait(eng, 'vector', v2)
                continue
            self._wait(eng, key, val)

    def op(self, eng, fn, reads=(), writes=()):
        self._deps(eng, reads, writes)
        self.cnt[eng] += 1
        val = self.cnt[eng]
        sem = self.sems[eng]
        self.prog[eng].append(lambda e, fn=fn, sem=sem: fn(e).then_inc(sem, 1))
        self.ninstr += 1
        for t in writes:
            t.last_w = (eng, val)
            t.readers = []
        for t in reads:
            if not any(t is w for w in writes):
                t.readers.append((eng, val))
                if len(t.readers) > 24:
                    self._compress(t)

    @staticmethod
    def _compress(t):
        m = {}
        for k, v in t.readers:
            if m.get(k, 0) < v:
                m[k] = v
        t.readers = list(m.items())

    def dma(self, eng, out, in_, reads=(), writes=(), **kw):
        i = self.dma_rr
        self.dma_rr = (self.dma_rr + 1) % self.n_dma_sems
        key = ('d', i)
        self._wait(eng, key, self.dma_cnt[i])
        self._deps(eng, reads, writes)
        self.dma_cnt[i] += 16
        val = self.dma_cnt[i]
        sem = self.sems[key]
        self.prog[eng].append(
            lambda e, out=out, in_=in_, sem=sem, kw=kw: e.dma_start(out=out, in_=in_, **kw).then_inc(sem, 16))
        self.ninstr += 1
        for t in writes:
            t.last_w = (key, val)
            t.readers = []
        for t in reads:
            t.readers.append((key, val))
            if len(t.readers) > 24:
                self._compress(t)

    def finish(self, eng='sync'):
        for i in range(self.n_dma_sems):
            self._wait(eng, ('d', i), self.dma_cnt[i])
        for e in ENGS:
            if e != eng:
                self._wait(eng, e, self.cnt[e])

    def emit(self, block):
        for e in ENGS:
            lst = self.prog[e]

            def body(engine, lst=lst):
                for th in lst:
                    th(engine)
            getattr(block, e)(body)


class Arena:
    def __init__(self, tensor, nelem):
        self.t = tensor
        self.n = nelem
        self.off = 0
        self.tiles = []
        self.pending = []

    def reset(self):
        m = {}
        for k, v in self.pending:
            if m.get(k, 0) < v:
                m[k] = v
        for t in self.tiles:
            acc = list(t.readers)
            if t.last_w is not None:
                acc.append(t.last_w)
            for k, v in acc:
                if m.get(k, 0) < v:
                    m[k] = v
        self.pending = list(m.items())
        self.tiles = []
        self.off = 0

    def mark(self):
        return (self.off, len(self.tiles))

    def release(self, mk):
        off, nt = mk
        m = {}
        for k, v in self.pending:
            if m.get(k, 0) < v:
                m[k] = v
        for t in self.tiles[nt:]:
            acc = list(t.readers)
            if t.last_w is not None:
                acc.append(t.last_w)
            for k, v in acc:
                if m.get(k, 0) < v:
                    m[k] = v
        self.pending = list(m.items())
        self.tiles = self.tiles[:nt]
        self.off = off

    def alloc(self, shape, dtype=F32, name=''):
        n = 1
        for s in shape[1:]:
            n *= s
        n32 = n if dtype == F32 else (n + 1) // 2
        n32 = (n32 + 7) // 8 * 8
        assert self.off + n32 <= self.n, "arena overflow %s %d+%d>%d" % (name, self.off, n32, self.n)
        v = self.t[:, self.off:self.off + n32]
        self.off += n32
        if dtype != F32:
            v = v.bitcast(dtype)
        v = v[0:shape[0], 0:n]
        if len(shape) == 3:
            v = v.rearrange("p (a b) -> p a b", b=shape[2])
        elif len(shape) == 4:
            v = v.rearrange("p (a b c) -> p a b c", b=shape[2], c=shape[3])
        t = T(name)
        t.readers = list(self.pending)
        self.tiles.append(t)
        return v, t


def make_consts(TP):
    c = {}
    c['ident'] = np.eye(128, dtype=np.float32)
    p = np.arange(128)
    hh = p // 64
    ss = p % 64
    bd = (hh[:, None] == hh[None, :]).astype(np.float32)
    c['bones'] = bd.copy()
    mS = bd * (ss[:, None] < ss[None, :])
    mST = bd * (ss[:, None] > ss[None, :])
    c['maskS'] = np.tile(mS, (1, 4)).astype(np.float32)
    c['maskST'] = np.tile(mST, (1, 4)).astype(np.float32)
    mI = (ss[:, None] <= np.arange(64)[None, :]).astype(np.float32)
    c['maskI'] = np.tile(mI, (1, 8)).astype(np.float32)
    rm = np.ones((128, TP), np.float32)
    rm[:, ::64] = 0.0
    c['resetm'] = rm
    sel = np.zeros((16, 16, 128), np.float32)
    for n in range(16):
        sel[n, n, :] = 1.0
    c['sel'] = sel.reshape(16, 16 * 128)
    c['i2'] = np.tile(np.eye(64, dtype=np.float32), (2, 1))
    return c


def build(T_OWN, TP, dbg=False):
    NPH = T_OWN // TP
    NPASS = 2 * NPH
    W = HALO + TP + NS
    NCH = TP // 64
    OWN = slice(HALO, HALO + TP)
    SMP = slice(HALO + TP, HALO + TP + NS)
    OS = slice(HALO, HALO + TP + NS)
    NOS = TP + NS
    nc = bass.Bass("TRN2", target_bir_lowering=False)

    def din(name, shape):
        return nc.dram_tensor(name, list(shape), F32, kind="ExternalInput").ap()

    def dout(name, shape):
        return nc.dram_tensor(name, list(shape), F32, kind="ExternalOutput").ap()

    xin = din("xin", [HALO + 2 * T_OWN, D])
    xs = din("xs", [NS, D])
    pos = din("pos", [1, T_OWN])
    memx = din("memx", [NMEM, D])
    ck = din("ck", [NS, NMEM, MW])
    cv = din("cv", [NS, NMEM, MW])
    spool = din("spool", [NS, 15, PW])
    sshift = din("sshift", [NS, SHW])
    srwkv = din("srwkv", [NS, 32, 64, 64])
    w_in = din("w_in", [D, INC])
    b_gate = din("b_gate", [3 * D])
    pool_w = din("pool_w", [4, 512, 512])
    pool_scale = din("pool_scale", [PW])
    mu = din("rwkv_mu", [SHW])
    w0 = din("rwkv_w0", [RW])
    w2 = din("rwkv_w2", [LORA, RW])
    a0 = din("rwkv_a0", [RW])
    a2 = din("rwkv_a2", [LORA, RW])
    k_k = din("rwkv_k_k", [RW])
    k_a = din("rwkv_k_a", [RW])
    r_k = din("rwkv_r_k", [RW])
    gln_w = din("rwkv_ln_w", [RW])
    gln_b = din("rwkv_ln_b", [RW])
    w_mem_kv = din("w_mem_kv", [D, 2 * MW])
    wb_pool = din("w_branch_pool", [PW, D])
    wb_rwkv = din("w_branch_rwkv", [RW, D])
    wb_mem = din("w_branch_mem", [MW, D])
    w_out = din("w_out", [D, D])
    ln_g = din("ln_g", [1, D])
    ln_b = din("ln_b", [1, D])
    cshapes = {k: v.shape for k, v in make_consts(TP).items()}
    cin = {k: din("c_" + k, cshapes[k]) for k in cshapes}

    y_own = dout("y_own", [T_OWN, D])
    y_s = dout("y_s", [NS, D])
    o_mk = dout("o_mk", [NMEM, MW])
    o_mv = dout("o_mv", [NMEM, MW])
    o_tailu = dout("o_tailu", [32, PW])
    o_tailsh = dout("o_tailsh", [32, SHW])
    o_pools = dout("o_pools", [NS, 14, PW])
    o_rwkvp = dout("o_rwkvp", [32, 64, 64])
    o_rwkvs = dout("o_rwkvs", [NS, 32, 64, 64])
    KDBG = bool(os.environ.get('KDBG'))
    dbg = dout("dbg", [128, 76 * 32]) if KDBG else None

    st = contextlib.ExitStack()
    with st:
        S = Sched(nc)
        S.alloc_sems(st)

        def sb(name, shape, dt=F32):
            return st.enter_context(nc.sbuf_tensor(name, list(shape), dt))

        xT = sb("xT", [128, 32, W], BF16); t_xT = T('xT')
        hT = sb("hT", [128, 32, W], BF16); t_hT = [T('hT%d' % i) for i in range(32)]
        bo = sb("bo", [128, 16, W], BF16); t_bo = [T('bo%d' % i) for i in range(16)]
        NWB = 3
        wch = [sb("wch%d" % i, [128, 32 * 128], BF16) for i in range(NWB)]
        t_wch = [T('wch%d' % i) for i in range(NWB)]
        KT = sb("KT", [128, 12, NMEM], BF16); t_KT = T('KT')
        Vb = sb("Vb", [128, 2, MW], BF16); t_Vb = T('Vb')
        ident = sb("ident", [128, 128]); t_c = T('consts')
        identb = sb("identb", [128, 128], BF16)
        bones = sb("bones", [128, 128])
        bones64 = sb("bones64", [128, 128])
        onesb = sb("onesb", [128, 128], BF16)
        maskS = sb("maskS", [128, 512])
        maskST = sb("maskST", [128, 512])
        maskI = sb("maskI", [128, 512])
        resetm = sb("resetm", [128, TP])
        sel = sb("sel", [16, 16 * 128])
        i2 = sb("i2", [128, 64])
        prm = {}
        relay_t = sb("relay", [128, 8])
        S.relay = relay_t[:]
        H32 = sb("H32", [128, 16, 64]); t_H = [T('H%d' % i) for i in range(16)]
        Hbf = sb("Hbf", [128, 16, 64], BF16)
        Hbd = sb("Hbd", [128, 16, 128], BF16)
        pwb = sb("pwb", [128, 4 * 512], BF16); t_pwb = T('pwb')
        w2b = sb("w2b", [LORA, RW], BF16)
        a2b = sb("a2b", [LORA, RW], BF16)
        ARN = 17408
        arena_t = sb("arena", [128, ARN])
        AR = Arena(arena_t, ARN)
        pst = st.enter_context(nc.psum_tensor("ps", [128, 8 * 512], F32))
        ps = [pst[:, i * 512:(i + 1) * 512] for i in range(8)]
        psb = [pst[:, i * 512:(i + 1) * 512].bitcast(BF16) for i in range(8)]
        t_ps = [T('ps%d' % i, excl=True) for i in range(8)]
        psrr = [0]
        block = st.enter_context(nc.Block())

        def bank():
            i = psrr[0]
            psrr[0] = (i + 1) % 8
            return i

        dq = [0]

        def hwq():
            dq[0] ^= 1
            if os.environ.get('KQ'):
                return 'sync'
            return 'sync' if dq[0] else 'scalar'

        for nm, tl in [('ident', ident), ('bones', bones), ('maskS', maskS), ('maskST', maskST),
                       ('maskI', maskI), ('resetm', resetm), ('sel', sel), ('i2', i2)]:
            S.dma('sync', tl[:], cin[nm][:, :], writes=[t_c])
        KB = int(os.environ.get('KB', '99'))
        if KB >= 2:
          S.op('vector', lambda e: e.tensor_copy(out=identb[:], in_=ident[:]), reads=[t_c], writes=[t_c])
        if KB >= 2:
          S.op('vector', lambda e: e.tensor_scalar(out=bones64[:], in0=bones[:], scalar1=1.0 / 64, scalar2=None,
                                                 op0=ALU.mult), reads=[t_c], writes=[t_c])
        if KB >= 2:
          S.op('vector', lambda e: e.memset(onesb[:], 1.0), writes=[t_c])
        if KB >= 3:
          S.dma('gpsimd', w2b[:], w2[:, :], writes=[t_c])
          S.dma('gpsimd', a2b[:], a2[:, :], writes=[t_c])

        def ptile(name, src, n, ntile, kp=128):
            if KB < (5 if kp != 128 else 4):
                prm[name] = None
                return
            tl = sb("p_" + name, [kp, ntile])
            stg_ = sb("ps_" + name, [ntile, kp])
            t_stg_ = T()
            S.dma('sync', stg_[:], src.rearrange("(c p) -> c p", p=kp), writes=[t_stg_])
            bi = bank()
            S.op('tensor', lambda e, bi=bi, stg_=stg_: e.transpose(ps[bi][0:kp, 0:ntile], stg_[:, :], ident[0:ntile, 0:ntile]),
                 reads=[t_stg_, t_c], writes=[t_ps[bi]])
            S.op('vector', lambda e, bi=bi, tl=tl: e.tensor_copy(out=tl[:, :], in_=ps[bi][0:kp, 0:ntile]),
                 reads=[t_ps[bi]], writes=[t_c])
            prm[name] = tl

        ptile('mu_r', mu[0:RW], RW, 16)
        ptile('mu_k', mu[RW:2 * RW], RW, 16)
        ptile('mu_v', mu[2 * RW:3 * RW], RW, 16)
        for nm, src in [('w0', w0), ('a0', a0), ('k_k', k_k), ('k_a', k_a), ('r_k', r_k), ('gln_w', gln_w),
                        ('gln_b', gln_b), ('pscale', pool_scale)]:
            ptile(nm, src, RW, 16)
        ptile('b_gate', b_gate, 3 * D, 96)
        ptile('mul', mu[3 * RW:3 * RW + 2 * LORA], 2 * LORA, 2, kp=LORA)
        mul = prm['mul']
        omka = sb("p_omka", [128, 16])
        if KB >= 6:
          S.op('vector', lambda e: e.tensor_scalar(out=omka[:], in0=prm['k_a'][:], scalar1=-1.0, scalar2=1.0,
                                                 op0=ALU.mult, op1=ALU.add), reads=[t_c], writes=[t_c])
        if KB >= 7:
          S.op('vector', lambda e: e.memset(H32[:], 0.0), writes=t_H)
          S.op('vector', lambda e: e.memset(Hbf[:], 0.0), writes=t_H)
          S.op('vector', lambda e: e.memset(Hbd[:], 0.0), writes=t_H)

        wrr = [0]

        def wload(src_ap, K, ncols):
            i = wrr[0]
            wrr[0] = (i + 1) % NWB
            if K % 128 == 0:
                KC = K // 128
                v = wch[i][:, 0:KC * ncols].rearrange("p (a b) -> p a b", b=ncols)
                S.dma('gpsimd', v, src_ap.rearrange("(c p) n -> p c n", p=128), writes=[t_wch[i]])
                return v, t_wch[i], 128, KC
            raise AssertionError

        def proj(wv, wt, KC, c0, c1, rhs3, rhs_ts, cols, bi=None):
            if bi is None:
                bi = bank()
            n = cols.stop - cols.start
            for kc in range(KC):
                S.op('tensor', lambda e, kc=kc: e.matmul(ps[bi][0:c1 - c0, 0:n], wv[:, kc, c0:c1], rhs3[:, kc, cols],
                                                        start=(kc == 0), stop=(kc == KC - 1)),
                     reads=[wt] + list(rhs_ts), writes=[t_ps[bi]])
            return bi

        def inproj(col0, ncols=128, cols=slice(0, W)):
            wv, wt, kp, KC = wload(w_in[:, col0:col0 + ncols], D, ncols)
            return wv, wt, KC

        def load_xT(row0, nrows, col0, src):
            r = 0
            while r < nrows:
                nr = min(128, nrows - r)
                for hlf in range(2):
                    if AR.off + 2048 > AR.n:
                        AR.reset()
                    xr, t_xr = AR.alloc([128, 2048], F32, 'xrow')
                    S.dma(hwq(), xr[0:nr, :], src[row0 + r:row0 + r + nr, hlf * 2048:(hlf + 1) * 2048], writes=[t_xr])
                    for g4 in range(4):
                        bi = bank()
                        for j in range(4):
                            cc = g4 * 4 + j
                            S.op('tensor', lambda e, cc=cc, j=j, bi=bi, nr=nr, xr=xr: e.transpose(
                                ps[bi][:, j * 128:j * 128 + nr], xr[0:nr, cc * 128:(cc + 1) * 128], ident[0:nr, 0:nr]),
                                reads=[t_xr, t_c], writes=[t_ps[bi]])
                        kc0 = hlf * 16 + g4 * 4
                        S.op('vector' if g4 % 2 == 0 else 'scalar',
                             (lambda e, bi=bi, kc0=kc0, nr=nr, c=col0 + r: e.tensor_copy(
                                 out=xT[:, kc0:kc0 + 4, c:c + nr],
                                 in_=ps[bi][:, :].rearrange("p (a b) -> p a b", b=128)[:, :, 0:nr]))
                             if g4 % 2 == 0 else
                             (lambda e, bi=bi, kc0=kc0, nr=nr, c=col0 + r: e.activation(
                                 out=xT[:, kc0:kc0 + 4, c:c + nr],
                                 in_=ps[bi][:, :].rearrange("p (a b) -> p a b", b=128)[:, :, 0:nr], func=AF.Copy)),
                             reads=[t_ps[bi]], writes=[t_xT])
                r += nr

        AR.reset()
        memT = hT[:, :, :].rearrange("p a b -> p (a b)")[:, 0:32 * NMEM].rearrange("p (a b) -> p a b", b=NMEM)

        def load_memT():
            for r in range(2):
                for hlf in range(2):
                    xr, t_xr = AR.alloc([128, 2048], F32, 'mrow')
                    S.dma(hwq(), xr[:, :], memx[r * 128:(r + 1) * 128, hlf * 2048:(hlf + 1) * 2048], writes=[t_xr])
                    for g4 in range(4):
                        bi = bank()
                        for j in range(4):
                            cc = g4 * 4 + j
                            S.op('tensor', lambda e, cc=cc, j=j, bi=bi, xr=xr: e.transpose(
                                ps[bi][:, j * 128:(j + 1) * 128], xr[:, cc * 128:(cc + 1) * 128], ident[:, :]),
                                reads=[t_xr, t_c], writes=[t_ps[bi]])
                        kc0 = hlf * 16 + g4 * 4
                        S.op('vector', lambda e, bi=bi, kc0=kc0, r=r: e.tensor_copy(
                            out=memT[:, kc0:kc0 + 4, r * 128:(r + 1) * 128],
                            in_=ps[bi][:, :].rearrange("p (a b) -> p a b", b=128)), reads=[t_ps[bi]], writes=t_hT)
        KSTOP0 = int(os.environ.get('KSTOP', '99'))
        if KSTOP0 > 0:
            load_memT()
        for c in range(int(os.environ.get('KC', '24')) if KSTOP0 > 0 else 0):
            wv, wt, kp, KC = wload(w_mem_kv[:, c * 128:(c + 1) * 128], D, 128)
            bi = proj(wv, wt, KC, 0, 128, memT, t_hT, slice(0, NMEM))
            kv32, t_kv32 = AR.alloc([128, NMEM], F32, 'kv32')
            S.op('vector', lambda e, bi=bi, kv32=kv32: e.tensor_copy(out=kv32[:, :], in_=ps[bi][:, 0:NMEM]),
                 reads=[t_ps[bi]], writes=[t_kv32])
            if c < 12:
                S.op('scalar', lambda e, bi=bi, c=c: e.activation(out=KT[:, c, :], in_=ps[bi][:, 0:NMEM], func=AF.Copy),
                     reads=[t_ps[bi]], writes=[t_KT])
            b2 = bank()
            for mt in range(2):
                S.op('tensor', lambda e, mt=mt, b2=b2, kv32=kv32: e.transpose(
                    ps[b2][:, mt * 128:(mt + 1) * 128], kv32[:, mt * 128:(mt + 1) * 128], ident[:, :]),
                    reads=[t_kv32, t_c], writes=[t_ps[b2]])
            kvt, t_kvt = AR.alloc([128, 2, 128], F32, 'kvt')
            S.op('vector', lambda e, b2=b2, kvt=kvt: e.tensor_copy(
                out=kvt[:, :, :], in_=ps[b2][:, 0:256].rearrange("p (a b) -> p a b", b=128)),
                reads=[t_ps[b2]], writes=[t_kvt])
            if c >= 12:
                S.op('scalar', lambda e, b2=b2, c=c: e.activation(
                    out=Vb[:, :, (c - 12) * 128:(c - 11) * 128],
                    in_=ps[b2][:, 0:256].rearrange("p (a b) -> p a b", b=128), func=AF.Copy),
                    reads=[t_ps[b2]], writes=[t_Vb])
            dst = o_mk if c < 12 else o_mv
            cc = c % 12
            if not os.environ.get('KO'):
                S.dma(hwq(), dst[:, cc * 128:(cc + 1) * 128].rearrange("(a p) n -> p a n", p=128), kvt[:, :, :],
                      reads=[t_kvt], writes=[T()])
            if c % 6 == 5:
                AR.reset()
        AR.reset()

        t_vscr = [T('v%d' % i) for i in range(NPH * (TP // 128) + 1)]

        def run_pass(p):
            main = p >= NPH
            last = p == NPASS - 1
            cur_last[0] = last
            mp = p - NPH
            AR.reset()
            load_xT(p * TP, HALO + TP, 0, xin)
            load_xT(0, NS, HALO + TP, xs)
            AR.reset()
            chk(2)
            rwkv_stage(p, main, last)
            chk(3)
            if main:
                chk(4)
                branch_proj(wb_rwkv, RW, 1, first=True)
                chk(5)
                pool_stage(p, last)
                branch_proj(wb_pool, PW, 0, first=False)
                chk(6)
                mem_stage(p, last)
                branch_proj(wb_mem, MW, 2, first=False)
                chk(7)
                final_stage(p, last)

        cur_last = [False]

        def dump(src3, ntile, tls, off):
            if not (KDBG and cur_last[0]):
                return
            stg, t_stg = AR.alloc([128, ntile, 32], F32, 'dbgstg')
            S.op('vector', lambda e: e.tensor_copy(out=stg[:, :, :], in_=src3[:, 0:ntile, HALO + TP - 16:HALO + TP + 16]),
                 reads=tls, writes=[t_stg])
            S.dma('sync', dbg[:, off * 32:(off + ntile) * 32], stg[:, :, :].rearrange("p a b -> p (a b)"), reads=[t_stg],
                  writes=[T()])

        def branch_proj(wb, Kb, gi, first):
            AR.reset()
            KCb = Kb // 128
            dump(bo, KCb, t_bo[0:KCb], {1: 0, 0: 16, 2: 32}[gi])
            for c in range(32):
                wv, wt, kp, KC = wload(wb[:, c * 128:(c + 1) * 128], Kb, 128)
                bp = proj(wv, wt, KC, 0, 128, bo, t_bo[0:KCb], slice(0, W))
                gv, gt, gKC = inproj(C_G + gi * D + c * 128)
                bg = proj(gv, gt, gKC, 0, 128, xT, [t_xT], slice(0, W))
                g, t_g = AR.alloc([128, W], F32, 'g')
                S.op('scalar', lambda e, bg=bg, g=g, gi=gi, c=c: e.activation(
                    out=g[:, :], in_=ps[bg][:, 0:W], func=AF.Sigmoid,
                    bias=prm['b_gate'][:, gi * 32 + c:gi * 32 + c + 1]), reads=[t_ps[bg], t_c], writes=[t_g])
                if first:
                    S.op('vector', lambda e, bp=bp, g=g, c=c: e.tensor_tensor(
                        out=hT[:, c, :], in0=ps[bp][:, 0:W], in1=g[:, :], op=ALU.mult),
                        reads=[t_ps[bp], t_g], writes=[t_hT[c]])
                else:
                    tmp, t_tmp = AR.alloc([128, W], F32, 'gtmp')
                    S.op('vector', lambda e, bp=bp, g=g, tmp=tmp: e.tensor_tensor(
                        out=tmp[:, :], in0=ps[bp][:, 0:W], in1=g[:, :], op=ALU.mult),
                        reads=[t_ps[bp], t_g], writes=[t_tmp])
                    S.op('gpsimd', lambda e, tmp=tmp, c=c: e.tensor_tensor(
                        out=hT[:, c, :], in0=hT[:, c, :], in1=tmp[:, :], op=ALU.add),
                        reads=[t_tmp, t_hT[c]], writes=[t_hT[c]])
                if c % 4 == 3:
                    AR.reset()

        def tail_out(src32, t_src, dst, col0):
            bi = bank()
            S.op('tensor', lambda e, bi=bi: e.transpose(ps[bi][0:32, 0:128], src32[:, HALO + TP - 16:HALO + TP + 16],
                                                        ident[:, :]), reads=[t_src, t_c], writes=[t_ps[bi]])
            tl, t_tl = AR.alloc([32, 128], F32, 'tail')
            S.op('scalar', lambda e, bi=bi, tl=tl: e.activation(out=tl[:, :], in_=ps[bi][0:32, 0:128], func=AF.Copy),
                 reads=[t_ps[bi]], writes=[t_tl])
            S.dma(hwq(), dst[:, col0:col0 + 128], tl[:, :], reads=[t_tl], writes=[T()])

        def rwkv_stage(p, main, last):
            AR.reset()
            lmix = []
            shl = None
            for li in range(2):
                wv, wt, KC = inproj(C_WD + li * LORA, LORA)
                bi = proj(wv, wt, KC, 0, LORA, xT, [t_xT], slice(0, W))
                raw, t_raw = AR.alloc([LORA, W], F32, 'lraw')
                S.op('vector', lambda e, bi=bi, raw=raw: e.tensor_copy(out=raw[:, :], in_=ps[bi][0:LORA, 0:W]),
                     reads=[t_ps[bi]], writes=[t_raw])
                if last:
                    b2 = bank()
                    S.op('tensor', lambda e, b2=b2, raw=raw: e.transpose(
                        ps[b2][0:32, 0:LORA], raw[:, HALO + TP - 16:HALO + TP + 16], ident[0:LORA, 0:LORA]),
                        reads=[t_raw, t_c], writes=[t_ps[b2]])
                    tl, t_tl = AR.alloc([32, LORA], F32, 'tailL')
                    S.op('scalar', lambda e, b2=b2, tl=tl: e.activation(out=tl[:, :], in_=ps[b2][0:32, 0:LORA], func=AF.Copy),
                         reads=[t_ps[b2]], writes=[t_tl])
                    S.dma(hwq(), o_tailsh[:, 3 * RW + li * LORA:3 * RW + (li + 1) * LORA], tl[:, :], reads=[t_tl], writes=[T()])
                mix, t_mix = AR.alloc([LORA, NOS], F32, 'lmix')
                d, t_d = AR.alloc([LORA, NOS], F32, 'ld')
                S.op('vector', lambda e, raw=raw, d=d: e.tensor_tensor(
                    out=d[:, 0:TP], in0=raw[:, HALO - 1:HALO + TP - 1], in1=raw[:, OWN], op=ALU.subtract),
                    reads=[t_raw], writes=[t_d])
                S.op('vector', lambda e, raw=raw, d=d, mix=mix, li=li: e.scalar_tensor_tensor(
                    out=mix[:, 0:TP], in0=d[:, 0:TP], scalar=mul[:, li:li + 1], in1=raw[:, OWN], op0=ALU.mult, op1=ALU.add),
                    reads=[t_raw, t_d, t_c], writes=[t_mix])
                if last:
                    if shl is None:
                        shl, t_shl = load_shiftT(3 * RW, 2 * LORA, LORA)
                    S.op('vector', lambda e, raw=raw, d=d, li=li, shl=shl: e.tensor_tensor(
                        out=d[:, TP:NOS], in0=shl[li][0:LORA, :], in1=raw[:, SMP], op=ALU.subtract),
                        reads=[t_raw, t_shl], writes=[t_d])
                    S.op('vector', lambda e, raw=raw, d=d, mix=mix, li=li: e.scalar_tensor_tensor(
                        out=mix[:, TP:NOS], in0=d[:, TP:NOS], scalar=mul[:, li:li + 1], in1=raw[:, SMP], op0=ALU.mult,
                        op1=ALU.add), reads=[t_raw, t_d, t_c], writes=[t_mix])
                lb, t_lb = AR.alloc([LORA, NOS], BF16, 'lbf')
                ncol = NOS if last else TP
                S.op('scalar', lambda e, mix=mix, lb=lb, li=li, ncol=ncol: e.activation(
                    out=lb[:, 0:ncol], in_=mix[:, 0:ncol], func=AF.Tanh if li == 0 else AF.Copy),
                    reads=[t_mix], writes=[t_lb])
                lmix.append((lb, t_lb))
            arena_base = AR.off
            base_tiles = list(AR.tiles)
            for hp in range(16):
                keep = AR.tiles[:len(base_tiles)]
                AR.reset()
                AR.off = arena_base
                AR.tiles = keep
                rwkv_pair(p, main, last, hp, lmix)
            AR.reset()

        def load_shiftT(c0, ncol, chunk):
            tok, t_tok = AR.alloc([NS, ncol], F32, 'shtok')
            S.dma(hwq(), tok[:, :], sshift[:, c0:c0 + ncol], writes=[t_tok])
            outs = []
            res, t_res = AR.alloc([128, (ncol // chunk) * NS], F32, 'shT')
            for i in range(ncol // chunk):
                bi = bank()
                S.op('tensor', lambda e, bi=bi, i=i: e.transpose(ps[bi][0:chunk, 0:NS], tok[:, i * chunk:(i + 1) * chunk],
                                                                ident[0:NS, 0:NS]), reads=[t_tok, t_c], writes=[t_ps[bi]])
                S.op('vector', lambda e, bi=bi, i=i: e.tensor_copy(out=res[0:chunk, i * NS:(i + 1) * NS],
                                                                   in_=ps[bi][0:chunk, 0:NS]),
                     reads=[t_ps[bi]], writes=[t_res])
                outs.append(res[:, i * NS:(i + 1) * NS])
            return outs, t_res

        def rwkv_pair(p, main, last, hp, lmix):
            ncol = NOS if last else TP
            CS = slice(0, ncol)
            pcol = lambda name: prm[name][:, hp:hp + 1]
            mixed = {}
            names = ['r', 'k', 'v'] if main else ['k', 'v']
            cbase = {'r': C_R, 'k': C_K, 'v': C_V}
            shs = None
            for nm in names:
                wv, wt, KC = inproj(cbase[nm] + hp * 128)
                bi = proj(wv, wt, KC, 0, 128, xT, [t_xT], slice(0, W))
                raw, t_raw = AR.alloc([128, W], F32, 'raw' + nm)
                S.op('scalar', lambda e, bi=bi, raw=raw: e.activation(out=raw[:, :], in_=ps[bi][:, 0:W], func=AF.Copy),
                     reads=[t_ps[bi]], writes=[t_raw])
                if last:
                    tail_out(raw, t_raw, o_tailsh, (cbase[nm] - C_R) + hp * 128)
                d, t_d = AR.alloc([128, NOS], F32, 'd' + nm)
                mx, t_mx = AR.alloc([128, NOS], F32, 'm' + nm)
                S.op('vector', lambda e, raw=raw, d=d: e.tensor_tensor(
                    out=d[:, 0:TP], in0=raw[:, HALO - 1:HALO + TP - 1], in1=raw[:, OWN], op=ALU.subtract),
                    reads=[t_raw], writes=[t_d])
                S.op('vector', lambda e, raw=raw, d=d, mx=mx, nm=nm: e.scalar_tensor_tensor(
                    out=mx[:, 0:TP], in0=d[:, 0:TP], scalar=pcol('mu_' + nm), in1=raw[:, OWN], op0=ALU.mult, op1=ALU.add),
                    reads=[t_raw, t_d, t_c], writes=[t_mx])
                if last:
                    sh1, t_sh1 = load_shiftT((cbase[nm] - C_R) + hp * 128, 128, 128)
                    S.op('vector', lambda e, raw=raw, d=d, sh1=sh1: e.tensor_tensor(
                        out=d[:, TP:NOS], in0=sh1[0][:, :], in1=raw[:, SMP], op=ALU.subtract),
                        reads=[t_raw, t_sh1], writes=[t_d])
                    S.op('vector', lambda e, raw=raw, d=d, mx=mx, nm=nm: e.scalar_tensor_tensor(
                        out=mx[:, TP:NOS], in0=d[:, TP:NOS], scalar=pcol('mu_' + nm), in1=raw[:, SMP], op0=ALU.mult,
                        op1=ALU.add), reads=[t_raw, t_d, t_c], writes=[t_mx])
                mixed[nm] = (mx, t_mx)
            k, t_k = mixed['k']
            v, t_v = mixed['v']
            lw, t_lw = AR.alloc([128, NOS], F32, 'lw')
            av, t_av = AR.alloc([128, NOS], F32, 'a')
            for li, (wl, dst, t_dst, bname) in enumerate([(w2b, lw, t_lw, 'w0'), (a2b, av, t_av, 'a0')]):
                bi = bank()
                lb, t_lb = lmix[li]
                S.op('tensor', lambda e, bi=bi, wl=wl, lb=lb: e.matmul(ps[bi][:, 0:ncol], wl[:, hp * 128:(hp + 1) * 128],
                                                                       lb[:, 0:ncol], start=True, stop=True),
                     reads=[t_c, t_lb], writes=[t_ps[bi]])
                S.op('scalar', lambda e, bi=bi, dst=dst, bname=bname: e.activation(
                    out=dst[:, CS], in_=ps[bi][:, 0:ncol], func=AF.Sigmoid, bias=pcol(bname)),
                    reads=[t_ps[bi], t_c], writes=[t_dst])
            S.op('vector', lambda e: e.tensor_scalar(out=lw[:, CS], in0=lw[:, CS], scalar1=-0.6065306597126334,
                                                     scalar2=None, op0=ALU.mult), reads=[t_lw], writes=[t_lw])
            kk, t_kk = AR.alloc([128, NOS], F32, 'kk')
            sq, t_sq = AR.alloc([128, NOS], F32, 'sq')
            S.op('vector', lambda e: e.tensor_scalar(out=kk[:, CS], in0=k[:, CS], scalar1=pcol('k_k'), scalar2=None,
                                                     op0=ALU.mult), reads=[t_k, t_c], writes=[t_kk])
            S.op('gpsimd', lambda e: e.tensor_tensor(out=sq[:, CS], in0=kk[:, CS], in1=kk[:, CS], op=ALU.mult),
                 reads=[t_kk], writes=[t_sq])
            bi = bank()
            S.op('tensor', lambda e, bi=bi: e.matmul(ps[bi][:, 0:ncol], bones[:, :], sq[:, CS], start=True, stop=True),
                 reads=[t_c, t_sq], writes=[t_ps[bi]])
            S.op('vector', lambda e, bi=bi: e.tensor_scalar(out=sq[:, CS], in0=ps[bi][:, 0:ncol], scalar1=1e-24,
                                                            scalar2=None, op0=ALU.max), reads=[t_ps[bi]], writes=[t_sq])
            S.op('scalar', lambda e: e.activation(out=sq[:, CS], in_=sq[:, CS], func=AF.Sqrt), reads=[t_sq], writes=[t_sq])
            S.op('vector', lambda e: e.reciprocal(out=sq[:, CS], in_=sq[:, CS]), reads=[t_sq], writes=[t_sq])
            S.op('vector', lambda e: e.tensor_tensor(out=kk[:, CS], in0=kk[:, CS], in1=sq[:, CS], op=ALU.mult),
                 reads=[t_kk, t_sq], writes=[t_kk])
            k2, t_k2 = AR.alloc([128, NOS], F32, 'k2')
            bb, t_bb = AR.alloc([128, NOS], F32, 'b')
            S.op('vector', lambda e: e.tensor_scalar(out=k2[:, CS], in0=av[:, CS], scalar1=pcol('k_a'),
                                                     scalar2=omka[:, hp:hp + 1], op0=ALU.mult, op1=ALU.add),
                 reads=[t_av, t_c], writes=[t_k2])
            S.op('vector', lambda e: e.tensor_tensor(out=k2[:, CS], in0=k2[:, CS], in1=k[:, CS], op=ALU.mult),
                 reads=[t_k2, t_k], writes=[t_k2])
            S.op('gpsimd', lambda e: e.tensor_tensor(out=bb[:, CS], in0=kk[:, CS], in1=av[:, CS], op=ALU.mult),
                 reads=[t_kk, t_av], writes=[t_bb])
            r = t_r = None
            if main:
                r, t_r = mixed['r']
            y, t_y = AR.alloc([128, NOS], F32, 'y')
            mk_ = AR.mark()
            scan(p, main, hp, r, t_r, k2, t_k2, v, t_v, kk, t_kk, bb, t_bb, lw, t_lw, y, t_y)
            AR.release(mk_)
            if last:
                sample_step(hp, r, t_r, k2, t_k2, v, t_v, kk, t_kk, bb, t_bb, lw, t_lw, y, t_y)
                AR.release(mk_)
            if not main:
                return
            wv, wt, KC = inproj(C_ZR + hp * 128)
            bz = proj(wv, wt, KC, 0, 128, xT, [t_xT], slice(0, W))
            sz, t_sz = AR.alloc([128, NOS], F32, 'sz')
            S.op('scalar', lambda e, bz=bz: e.activation(out=sz[:, 0:NOS], in_=ps[bz][:, HALO:HALO + NOS], func=AF.Silu),
                 reads=[t_ps[bz]], writes=[t_sz])
            bon, t_bon = AR.alloc([128, NOS], F32, 'bon')
            S.op('vector', lambda e: e.scalar_tensor_tensor(out=bon[:, CS], in0=r[:, CS], scalar=pcol('r_k'),
                                                            in1=k2[:, CS], op0=ALU.mult, op1=ALU.mult),
                 reads=[t_r, t_k2, t_c], writes=[t_bon])
            bi = bank()
            S.op('tensor', lambda e, bi=bi: e.matmul(ps[bi][:, 0:ncol], bones[:, :], bon[:, CS], start=True, stop=True),
                 reads=[t_c, t_bon], writes=[t_ps[bi]])
            S.op('vector', lambda e, bi=bi: e.tensor_tensor(out=bon[:, CS], in0=ps[bi][:, 0:ncol], in1=v[:, CS],
                                                            op=ALU.mult), reads=[t_ps[bi], t_v], writes=[t_bon])
            bi = bank()
            S.op('tensor', lambda e, bi=bi: e.matmul(ps[bi][:, 0:ncol], bones64[:, :], y[:, CS], start=True, stop=True),
                 reads=[t_c, t_y], writes=[t_ps[bi]])
            S.op('vector', lambda e, bi=bi: e.tensor_tensor(out=y[:, CS], in0=y[:, CS], in1=ps[bi][:, 0:ncol],
                                                            op=ALU.subtract), reads=[t_ps[bi], t_y], writes=[t_y])
            S.op('gpsimd', lambda e: e.tensor_tensor(out=sq[:, CS], in0=y[:, CS], in1=y[:, CS], op=ALU.mult),
                 reads=[t_y], writes=[t_sq])
            bi = bank()
            S.op('tensor', lambda e, bi=bi: e.matmul(ps[bi][:, 0:ncol], bones64[:, :], sq[:, CS], start=True, stop=True),
                 reads=[t_c, t_sq], writes=[t_ps[bi]])
            S.op('vector', lambda e, bi=bi: e.tensor_scalar(out=sq[:, CS], in0=ps[bi][:, 0:ncol], scalar1=GN_EPS,
                                                            scalar2=None, op0=ALU.add), reads=[t_ps[bi]], writes=[t_sq])
            S.op('scalar', lambda e: e.activation(out=sq[:, CS], in_=sq[:, CS], func=AF.Sqrt), reads=[t_sq], writes=[t_sq])
            S.op('vector', lambda e: e.reciprocal(out=sq[:, CS], in_=sq[:, CS]), reads=[t_sq], writes=[t_sq])
            S.op('vector', lambda e: e.tensor_tensor(out=y[:, CS], in0=y[:, CS], in1=sq[:, CS], op=ALU.mult),
                 reads=[t_y, t_sq], writes=[t_y])
            S.op('vector', lambda e: e.tensor_scalar(out=y[:, CS], in0=y[:, CS], scalar1=pcol('gln_w'),
                                                     scalar2=pcol('gln_b'), op0=ALU.mult, op1=ALU.add),
                 reads=[t_y, t_c], writes=[t_y])
            S.op('vector', lambda e: e.tensor_tensor(out=y[:, CS], in0=y[:, CS], in1=bon[:, CS], op=ALU.add),
                 reads=[t_y, t_bon], writes=[t_y])
            S.op('vector', lambda e: e.tensor_tensor(out=bo[:, hp, HALO:HALO + ncol], in0=y[:, CS], in1=sz[:, CS],
                                                     op=ALU.mult), reads=[t_y, t_sz], writes=[t_bo[hp]])

        def scan(p, main, hp, r, t_r, k2, t_k2, v, t_v, kk, t_kk, bb, t_bb, lw, t_lw, y, t_y):
            TS = slice(0, TP)
            cl, t_cl = AR.alloc([128, TP], F32, 'cl')
            e1, t_e1 = AR.alloc([128, TP], F32, 'e1')
            e2, t_e2 = AR.alloc([128, TP], F32, 'e2')
            e3, t_e3 = AR.alloc([128, TP], F32, 'e3')
            S.op('vector', lambda e: e.tensor_tensor_scan(out=cl[:, :], data0=resetm[:, :], data1=lw[:, TS], initial=0.0,
                                                          op0=ALU.mult, op1=ALU.add), reads=[t_c, t_lw], writes=[t_cl])
            S.op('scalar', lambda e: e.activation(out=e1[:, :], in_=cl[:, :], func=AF.Exp), reads=[t_cl], writes=[t_e1])
            S.op('scalar', lambda e: e.activation(out=e2[:, :], in_=cl[:, :], func=AF.Exp, scale=-1.0), reads=[t_cl],
                 writes=[t_e2])
            S.op('vector', lambda e: e.tensor_tensor(out=e3[:, :], in0=cl[:, :], in1=lw[:, TS], op=ALU.subtract),
                 reads=[t_cl, t_lw], writes=[t_e3])
            S.op('scalar', lambda e: e.activation(out=e3[:, :], in_=e3[:, :], func=AF.Exp), reads=[t_e3], writes=[t_e3])
            def bdtile(name):
                tl, tt = AR.alloc([128, NCH, 128], BF16, name)
                S.op('gpsimd', lambda e, tl=tl: e.memset(tl[:, :, :], 0.0), writes=[tt])
                return tl, tt
            at_bd, t_at = bdtile('at_bd')
            bt_bd, t_bt = bdtile('bt_bd')
            kt_bd, t_kt = bdtile('kt_bd')
            v_bd, t_vbd = bdtile('v_bd')
            for h in range(2):
                PS_ = slice(h * 64, (h + 1) * 64)
                CS_ = slice(h * 64, (h + 1) * 64)
                def v3(x, PS_=PS_):
                    return x[PS_, 0:TP].rearrange("p (q t) -> p q t", t=64)
                S.op('vector', lambda e, PS_=PS_, CS_=CS_, v3=v3: e.scalar_tensor_tensor(
                    out=at_bd[PS_, :, CS_], in0=v3(kk), scalar=-1.0, in1=v3(e3), op0=ALU.mult, op1=ALU.mult),
                    reads=[t_kk, t_e3], writes=[t_at])
                S.op('vector', lambda e, PS_=PS_, CS_=CS_, v3=v3: e.tensor_tensor(
                    out=bt_bd[PS_, :, CS_], in0=v3(bb), in1=v3(e2), op=ALU.mult), reads=[t_bb, t_e2], writes=[t_bt])
                S.op('vector', lambda e, PS_=PS_, CS_=CS_, v3=v3: e.tensor_tensor(
                    out=kt_bd[PS_, :, CS_], in0=v3(k2), in1=v3(e2), op=ALU.mult), reads=[t_k2, t_e2], writes=[t_kt])
                S.op('scalar', lambda e, PS_=PS_, CS_=CS_, v3=v3: e.activation(
                    out=v_bd[PS_, :, CS_], in_=v3(v), func=AF.Copy), reads=[t_v], writes=[t_vbd])
            rt_c = t_rt = None
            if main:
                rt_c, t_rt = AR.alloc([128, NCH, 64], BF16, 'rt_c')
                S.op('vector', lambda e: e.tensor_tensor(out=rt_c[:, :, :].rearrange("p q t -> p (q t)"), in0=r[:, TS],
                                                         in1=e1[:, :], op=ALU.mult), reads=[t_r, t_e1], writes=[t_rt])

            def per_chunk_mm(width, mmfn, reads):
                per = 512 // width
                groups = []
                q = 0
                while q < NCH:
                    nq = min(per, NCH - q)
                    bi = bank()
                    for j in range(nq):
                        mmfn(q + j, ps[bi][:, j * width:(j + 1) * width], bi)
                    groups.append((bi, q, nq))
                    q += nq
                return groups

            def mm1(lhsT_fn, rhs_fn, reads):
                def f(q, out, bi):
                    S.op('tensor', lambda e, q=q, out=out: e.matmul(out, lhsT_fn(q), rhs_fn(q), start=True, stop=True),
                         reads=reads, writes=[t_ps[bi]])
                return f

            def evac(groups, width, fn, reads, writes, eng='vector'):
                for (bi, q0, nq) in groups:
                    src = ps[bi][:, 0:nq * width].rearrange("p (q w) -> p q w", w=width)
                    S.op(eng, lambda e, src=src, q0=q0, nq=nq: fn(e, src, q0, nq), reads=[t_ps[bi]] + reads, writes=writes)

            L = []
            Nm = []
            for i in range(2):
                a_, ta_ = AR.alloc([128, NCH, 128], BF16, 'L%d' % i)
                b_, tb_ = AR.alloc([128, NCH, 128], BF16, 'N%d' % i)
                L.append((a_, ta_))
                Nm.append((b_, tb_))
            AkT, t_AkT = AR.alloc([128, NCH, 128], BF16, 'AkT')
            mS3 = maskS[:, :].rearrange("p (q w) -> p q w", w=128)
            mST3 = maskST[:, :].rearrange("p (q w) -> p q w", w=128)
            mI3 = maskI[:, :].rearrange("p (q w) -> p q w", w=64)
            g = per_chunk_mm(128, mm1(lambda q: bt_bd[:, q, :], lambda q: at_bd[:, q, :], [t_bt, t_at]), None)
            evac(g, 128, lambda e, src, q0, nq: e.tensor_tensor(out=L[0][0][:, q0:q0 + nq, :], in0=src, in1=mS3[:, 0:nq, :],
                                                                op=ALU.mult), [t_c], [L[0][1]])
            g = per_chunk_mm(128, mm1(lambda q: at_bd[:, q, :], lambda q: bt_bd[:, q, :], [t_bt, t_at]), None)
            evac(g, 128, lambda e, src, q0, nq: e.tensor_tensor(out=Nm[0][0][:, q0:q0 + nq, :], in0=src, in1=mST3[:, 0:nq, :],
                                                                op=ALU.mult), [t_c], [Nm[0][1]])
            g = per_chunk_mm(128, mm1(lambda q: kt_bd[:, q, :], lambda q: at_bd[:, q, :], [t_kt, t_at]), None)
            evac(g, 128, lambda e, src, q0, nq: e.tensor_tensor(out=AkT[:, q0:q0 + nq, :], in0=src, in1=mS3[:, 0:nq, :],
                                                                op=ALU.mult), [t_c], [t_AkT])
            if main:
                ArbT, t_ArbT = AR.alloc([128, NCH, 64], BF16, 'ArbT')
                ArkT, t_ArkT = AR.alloc([128, NCH, 64], BF16, 'ArkT')
                g = per_chunk_mm(64, mm1(lambda q: bt_bd[:, q, :], lambda q: rt_c[:, q, :], [t_bt, t_rt]), None)
                evac(g, 64, lambda e, src, q0, nq: e.tensor_tensor(out=ArbT[:, q0:q0 + nq, :], in0=src, in1=mI3[:, 0:nq, :],
                                                                   op=ALU.mult), [t_c], [t_ArbT])
                g = per_chunk_mm(64, mm1(lambda q: kt_bd[:, q, :], lambda q: rt_c[:, q, :], [t_kt, t_rt]), None)
                evac(g, 64, lambda e, src, q0, nq: e.tensor_tensor(out=ArkT[:, q0:q0 + nq, :], in0=src, in1=mI3[:, 0:nq, :],
                                                                   op=ALU.mult), [t_c], [t_ArkT])
            def tr_groups(src_bd, t_src):
                groups = []
                q = 0
                while q < NCH:
                    nq = min(4, NCH - q)
                    bi = bank()
                    for j in range(nq):
                        S.op('tensor', lambda e, q=q, j=j, bi=bi: e.transpose(psb[bi][:, j * 128:(j + 1) * 128],
                                                                             src_bd[:, q + j, :], identb[:, :]),
                             reads=[t_src, t_c], writes=[t_ps[bi]])
                    groups.append((bi, q, nq))
                    q += nq
                return groups

            def evac_b(groups, fn, reads, writes, eng='vector'):
                for (bi, q0, nq) in groups:
                    src = psb[bi][:, 0:nq * 128].rearrange("p (q w) -> p q w", w=128)
                    S.op(eng, lambda e, src=src, q0=q0, nq=nq: fn(e, src, q0, nq), reads=[t_ps[bi]] + reads, writes=writes)

            X32, t_X32 = AR.alloc([128, NCH, 128], F32, 'X32')
            Xbf, t_Xbf = AR.alloc([128, NCH, 128], BF16, 'Xbf')
            btk, t_btk = AR.alloc([128, NCH, 128], BF16, 'btk')
            ktk, t_ktk = AR.alloc([128, NCH, 128], BF16, 'ktk')
            vtk_c, t_vtkc = AR.alloc([128, NCH, 64], BF16, 'vtk_c')
            g = tr_groups(at_bd, t_at)
            for h in range(2):
                PS_ = slice(h * 64, (h + 1) * 64)
                evac_b(g, lambda e, src, q0, nq, PS_=PS_: e.tensor_copy(out=X32[PS_, q0:q0 + nq, 0:64],
                                                                       in_=src[PS_, :, PS_]), [], [t_X32])
            g = tr_groups(bt_bd, t_bt)
            evac_b(g, lambda e, src, q0, nq: e.tensor_copy(out=btk[:, q0:q0 + nq, :], in_=src), [], [t_btk])
            g = tr_groups(kt_bd, t_kt)
            evac_b(g, lambda e, src, q0, nq: e.activation(out=ktk[:, q0:q0 + nq, :], in_=src, func=AF.Copy), [], [t_ktk],
                   eng='scalar')
            g = tr_groups(v_bd, t_vbd)
            vtk_bd = t_vtkbd = None
            if main:
                vtk_bd, t_vtkbd = AR.alloc([128, NCH, 128], BF16, 'vtk_bd')
                evac_b(g, lambda e, src, q0, nq: e.activation(out=vtk_bd[:, q0:q0 + nq, :], in_=src, func=AF.Copy), [],
                       [t_vtkbd], eng='scalar')
            for h in range(2):
                PS_ = slice(h * 64, (h + 1) * 64)
                evac_b(g, lambda e, src, q0, nq, PS_=PS_: e.tensor_copy(out=vtk_c[PS_, q0:q0 + nq, :],
                                                                       in_=src[PS_, :, PS_]), [], [t_vtkc])
            g = per_chunk_mm(64, mm1(lambda q: AkT[:, q, :], lambda q: vtk_c[:, q, :], [t_AkT, t_vtkc]), None)
            evac(g, 64, lambda e, src, q0, nq: e.tensor_copy(out=X32[:, q0:q0 + nq, 64:128], in_=src), [], [t_X32])
            S.op('scalar', lambda e: e.activation(out=Xbf[:, :, :], in_=X32[:, :, :], func=AF.Copy), reads=[t_X32],
                 writes=[t_Xbf])
            for lv in range(6):
                Lc, t_Lc = L[lv % 2]
                Nc, t_Nc = Nm[lv % 2]
                g = per_chunk_mm(128, mm1(lambda q, Lc=Lc: Lc[:, q, :], lambda q: Xbf[:, q, :], [t_Lc, t_Xbf]), None)
                evac(g, 128, lambda e, src, q0, nq: e.tensor_tensor(out=X32[:, q0:q0 + nq, :], in0=src,
                                                                    in1=X32[:, q0:q0 + nq, :], op=ALU.add), [t_X32], [t_X32])
                if lv < 5:
                    Ln, t_Ln = L[(lv + 1) % 2]
                    Nn, t_Nn = Nm[(lv + 1) % 2]
                    g1 = per_chunk_mm(128, mm1(lambda q, Nc=Nc: Nc[:, q, :], lambda q, Lc=Lc: Lc[:, q, :], [t_Lc, t_Nc]), None)
                    g2 = per_chunk_mm(128, mm1(lambda q, Lc=Lc: Lc[:, q, :], lambda q, Nc=Nc: Nc[:, q, :], [t_Lc, t_Nc]), None)
                    evac(g1, 128, lambda e, src, q0, nq, Ln=Ln: e.activation(out=Ln[:, q0:q0 + nq, :], in_=src, func=AF.Copy),
                         [], [t_Ln], eng='scalar')
                    evac(g2, 128, lambda e, src, q0, nq, Nn=Nn: e.tensor_copy(out=Nn[:, q0:q0 + nq, :], in_=src), [], [t_Nn],
                         eng='gpsimd' if False else 'vector')
                S.op('scalar', lambda e: e.activation(out=Xbf[:, :, :], in_=X32[:, :, :], func=AF.Copy), reads=[t_X32],
                     writes=[t_Xbf])
            Ah_bd, t_Ah = bdtile('Ah_bd')
            for h in range(2):
                PS_ = slice(h * 64, (h + 1) * 64)
                S.op('vector', lambda e, PS_=PS_: e.tensor_copy(out=Ah_bd[PS_, :, PS_], in_=Xbf[PS_, :, 0:64]),
                     reads=[t_Xbf], writes=[t_Ah])
            U0_bd = t_U0 = None
            if main:
                U0_bd, t_U0 = bdtile('U0_bd')
                for h in range(2):
                    PS_ = slice(h * 64, (h + 1) * 64)
                    S.op('vector', lambda e, PS_=PS_: e.tensor_copy(out=U0_bd[PS_, :, PS_], in_=Xbf[PS_, :, 64:128]),
                         reads=[t_Xbf], writes=[t_U0])
            TpT, t_TpT = AR.alloc([128, NCH, 128], BF16, 'TpT')
            g = per_chunk_mm(128, mm1(lambda q: Ah_bd[:, q, :], lambda q: btk[:, q, :], [t_Ah, t_btk]), None)
            evac(g, 128, lambda e, src, q0, nq: e.tensor_copy(out=TpT[:, q0:q0 + nq, :], in_=src), [], [t_TpT])
            G0p, t_G0 = AR.alloc([128, NCH, 64], F32, 'G0p')
            pc3 = e1[:, :].rearrange("p (q t) -> p q t", t=64)[:, :, 63:64]

            def g0mm(q, out, bi):
                S.op('tensor', lambda e, q=q, out=out: e.matmul(out, btk[:, q, :], Xbf[:, q, 64:128], start=True, stop=False),
                     reads=[t_btk, t_Xbf], writes=[t_ps[bi]])
                S.op('tensor', lambda e, q=q, out=out: e.matmul(out, ktk[:, q, :], vtk_c[:, q, :], start=False, stop=True),
                     reads=[t_ktk, t_vtkc], writes=[t_ps[bi]])
            g = per_chunk_mm(64, g0mm, None)
            evac(g, 64, lambda e, src, q0, nq: e.tensor_tensor(out=G0p[:, q0:q0 + nq, :], in0=src,
                                                               in1=pc3[:, q0:q0 + nq, :].to_broadcast([128, nq, 64]),
                                                               op=ALU.mult), [t_e1], [t_G0])
            RhT = t_RhT = None
            if main:
                RhT, t_RhT = AR.alloc([128, NCH, 64], BF16, 'RhT')
                g = per_chunk_mm(64, mm1(lambda q: Ah_bd[:, q, :], lambda q: ArbT[:, q, :], [t_Ah, t_ArbT]), None)
                evac(g, 64, lambda e, src, q0, nq: e.tensor_tensor(out=RhT[:, q0:q0 + nq, :], in0=src,
                                                                   in1=rt_c[:, q0:q0 + nq, :], op=ALU.add), [t_rt], [t_RhT])
            tmp, t_tmp = AR.alloc([128, 64], F32, 'chtmp')
            if main:
                for h in range(2):
                    PS_ = slice(h * 64, (h + 1) * 64)
                    S.op('vector', lambda e, PS_=PS_: e.tensor_copy(out=Hbd[PS_, hp, PS_], in_=H32[PS_, hp, :]),
                         reads=[t_H[hp]], writes=[t_H[hp]])
            for q in range(NCH):
                if main:
                    by = bank()
                    S.op('tensor', lambda e, q=q, by=by: e.matmul(ps[by][:, 0:64], U0_bd[:, q, :], ArbT[:, q, :],
                                                                  start=True, stop=False),
                         reads=[t_U0, t_ArbT], writes=[t_ps[by]])
                    S.op('tensor', lambda e, q=q, by=by: e.matmul(ps[by][:, 0:64], vtk_bd[:, q, :], ArkT[:, q, :],
                                                                  start=False, stop=False),
                         reads=[t_vtkbd, t_ArkT], writes=[t_ps[by]])
                    S.op('tensor', lambda e, q=q, by=by: e.matmul(ps[by][:, 0:64], Hbd[:, hp, :], RhT[:, q, :],
                                                                  start=False, stop=True),
                         reads=[t_H[hp], t_RhT], writes=[t_ps[by]])
                    S.op('scalar', lambda e, q=q, by=by: e.activation(out=y[:, q * 64:(q + 1) * 64], in_=ps[by][:, 0:64],
                                                                      func=AF.Copy), reads=[t_ps[by]], writes=[t_y])
                bc = bank()
                S.op('tensor', lambda e, q=q, bc=bc: e.matmul(ps[bc][:, 0:64], TpT[:, q, :], Hbf[:, hp, :],
                                                              start=True, stop=True),
                     reads=[t_TpT, t_H[hp]], writes=[t_ps[bc]])
                S.op('vector', lambda e, bc=bc: e.tensor_tensor(out=tmp[:, :], in0=ps[bc][:, 0:64], in1=H32[:, hp, :],
                                                                op=ALU.add), reads=[t_ps[bc], t_H[hp]], writes=[t_tmp])
                S.op('vector', lambda e, q=q: e.scalar_tensor_tensor(out=H32[:, hp, :], in0=tmp[:, :],
                                                                     scalar=e1[:, q * 64 + 63:q * 64 + 64], in1=G0p[:, q, :],
                                                                     op0=ALU.mult, op1=ALU.add),
                     reads=[t_tmp, t_e1, t_G0], writes=[t_H[hp]])
                S.op('scalar', lambda e: e.activation(out=Hbf[:, hp, :], in_=H32[:, hp, :], func=AF.Copy),
                     reads=[t_H[hp]], writes=[t_H[hp]])
                if main:
                    for h in range(2):
                        PS_ = slice(h * 64, (h + 1) * 64)
                        S.op('gpsimd', lambda e, PS_=PS_: e.tensor_copy(out=Hbd[PS_, hp, PS_], in_=H32[PS_, hp, :]),
                             reads=[t_H[hp]], writes=[t_H[hp]])

        def sample_step(hp, r, t_r, k2, t_k2, v, t_v, kk, t_kk, bb, t_bb, lw, t_lw, y, t_y):
            SC = slice(TP, NOS)
            Sst, t_S = AR.alloc([128, NS, 64], F32, 'Sst')
            with nc.allow_non_contiguous_dma(reason="state"):
                pass
            S.dma(hwq(), Sst[:, :, :], srwkv[:, 2 * hp:2 * hp + 2, :, :].rearrange("n h i j -> (h i) n j"), writes=[t_S])
            dec, t_dec = AR.alloc([128, NS], F32, 'dec')
            S.op('scalar', lambda e: e.activation(out=dec[:, :], in_=lw[:, SC], func=AF.Exp), reads=[t_lw], writes=[t_dec])
            nkk, t_nkk = AR.alloc([128, NS], F32, 'nkk')
            S.op('vector', lambda e: e.tensor_scalar(out=nkk[:, :], in0=kk[:, SC], scalar1=-1.0, scalar2=None, op0=ALU.mult),
                 reads=[t_kk], writes=[t_nkk])
            bc = {}
            dgc = [None]
            for nm, (src, t_src) in {'w': (dec[:, :], t_dec), 'nkk': (nkk[:, :], t_nkk), 'b': (bb[:, SC], t_bb),
                                     'k2': (k2[:, SC], t_k2), 'r': (r[:, SC], t_r)}.items():
                if dgc[0] is None:
                    dgc[0] = AR.alloc([128, NS, 64], F32, 'dg')
                dg, t_dg = dgc[0]
                S.op('vector', lambda e, src=src, dg=dg: e.tensor_tensor(
                    out=dg[:, :, :], in0=src.unsqueeze(2).to_broadcast([128, NS, 64]),
                    in1=i2[:, :].unsqueeze(1).to_broadcast([128, NS, 64]), op=ALU.mult), reads=[t_src, t_c], writes=[t_dg])
                xb, t_xb = AR.alloc([128, NS, 64], F32, 'xb' + nm)
                for hf in range(2):
                    bi = bank()
                    S.op('tensor', lambda e, bi=bi, dg=dg, hf=hf: e.matmul(
                        ps[bi][:, 0:512], bones[:, :], dg[:, hf * 8:(hf + 1) * 8, :].rearrange("p a b -> p (a b)"),
                        start=True, stop=True), reads=[t_c, t_dg], writes=[t_ps[bi]])
                    S.op('scalar', lambda e, bi=bi, xb=xb, hf=hf: e.activation(
                        out=xb[:, hf * 8:(hf + 1) * 8, :].rearrange("p a b -> p (a b)"), in_=ps[bi][:, 0:512], func=AF.Copy),
                        reads=[t_ps[bi]], writes=[t_xb])
                bc[nm] = (xb, t_xb)
            t1, t_t1 = AR.alloc([128, NS, 64], F32, 't1')
            sa, t_sa = AR.alloc([128, NS], F32, 'sa')
            S.op('vector', lambda e: e.tensor_tensor(out=t1[:, :, :], in0=Sst[:, :, :], in1=bc['nkk'][0][:, :, :], op=ALU.mult),
                 reads=[t_S, bc['nkk'][1]], writes=[t_t1])
            S.op('vector', lambda e: e.tensor_reduce(out=sa[:, :], in_=t1[:, :, :], axis=AX.X, op=ALU.add),
                 reads=[t_t1], writes=[t_sa])
            S.op('vector', lambda e: e.tensor_tensor(out=Sst[:, :, :], in0=Sst[:, :, :], in1=bc['w'][0][:, :, :], op=ALU.mult),
                 reads=[t_S, bc['w'][1]], writes=[t_S])
            S.op('vector', lambda e: e.tensor_tensor(out=t1[:, :, :], in0=bc['b'][0][:, :, :],
                                                     in1=sa[:, :].unsqueeze(2).to_broadcast([128, NS, 64]), op=ALU.mult),
                 reads=[t_sa, bc['b'][1]], writes=[t_t1])
            S.op('vector', lambda e: e.tensor_tensor(out=Sst[:, :, :], in0=Sst[:, :, :], in1=t1[:, :, :], op=ALU.add),
                 reads=[t_S, t_t1], writes=[t_S])
            S.op('vector', lambda e: e.tensor_tensor(out=t1[:, :, :], in0=bc['k2'][0][:, :, :],
                                                     in1=v[:, SC].unsqueeze(2).to_broadcast([128, NS, 64]), op=ALU.mult),
                 reads=[t_v, bc['k2'][1]], writes=[t_t1])
            S.op('vector', lambda e: e.tensor_tensor(out=Sst[:, :, :], in0=Sst[:, :, :], in1=t1[:, :, :], op=ALU.add),
                 reads=[t_S, t_t1], writes=[t_S])
            S.op('vector', lambda e: e.tensor_tensor(out=t1[:, :, :], in0=Sst[:, :, :], in1=bc['r'][0][:, :, :], op=ALU.mult),
                 reads=[t_S, bc['r'][1]], writes=[t_t1])
            S.op('vector', lambda e: e.tensor_reduce(out=y[:, SC], in_=t1[:, :, :], axis=AX.X, op=ALU.add),
                 reads=[t_t1], writes=[t_y])
            S.dma(hwq(), o_rwkvs[:, 2 * hp:2 * hp + 2, :, :].rearrange("n h i j -> (h i) n j"), Sst[:, :, :],
                  reads=[t_S], writes=[T()])

        def pool_stage(p, last):
            AR.reset()
            mp = p - NPH
            posb, t_posb = AR.alloc([128, TP], F32, 'posb')
            S.dma('sync', posb[:, :], pos[0:1, mp * TP:(mp + 1) * TP].partition_broadcast(128).rearrange("p o n -> p (o n)"),
                  writes=[t_posb])
            rc, t_rc = AR.alloc([128, 4, TP], F32, 'rc')
            for gi in range(4):
                win = float(2 ** (gi + 1))
                S.op('vector', lambda e, gi=gi, win=win: e.tensor_scalar(out=rc[:, gi, :], in0=posb[:, :], scalar1=1.0,
                                                                        scalar2=win, op0=ALU.add, op1=ALU.min),
                     reads=[t_posb], writes=[t_rc])
            S.op('vector', lambda e: e.reciprocal(out=rc[:, :, :], in_=rc[:, :, :]), reads=[t_rc], writes=[t_rc])
            base_off = AR.off
            base_tiles = list(AR.tiles)
            for gi in range(4):
                keep = AR.tiles[:len(base_tiles)]
                AR.reset()
                AR.off = base_off
                AR.tiles = keep
                win = 2 ** (gi + 1)
                pooled, t_pl = AR.alloc([128, 4, W], BF16, 'pooled')
                for ct in range(4):
                    tix = gi * 4 + ct
                    wv, wt, KC = inproj(C_U + tix * 128)
                    bi = proj(wv, wt, KC, 0, 128, xT, [t_xT], slice(0, W))
                    u, t_u = AR.alloc([128, W], F32, 'u')
                    S.op('scalar', lambda e, bi=bi, u=u: e.activation(out=u[:, :], in_=ps[bi][:, 0:W], func=AF.Copy),
                         reads=[t_ps[bi]], writes=[t_u])
                    if last:
                        tail_out(u, t_u, o_tailu, tix * 128)
                    s_prev, t_sp = u, t_u
                    step = 1
                    while step < win:
                        s_new, t_sn = AR.alloc([128, W], F32, 's')
                        lo = 2 * step - 1
                        S.op('vector' if step % 4 == 1 else 'gpsimd', lambda e, s_prev=s_prev, s_new=s_new, step=step, lo=lo:
                             e.tensor_tensor(out=s_new[:, lo:HALO + TP], in0=s_prev[:, lo:HALO + TP],
                                             in1=s_prev[:, lo - step:HALO + TP - step], op=ALU.add),
                             reads=[t_sp], writes=[t_sn])
                        s_prev, t_sp = s_new, t_sn
                        step *= 2
                    tmpw, t_tw = AR.alloc([128, TP], F32, 'tmpw')
                    S.op('vector', lambda e, s_prev=s_prev, tmpw=tmpw, gi=gi: e.tensor_tensor(
                        out=tmpw[:, :], in0=s_prev[:, OWN], in1=rc[:, gi, :], op=ALU.mult), reads=[t_sp, t_rc], writes=[t_tw])
                    S.op('vector', lambda e, tmpw=tmpw, u=u, ct=ct, pooled=pooled: e.tensor_tensor(
                        out=pooled[:, ct, OWN], in0=tmpw[:, :], in1=u[:, OWN], op=ALU.subtract),
                        reads=[t_tw, t_u], writes=[t_pl])
                    if last:
                        stk, t_stk = AR.alloc([128, 2, 128], F32, 'pstk')
                        sv_ = spool.rearrange("n t c -> (n t) c")
                        S.dma(hwq(), stk[:, 0, :], sv_[0:128, tix * 128:(tix + 1) * 128], writes=[t_stk])
                        S.dma(hwq(), stk[0:112, 1, :], sv_[128:240, tix * 128:(tix + 1) * 128], writes=[t_stk])
                        pT, t_pT = AR.alloc([128, 256], F32, 'pT')
                        bi = bank()
                        S.op('tensor', lambda e, bi=bi, stk=stk: e.transpose(ps[bi][:, 0:128], stk[:, 0, :], ident[:, :]),
                             reads=[t_stk, t_c], writes=[t_ps[bi]])
                        S.op('tensor', lambda e, bi=bi, stk=stk: e.transpose(ps[bi][:, 128:240], stk[0:112, 1, :],
                                                                             ident[0:112, 0:112]),
                             reads=[t_stk, t_c], writes=[t_ps[bi]])
                        S.op('vector', lambda e, bi=bi, pT=pT: e.tensor_copy(out=pT[:, 0:240], in_=ps[bi][:, 0:240]),
                             reads=[t_ps[bi]], writes=[t_pT])
                        ws, t_ws = AR.alloc([128, NS], F32, 'ws')
                        pT3 = pT[:, 0:240].rearrange("p (n t) -> p n t", t=15)
                        S.op('vector', lambda e, pT3=pT3, ws=ws, win=win: e.tensor_reduce(
                            out=ws[:, :], in_=pT3[:, :, 15 - (win - 1):15], axis=AX.X, op=ALU.add), reads=[t_pT], writes=[t_ws])
                        S.op('vector', lambda e, ws=ws, u=u: e.tensor_tensor(out=ws[:, :], in0=ws[:, :], in1=u[:, SMP], op=ALU.add),
                             reads=[t_ws, t_u], writes=[t_ws])
                        S.op('vector', lambda e, ws=ws, u=u, ct=ct, pooled=pooled, win=win: e.scalar_tensor_tensor(
                            out=pooled[:, ct, SMP], in0=ws[:, :], scalar=1.0 / win, in1=u[:, SMP], op0=ALU.mult,
                            op1=ALU.subtract), reads=[t_ws, t_u], writes=[t_pl])
                pwv = pwb[:, :].rearrange("p (a b) -> p a b", b=512)
                pwt, pKC = t_pwb, 4
                S.dma('gpsimd', pwv, pool_w[gi].rearrange("(c p) n -> p c n", p=128), writes=[t_pwb])
                for dc in range(4):
                    tix = gi * 4 + dc
                    wv, wt, KC = inproj(C_ZP + tix * 128)
                    bz = proj(wv, wt, KC, 0, 128, xT, [t_xT], slice(0, W))
                    sz, t_sz = AR.alloc([128, NOS], F32, 'szp')
                    S.op('scalar', lambda e, bz=bz, sz=sz: e.activation(out=sz[:, :], in_=ps[bz][:, HALO:HALO + NOS], func=AF.Silu),
                         reads=[t_ps[bz]], writes=[t_sz])
                    bm = proj(pwv, pwt, pKC, dc * 128, (dc + 1) * 128, pooled, [t_pl], OS)
                    S.op('vector', lambda e, bm=bm, sz=sz, tix=tix: e.scalar_tensor_tensor(
                        out=bo[:, tix, OS], in0=ps[bm][:, 0:NOS], scalar=prm['pscale'][:, tix:tix + 1], in1=sz[:, :],
                        op0=ALU.mult, op1=ALU.mult), reads=[t_ps[bm], t_sz, t_c], writes=[t_bo[tix]])
            if last:
                S.dma('sync', o_pools[:, :, :], spool[:, 1:15, :], writes=[T()])

        def mem_stage(p, last):
            AR.reset()
            qT, t_qT = AR.alloc([128, 12, W], BF16, 'qT')
            for c in range(12):
                wv, wt, KC = inproj(C_Q + c * 128)
                bi = proj(wv, wt, KC, 0, 128, xT, [t_xT], slice(0, W))
                S.op('scalar', lambda e, bi=bi, c=c: e.activation(out=qT[:, c, :], in_=ps[bi][:, 0:W], func=AF.Copy,
                                                                  scale=float(MHD) ** -0.5), reads=[t_ps[bi]], writes=[t_qT])
            szs = t_szs = None
            if last:
                szs, t_szs = AR.alloc([128, 12, NS], F32, 'szs')
            base_off = AR.off
            base_tiles = list(AR.tiles)
            for h in range(4):
                keep = AR.tiles[:len(base_tiles)]
                AR.reset()
                AR.off = base_off
                AR.tiles = keep
                eT, t_eT = AR.alloc([128, 2, W], BF16, 'eT')
                for mt in range(2):
                    bi = bank()
                    for dt in range(3):
                        S.op('tensor', lambda e, bi=bi, mt=mt, dt=dt, h=h: e.matmul(
                            ps[bi][:, 0:W], KT[:, h * 3 + dt, mt * 128:(mt + 1) * 128], qT[:, h * 3 + dt, :],
                            start=(dt == 0), stop=(dt == 2)), reads=[t_KT, t_qT], writes=[t_ps[bi]])
                    S.op('scalar', lambda e, bi=bi, mt=mt, eT=eT: e.activation(out=eT[:, mt, :], in_=ps[bi][:, 0:W], func=AF.Exp),
                         reads=[t_ps[bi]], writes=[t_eT])
                bd_ = bank()
                for mt in range(2):
                    S.op('tensor', lambda e, bd_=bd_, mt=mt, eT=eT: e.matmul(ps[bd_][:, 0:W], onesb[:, :], eT[:, mt, :],
                                                                             start=(mt == 0), stop=(mt == 1)),
                         reads=[t_c, t_eT], writes=[t_ps[bd_]])
                rden, t_rden = AR.alloc([128, W], F32, 'rden')
                S.op('vector', lambda e, bd_=bd_, rden=rden: e.reciprocal(out=rden[:, :], in_=ps[bd_][:, 0:W]),
                     reads=[t_ps[bd_]], writes=[t_rden])
                for dt in range(3):
                    c = h * 3 + dt
                    wv, wt, KC = inproj(C_ZM + c * 128)
                    bz = proj(wv, wt, KC, 0, 128, xT, [t_xT], slice(0, W))
                    sz, t_sz = AR.alloc([128, W], F32, 'szm')
                    S.op('scalar', lambda e, bz=bz, sz=sz: e.activation(out=sz[:, :], in_=ps[bz][:, 0:W], func=AF.Silu),
                         reads=[t_ps[bz]], writes=[t_sz])
                    if last:
                        S.op('gpsimd', lambda e, sz=sz, c=c: e.tensor_copy(out=szs[:, c, :], in_=sz[:, SMP]),
                             reads=[t_sz], writes=[t_szs])
                    bo_ = bank()
                    for mt in range(2):
                        S.op('tensor', lambda e, bo_=bo_, mt=mt, c=c, eT=eT: e.matmul(
                            ps[bo_][:, 0:W], Vb[:, mt, c * 128:(c + 1) * 128], eT[:, mt, :], start=(mt == 0), stop=(mt == 1)),
                            reads=[t_Vb, t_eT], writes=[t_ps[bo_]])
                    tmp, t_tmp = AR.alloc([128, W], F32, 'otmp')
                    S.op('vector', lambda e, bo_=bo_, tmp=tmp, rden=rden: e.tensor_tensor(
                        out=tmp[:, :], in0=ps[bo_][:, 0:W], in1=rden[:, :], op=ALU.mult), reads=[t_ps[bo_], t_rden], writes=[t_tmp])
                    S.op('vector', lambda e, tmp=tmp, sz=sz, c=c: e.tensor_tensor(
                        out=bo[:, c, :], in0=tmp[:, :], in1=sz[:, :], op=ALU.mult), reads=[t_tmp, t_sz], writes=[t_bo[c]])
            if last:
                keep = AR.tiles[:len(base_tiles)]
                AR.reset()
                AR.off = base_off
                AR.tiles = keep
                sample_attn(qT, t_qT, szs, t_szs)

        def sample_attn(qT, t_qT, szs, t_szs):
            qtok, t_qtok = AR.alloc([NS, MW], F32, 'qtok')
            for g in range(3):
                bi = bank()
                for j in range(4):
                    c = g * 4 + j
                    S.op('tensor', lambda e, bi=bi, j=j, c=c: e.transpose(psb[bi][0:NS, j * 128:(j + 1) * 128], qT[:, c, SMP],
                                                                         identb[:, :]), reads=[t_qT, t_c], writes=[t_ps[bi]])
                S.op('vector', lambda e, bi=bi, g=g: e.tensor_copy(out=qtok[:, g * 512:(g + 1) * 512], in_=psb[bi][0:NS, 0:512]),
                     reads=[t_ps[bi]], writes=[t_qtok])
            E, t_E = AR.alloc([128, NS, 2, 4], F32, 'E')
            oT, t_oT = AR.alloc([128, 12, NS], F32, 'oT')
            base_off = AR.off
            base_tiles = list(AR.tiles)
            for n in range(NS):
                keep = AR.tiles[:len(base_tiles)]
                AR.reset()
                AR.off = base_off
                AR.tiles = keep
                Kn, t_Kn = AR.alloc([128, 2, MW], F32, 'Kn')
                S.dma(hwq(), Kn[:, :, :], ck[n].rearrange("(a p) d -> p a d", p=128), writes=[t_Kn])
                Vn, t_Vn = AR.alloc([128, 2, MW], F32, 'Vn')
                S.dma(hwq(), Vn[:, :, :], cv[n].rearrange("(a p) d -> p a d", p=128), writes=[t_Vn])
                bq = [bank(), bank(), bank()]
                for g in range(3):
                    S.op('tensor', lambda e, g=g, n=n: e.matmul(pst[:, g * 512:(g + 1) * 512], sel[:, n * 128:(n + 1) * 128],
                                                                qtok[:, g * 512:(g + 1) * 512], start=True, stop=True),
                         reads=[t_c, t_qtok], writes=[t_ps[g]])
                S.op('vector', lambda e, Kn=Kn: e.tensor_tensor(
                    out=Kn[:, :, :], in0=Kn[:, :, :], in1=pst[:, 0:MW].unsqueeze(1).to_broadcast([128, 2, MW]), op=ALU.mult),
                    reads=[t_Kn, t_ps[0], t_ps[1], t_ps[2]], writes=[t_Kn])
                sc, t_sc = AR.alloc([128, 2, 4], F32, 'sc')
                S.op('vector', lambda e, Kn=Kn, sc=sc: e.tensor_reduce(
                    out=sc[:, :, :], in_=Kn[:, :, :].rearrange("p a (h d) -> p a h d", d=MHD), axis=AX.X, op=ALU.add),
                    reads=[t_Kn], writes=[t_sc])
                S.op('scalar', lambda e, sc=sc, n=n: e.activation(out=E[:, n, :, :], in_=sc[:, :, :], func=AF.Exp),
                     reads=[t_sc], writes=[t_E])
                bo_ = bank()
                for c in range(12):
                    h = c // 3
                    for mt in range(2):
                        S.op('tensor', lambda e, bo_=bo_, c=c, h=h, mt=mt, n=n, Vn=Vn: e.matmul(
                            ps[bo_][:, c:c + 1], Vn[:, mt, c * 128:(c + 1) * 128], E[:, n, mt, h:h + 1],
                            start=(mt == 0), stop=(mt == 1)), reads=[t_Vn, t_E], writes=[t_ps[bo_]])
                S.op('vector', lambda e, bo_=bo_, n=n: e.tensor_copy(out=oT[:, :, n], in_=ps[bo_][:, 0:12]),
                     reads=[t_ps[bo_]], writes=[t_oT])
            bd_ = bank()
            for mt in range(2):
                S.op('tensor', lambda e, bd_=bd_, mt=mt: e.matmul(
                    ps[bd_][:, 0:NS * 4].rearrange("p (n h) -> p n h", h=4), bones[:, :], E[:, :, mt, :],
                    start=(mt == 0), stop=False), reads=[t_c, t_E], writes=[t_ps[bd_]])
            S.op('tensor', lambda e, bd_=bd_: e.matmul(
                ps[bd_][:, 0:NS * 4].rearrange("p (n h) -> p n h", h=4), obones[:, :], E[:, :, 0, :], start=False, stop=False),
                reads=[t_c, t_E], writes=[t_ps[bd_]])
            S.op('tensor', lambda e, bd_=bd_: e.matmul(
                ps[bd_][:, 0:NS * 4].rearrange("p (n h) -> p n h", h=4), obones[:, :], E[:, :, 1, :], start=False, stop=True),
                reads=[t_c, t_E], writes=[t_ps[bd_]])
            rd, t_rd = AR.alloc([128, NS, 4], F32, 'rd')
            S.op('vector', lambda e, bd_=bd_: e.reciprocal(out=rd[:, :, :], in_=ps[bd_][:, 0:NS * 4].rearrange("p (n h) -> p n h", h=4)),
                 reads=[t_ps[bd_]], writes=[t_rd])
            for c in range(12):
                h = c // 3
                S.op('vector', lambda e, c=c, h=h: e.tensor_tensor(out=oT[:, c, :], in0=oT[:, c, :], in1=rd[:, :, h], op=ALU.mult),
                     reads=[t_oT, t_rd], writes=[t_oT])
                S.op('vector', lambda e, c=c: e.tensor_tensor(out=bo[:, c, SMP], in0=oT[:, c, :], in1=szs[:, c, :], op=ALU.mult),
                     reads=[t_oT, t_szs], writes=[t_bo[c]])

        obones = sb("obones", [128, 128])
        if int(os.environ.get('KB', '99')) >= 8:
          S.op('vector', lambda e: e.tensor_scalar(out=obones[:], in0=bones[:], scalar1=-1.0, scalar2=1.0, op0=ALU.mult,
                                                 op1=ALU.add), reads=[t_c], writes=[t_c])

        def final_stage(p, last):
            AR.reset()
            dump(hT, 32, t_hT, 44)
            AR.reset()
            mp = p - NPH
            ntt = TP // 128
            vt = [t_vscr[mp * ntt + i] for i in range(ntt)]
            for g4 in range(8):
                subT, t_sub = AR.alloc([128, 4, W], F32, 'subT')
                for j in range(4):
                    c = g4 * 4 + j
                    wv, wt, kp, KC = wload(w_out[:, c * 128:(c + 1) * 128], D, 128)
                    bi = proj(wv, wt, KC, 0, 128, hT, t_hT, slice(0, W))
                    S.op('scalar' if j % 2 else 'vector',
                         (lambda e, bi=bi, j=j, subT=subT: e.activation(out=subT[:, j, :], in_=ps[bi][:, 0:W], func=AF.Copy))
                         if j % 2 else
                         (lambda e, bi=bi, j=j, subT=subT: e.tensor_copy(out=subT[:, j, :], in_=ps[bi][:, 0:W])),
                         reads=[t_ps[bi]], writes=[t_sub])
                tiles = [(HALO + i * 128, 128, y_own[mp * TP + i * 128: mp * TP + (i + 1) * 128, :], vt[i]) for i in range(ntt)]
                if last:
                    tiles.append((HALO + TP, NS, y_s[:, :], t_vscr[-1]))
                for (c0, nr, dst, tv) in tiles:
                    bi = bank()
                    for j in range(4):
                        S.op('tensor', lambda e, bi=bi, j=j, c0=c0, nr=nr, subT=subT: e.transpose(
                            ps[bi][0:nr, j * 128:(j + 1) * 128], subT[:, j, c0:c0 + nr], ident[:, :]),
                            reads=[t_sub, t_c], writes=[t_ps[bi]])
                    stg, t_stg = AR.alloc([128, 512], F32, 'stg')
                    S.op('vector', lambda e, bi=bi, nr=nr, stg=stg: e.tensor_copy(out=stg[0:nr, :], in_=ps[bi][0:nr, 0:512]),
                         reads=[t_ps[bi]], writes=[t_stg])
                    S.dma(hwq(), dst[:, g4 * 512:(g4 + 1) * 512], stg[0:nr, :], reads=[t_stg], writes=[tv])
                if g4 % 2 == 1:
                    AR.reset()
            AR.reset()
            gb, t_gb = AR.alloc([128, 2, D], F32, 'gb')
            S.dma('sync', gb[:, 0, :], ln_g[0:1, :].partition_broadcast(128).rearrange("p o n -> p (o n)"), writes=[t_gb])
            S.dma('scalar', gb[:, 1, :], ln_b[0:1, :].partition_broadcast(128).rearrange("p o n -> p (o n)"), writes=[t_gb])
            base_off = AR.off
            base_tiles = list(AR.tiles)
            tiles = [(128, y_own[mp * TP + i * 128: mp * TP + (i + 1) * 128, :], vt[i],
                      xin[HALO + p * TP + i * 128: HALO + p * TP + (i + 1) * 128, :]) for i in range(ntt)]
            if last:
                tiles.append((NS, y_s[:, :], t_vscr[-1], xs[:, :]))
            for (nr, dst, tv, xsrc) in tiles:
                keep = AR.tiles[:len(base_tiles)]
                AR.reset()
                AR.off = base_off
                AR.tiles = keep
                vrow, t_vr = AR.alloc([128, D], F32, 'vrow')
                S.dma('sync', vrow[0:nr, :], dst, reads=[tv], writes=[t_vr])
                for hlf in range(2):
                    xrow, t_xr = AR.alloc([128, 2048], F32, 'xrow2')
                    S.dma('scalar', xrow[0:nr, :], xsrc[:, hlf * 2048:(hlf + 1) * 2048], writes=[t_xr])
                    S.op('vector', lambda e, nr=nr, vrow=vrow, xrow=xrow, hlf=hlf: e.scalar_tensor_tensor(
                        out=vrow[0:nr, hlf * 2048:(hlf + 1) * 2048], in0=xrow[0:nr, :], scalar=ALPHA,
                        in1=vrow[0:nr, hlf * 2048:(hlf + 1) * 2048], op0=ALU.mult, op1=ALU.add),
                        reads=[t_xr, t_vr], writes=[t_vr])
                stt, t_stt = AR.alloc([128, 8, 6], F32, 'stt')
                for j in range(8):
                    S.op('vector', lambda e, nr=nr, vrow=vrow, stt=stt, j=j: e.bn_stats(out=stt[0:nr, j, :],
                                                                                      in_=vrow[0:nr, j * 512:(j + 1) * 512]),
                         reads=[t_vr], writes=[t_stt])
                mv, t_mv = AR.alloc([128, 4], F32, 'mv')
                S.op('vector', lambda e, nr=nr, stt=stt, mv=mv: e.bn_aggr(out=mv[0:nr, 0:2],
                                                                         in_=stt[0:nr, :, :].rearrange("p a b -> p (a b)")),
                     reads=[t_stt], writes=[t_mv])
                S.op('vector', lambda e, nr=nr, mv=mv: e.tensor_scalar(out=mv[0:nr, 2:3], in0=mv[0:nr, 1:2], scalar1=LN_EPS,
                                                                       scalar2=None, op0=ALU.add), reads=[t_mv], writes=[t_mv])
                S.op('scalar', lambda e, nr=nr, mv=mv: e.activation(out=mv[0:nr, 2:3], in_=mv[0:nr, 2:3], func=AF.Sqrt),
                     reads=[t_mv], writes=[t_mv])
                S.op('vector', lambda e, nr=nr, mv=mv: e.reciprocal(out=mv[0:nr, 2:3], in_=mv[0:nr, 2:3]),
                     reads=[t_mv], writes=[t_mv])
                S.op('vector', lambda e, nr=nr, vrow=vrow, mv=mv: e.tensor_scalar(
                    out=vrow[0:nr, :], in0=vrow[0:nr, :], scalar1=mv[0:nr, 0:1], scalar2=mv[0:nr, 2:3], op0=ALU.subtract,
                    op1=ALU.mult), reads=[t_vr, t_mv], writes=[t_vr])
                S.op('gpsimd', lambda e, nr=nr, vrow=vrow: e.tensor_tensor(out=vrow[0:nr, :], in0=vrow[0:nr, :],
                                                                           in1=gb[0:nr, 0, :], op=ALU.mult),
                     reads=[t_vr, t_gb], writes=[t_vr])
                S.op('vector', lambda e, nr=nr, vrow=vrow: e.tensor_tensor(out=vrow[0:nr, :], in0=vrow[0:nr, :],
                                                                           in1=gb[0:nr, 1, :], op=ALU.add),
                     reads=[t_vr, t_gb], writes=[t_vr])
                S.dma('sync', dst, vrow[0:nr, :], reads=[t_vr], writes=[tv])

        KSTOP = int(os.environ.get('KSTOP', '99'))

        class StopBuild(Exception):
            pass

        def chk(level):
            if KSTOP <= level:
                raise StopBuild()
        stopped = False
        try:
            chk(1)
            for p in range(NPASS):
                run_pass(p)
        except StopBuild:
            stopped = True
        AR.reset()
        for hp in range(16 if not stopped else 0):
            bi = bank()
            S.op('tensor', lambda e, bi=bi, hp=hp: e.transpose(ps[bi][0:64, 0:128], H32[:, hp, :], ident[:, :]),
                 reads=[t_H[hp], t_c], writes=[t_ps[bi]])
            ho, t_ho = AR.alloc([64, 128], F32, 'ho')
            S.op('vector', lambda e, bi=bi, ho=ho: e.tensor_copy(out=ho[:, :], in_=ps[bi][0:64, 0:128]),
                 reads=[t_ps[bi]], writes=[t_ho])
            S.dma(hwq(), o_rwkvp[2 * hp:2 * hp + 2, :, :].rearrange("h i j -> i h j"),
                  ho[:, :].rearrange("p (h j) -> p h j", j=64), reads=[t_ho], writes=[T()])
        S.finish()
        print("ninstr", S.ninstr, {e: len(S.prog[e]) for e in ENGS})
        S.emit(block)
    return nc


_CACHE = {}


def kernel(**inp):
    x_prompt = np.asarray(inp['x_prompt'], np.float32)
    B, SEQ, _ = x_prompt.shape
    T_OWN = SEQ // 2
    TP = min(256, T_OWN)
    key = (T_OWN, TP)
    if key not in _CACHE:
        _CACHE[key] = build(T_OWN, TP)
    nc = _CACHE[key]
    consts = make_consts(TP)
    f = lambda a: np.ascontiguousarray(np.asarray(a, np.float32))
    shared = {}
    for nm in ['w_in', 'b_gate', 'pool_w', 'pool_scale', 'rwkv_mu', 'rwkv_w0', 'rwkv_w2', 'rwkv_a0', 'rwkv_a2', 'rwkv_k_k',
               'rwkv_k_a', 'rwkv_r_k', 'rwkv_ln_w', 'rwkv_ln_b', 'w_mem_kv', 'w_branch_pool', 'w_branch_rwkv',
               'w_branch_mem', 'w_out']:
        a = f(inp[nm])[0]
        if nm == 'rwkv_r_k':
            a = a.reshape(-1)
        shared[nm] = np.ascontiguousarray(a)
    shared['ln_g'] = f(inp['ln_g'])[0].reshape(1, D)
    shared['ln_b'] = f(inp['ln_b'])[0].reshape(1, D)
    for k_, v_ in consts.items():
        shared['c_' + k_] = v_
    in_maps = []
    for c in range(8):
        s, hf = c // 2, c % 2
        xin = np.zeros((HALO + 2 * T_OWN, D), np.float32)
        if hf == 0:
            xin[HALO + T_OWN:] = x_prompt[s, 0:T_OWN]
        else:
            xin[HALO:] = x_prompt[s]
        m = dict(shared)
        m['xin'] = xin
        sl = slice(c * NS, (c + 1) * NS)
        m['xs'] = f(inp['x_sample'])[sl, 0]
        m['pos'] = (np.arange(T_OWN, dtype=np.float32) + hf * T_OWN).reshape(1, T_OWN)
        m['memx'] = f(inp['mem_prompt'])[s]
        m['ck'] = f(inp['cache_mem_k'])[0, sl].reshape(NS, NMEM, MW)
        m['cv'] = f(inp['cache_mem_v'])[0, sl].reshape(NS, NMEM, MW)
        m['spool'] = f(inp['state_pool'])[0, sl]
        m['sshift'] = f(inp['state_shift'])[0, sl, 0]
        m['srwkv'] = f(inp['state_rwkv'])[0, sl]
        in_maps.append({k_: np.ascontiguousarray(v_) for k_, v_ in m.items()})
    kc = os.environ.get('KCORES')
    if kc is not None:
        sel_ = [int(x) for x in kc.split(',')]
        res = run_bass_kernel_spmd(nc, [in_maps[i] for i in sel_], core_ids=list(range(len(sel_))))
        return {sel_[i]: res.results[i] for i in range(len(sel_))}
    res = run_bass_kernel_spmd(nc, in_maps, core_ids=list(range(8)))
    R = res.results
    DEC = 8 * NS
    yp = np.zeros((B, SEQ, D), np.float32)
    ys = np.zeros((DEC, 1, D), np.float32)
    mk = np.zeros((1, B, NMEM, 4, MHD), np.float32)
    mv = np.zeros((1, B, NMEM, 4, MHD), np.float32)
    pp = np.zeros((1, B, 15, PW), np.float32)
    shp = np.zeros((1, B, 1, SHW), np.float32)
    sp = np.zeros((1, B, 32, 64, 64), np.float32)
    psm = np.zeros((1, DEC, 15, PW), np.float32)
    shs = np.zeros((1, DEC, 1, SHW), np.float32)
    ss = np.zeros((1, DEC, 32, 64, 64), np.float32)
    for c in range(8):
        s, hf = c // 2, c % 2
        r = R[c]
        yp[s, hf * T_OWN:(hf + 1) * T_OWN] = r['y_own']
        sl = slice(c * NS, (c + 1) * NS)
        ys[sl, 0] = r['y_s']
        psm[0, sl, 0:14] = r['o_pools']
        psm[0, sl, 14] = r['o_tailu'][16:32]
        shs[0, sl, 0] = r['o_tailsh'][16:32]
        ss[0, sl] = r['o_rwkvs']
        if hf == 0:
            mk[0, s] = r['o_mk'].reshape(NMEM, 4, MHD)
            mv[0, s] = r['o_mv'].reshape(NMEM, 4, MHD)
        else:
            pp[0, s] = r['o_tailu'][1:16]
            shp[0, s, 0] = r['o_tailsh'][15]
            sp[0, s] = r['o_rwkvp']
    return (yp, ys, mk, mv, pp, shp, sp, psm, shs, ss)
```

```python
import contextlib
import os
import numpy as np
import concourse.bass as bass
import concourse.mybir as mybir
from concourse.bass_utils import run_bass_kernel_spmd

F32 = mybir.dt.float32
BF16 = mybir.dt.bfloat16
AF = mybir.ActivationFunctionType
ALU = mybir.AluOpType
AX = mybir.AxisListType

ENGS = ['tensor', 'vector', 'scalar', 'gpsimd', 'sync']

D = 4096
PW = 2048
RW = 2048
LORA = 96
SHW = 3 * RW + 2 * LORA
MW = 1536
MHD = 384
NMEM = 256
INC = 27840
C_U, C_ZP, C_R, C_K, C_V, C_WD, C_AD, C_ZR, C_Q, C_ZM, C_G = 0, 2048, 4096, 6144, 8192, 10240, 10336, 10432, 12480, 14016, 15552
GN_EPS = 64e-5
LN_EPS = 1e-5
ALPHA = 2.0 ** 0.25
NS = 16
HALO = 16


class T:
    __slots__ = ('name', 'last_w', 'readers', 'excl')

    def __init__(self, name='', excl=False):
        self.name = name
        self.last_w = None
        self.readers = []
        self.excl = excl


class Sched:
    def __init__(self, nc, n_dma_sems=40, strict=('vector', 'scalar', 'gpsimd')):
        self.nc = nc
        self.prog = {e: [] for e in ENGS}
        self.cnt = {e: 0 for e in ENGS}
        self.waited = {e: {} for e in ENGS}
        self.strict = set(strict)
        self.n_dma_sems = n_dma_sems
        self.dma_cnt = [0] * n_dma_sems
        self.dma_rr = 0
        self.sems = {}
        self.ninstr = 0
        self.relay = None

    def alloc_sems(self, stack):
        for e in ENGS:
            self.sems[e] = stack.enter_context(self.nc.semaphore('s_' + e))
        for i in range(self.n_dma_sems):
            self.sems[('d', i)] = stack.enter_context(self.nc.semaphore('d%d' % i))

    def _wait(self, eng, key, val):
        if val <= 0:
            return
        w = self.waited[eng]
        if w.get(key, 0) >= val:
            return
        w[key] = val
        sem = self.sems[key]
        self.prog[eng].append(lambda e, sem=sem, val=val: e.wait_ge(sem, val))
        self.ninstr += 1

    def _deps(self, eng, reads, writes):
        deps = []
        for t in reads:
            if t.last_w is not None:
                deps.append(t.last_w)
            if t.excl:
                deps.extend(r for r in t.readers if r[0] != eng)
        for t in writes:
            if t.last_w is not None:
                deps.append(t.last_w)
            deps.extend(t.readers)
        for key, val in deps:
            if key == eng and eng not in self.strict:
                continue
            if eng == 'gpsimd' and key == 'tensor' and self.relay is not None:
                if self.waited[eng].get(key, 0) >= val:
                    continue
                self.waited[eng][key] = val
                self._wait('vector', key, val)
                self.cnt['vector'] += 1
                v2 = self.cnt['vector']
                rl = self.relay
                sem = self.sems['vector']
                self.prog['vector'].append(lambda e, rl=rl, sem=sem: e.memset(rl, 0.0).then_inc(sem, 1))
                self.ninstr += 1
                self._wait(eng, 'vector', v2)
                continue
            self._wait(eng, key, val)

    def op(self, eng, fn, reads=(), writes=()):
        self._deps(eng, reads, writes)
        self.cnt[eng] += 1
        val = self.cnt[eng]
        sem = self.sems[eng]
        self.prog[eng].append(lambda e, fn=fn, sem=sem: fn(e).then_inc(sem, 1))
        self.ninstr += 1
        for t in writes:
            t.last_w = (eng, val)
            t.readers = []
        for t in reads:
            if not any(t is w for w in writes):
                t.readers.append((eng, val))
                if len(t.readers) > 24:
                    self._compress(t)

    @staticmethod
    def _compress(t):
        m = {}
        for k, v in t.readers:
            if m.get(k, 0) < v:
                m[k] = v
        t.readers = list(m.items())

    def dma(self, eng, out, in_, reads=(), writes=(), **kw):
        i = self.dma_rr
        self.dma_rr = (self.dma_rr + 1) % self.n_dma_sems
        key = ('d', i)
        self._wait(eng, key, self.dma_cnt[i])
        self._deps(eng, reads, writes)
        self.dma_cnt[i] += 16
        val = self.dma_cnt[i]
        sem = self.sems[key]
        self.prog[eng].append(
            lambda e, out=out, in_=in_, sem=sem, kw=kw: e.dma_start(out=out, in_=in_, **kw).then_inc(sem, 16))
        self.ninstr += 1
        for t in writes:
            t.last_w = (key, val)
            t.readers = []
        for t in reads:
            t.readers.append((key, val))
            if len(t.readers) > 24:
                self._compress(t)

    def finish(self, eng='sync'):
        for q in ('scalar', 'gpsimd'):
            for i in range(self.n_dma_sems):
                self._wait(q, ('d', i), self.dma_cnt[i])
        for i in range(self.n_dma_sems):
            self._wait(eng, ('d', i), self.dma_cnt[i])
        for e in ENGS:
            if e != eng:
                self._wait(eng, e, self.cnt[e])

    def emit(self, block):
        for e in ENGS:
            lst = self.prog[e]

            def body(engine, lst=lst):
                for th in lst:
                    th(engine)
            getattr(block, e)(body)


class Arena:
    def __init__(self, tensor, nelem):
        self.t = tensor
        self.n = nelem
        self.off = 0
        self.tiles = []
        self.pending = []

    def reset(self):
        m = {}
        for k, v in self.pending:
            if m.get(k, 0) < v:
                m[k] = v
        for t in self.tiles:
            acc = list(t.readers)
            if t.last_w is not None:
                acc.append(t.last_w)
            for k, v in acc:
                if m.get(k, 0) < v:
                    m[k] = v
        self.pending = list(m.items())
        self.tiles = []
        self.off = 0

    def mark(self):
        return (self.off, len(self.tiles))

    def release(self, mk):
        off, nt = mk
        m = {}
        for k, v in self.pending:
            if m.get(k, 0) < v:
                m[k] = v
        for t in self.tiles[nt:]:
            acc = list(t.readers)
            if t.last_w is not None:
                acc.append(t.last_w)
            for k, v in acc:
                if m.get(k, 0) < v:
                    m[k] = v
        self.pending = list(m.items())
        self.tiles = self.tiles[:nt]
        self.off = off

    def alloc(self, shape, dtype=F32, name=''):
        n = 1
        for s in shape[1:]:
            n *= s
        n32 = n if dtype == F32 else (n + 1) // 2
        n32 = (n32 + 7) // 8 * 8
        assert self.off + n32 <= self.n, "arena overflow %s %d+%d>%d" % (name, self.off, n32, self.n)
        v = self.t[:, self.off:self.off + n32]
        self.off += n32
        if dtype != F32:
            v = v.bitcast(dtype)
        v = v[0:shape[0], 0:n]
        if len(shape) == 3:
            v = v.rearrange("p (a b) -> p a b", b=shape[2])
        elif len(shape) == 4:
            v = v.rearrange("p (a b c) -> p a b c", b=shape[2], c=shape[3])
        t = T(name)
        t.readers = list(self.pending)
        self.tiles.append(t)
        return v, t


def make_consts(TP):
    c = {}
    c['ident'] = np.eye(128, dtype=np.float32)
    p = np.arange(128)
    hh = p // 64
    ss = p % 64
    bd = (hh[:, None] == hh[None, :]).astype(np.float32)
    c['bones'] = bd.copy()
    mS = bd * (ss[:, None] < ss[None, :])
    mST = bd * (ss[:, None] > ss[None, :])
    c['maskS'] = np.tile(mS, (1, 4)).astype(np.float32)
    c['maskST'] = np.tile(mST, (1, 4)).astype(np.float32)
    mI = (ss[:, None] <= np.arange(64)[None, :]).astype(np.float32)
    c['maskI'] = np.tile(mI, (1, 8)).astype(np.float32)
    rm = np.ones((128, TP), np.float32)
    rm[:, ::64] = 0.0
    c['resetm'] = rm
    sel = np.zeros((16, 16, 128), np.float32)
    for n in range(16):
        sel[n, n, :] = 1.0
    c['sel'] = sel.reshape(16, 16 * 128)
    c['i2'] = np.tile(np.eye(64, dtype=np.float32), (2, 1))
    return c


def build(T_OWN, TP, dbg=False):
    NPH = T_OWN // TP
    NPASS = 2 * NPH
    W = HALO + TP + NS
    NCH = TP // 64
    OWN = slice(HALO, HALO + TP)
    SMP = slice(HALO + TP, HALO + TP + NS)
    OS = slice(HALO, HALO + TP + NS)
    NOS = TP + NS
    nc = bass.Bass("TRN2", target_bir_lowering=False)

    def din(name, shape):
        return nc.dram_tensor(name, list(shape), F32, kind="ExternalInput").ap()

    def dout(name, shape):
        return nc.dram_tensor(name, list(shape), F32, kind="ExternalOutput").ap()

    xin = din("xin", [HALO + 2 * T_OWN, D])
    xs = din("xs", [NS, D])
    pos = din("pos", [1, T_OWN])
    memx = din("memx", [NMEM, D])
    ck = din("ck", [NS, NMEM, MW])
    cv = din("cv", [NS, NMEM, MW])
    spool = din("spool", [NS, 15, PW])
    sshift = din("sshift", [NS, SHW])
    srwkv = din("srwkv", [NS, 32, 64, 64])
    w_in = din("w_in", [D, INC])
    b_gate = din("b_gate", [3 * D])
    pool_w = din("pool_w", [4, 512, 512])
    pool_scale = din("pool_scale", [PW])
    mu = din("rwkv_mu", [SHW])
    w0 = din("rwkv_w0", [RW])
    w2 = din("rwkv_w2", [LORA, RW])
    a0 = din("rwkv_a0", [RW])
    a2 = din("rwkv_a2", [LORA, RW])
    k_k = din("rwkv_k_k", [RW])
    k_a = din("rwkv_k_a", [RW])
    r_k = din("rwkv_r_k", [RW])
    gln_w = din("rwkv_ln_w", [RW])
    gln_b = din("rwkv_ln_b", [RW])
    w_mem_kv = din("w_mem_kv", [D, 2 * MW])
    wb_pool = din("w_branch_pool", [PW, D])
    wb_rwkv = din("w_branch_rwkv", [RW, D])
    wb_mem = din("w_branch_mem", [MW, D])
    w_out = din("w_out", [D, D])
    ln_g = din("ln_g", [1, D])
    ln_b = din("ln_b", [1, D])
    cshapes = {k: v.shape for k, v in make_consts(TP).items()}
    cin = {k: din("c_" + k, cshapes[k]) for k in cshapes}

    y_own = dout("y_own", [T_OWN, D])
    y_s = dout("y_s", [NS, D])
    o_mk = dout("o_mk", [NMEM, MW])
    o_mv = dout("o_mv", [NMEM, MW])
    o_tailu = dout("o_tailu", [32, PW])
    o_tailsh = dout("o_tailsh", [32, SHW])
    o_pools = dout("o_pools", [NS, 14, PW])
    o_rwkvp = dout("o_rwkvp", [32, 64, 64])
    o_rwkvs = dout("o_rwkvs", [NS, 32, 64, 64])
    KDBG = bool(os.environ.get('KDBG'))
    dbg = dout("dbg", [128, 76 * 32]) if KDBG else None

    st = contextlib.ExitStack()
    with st:
        S = Sched(nc, strict=tuple(x for x in os.environ.get('KSTRICT', 'vector,scalar,gpsimd').split(',') if x))
        S.alloc_sems(st)

        def sb(name, shape, dt=F32):
            return st.enter_context(nc.sbuf_tensor(name, list(shape), dt))

        xT = sb("xT", [128, 32, W], BF16); t_xT = T('xT')
        hT = sb("hT", [128, 32, W], BF16); t_hT = [T('hT%d' % i) for i in range(32)]
        bo = sb("bo", [128, 16, W], BF16); t_bo = [T('bo%d' % i) for i in range(16)]
        NWB = 3
        wch = [sb("wch%d" % i, [128, 32 * 128], BF16) for i in range(NWB)]
        t_wch = [T('wch%d' % i) for i in range(NWB)]
        KT = sb("KT", [128, 12, NMEM], BF16); t_KT = T('KT')
        Vb = sb("Vb", [128, 2, MW], BF16); t_Vb = T('Vb')
        ident = sb("ident", [128, 128]); t_c = T('consts')
        identb = sb("identb", [128, 128], BF16)
        bones = sb("bones", [128, 128])
        bones64 = sb("bones64", [128, 128])
        onesb = sb("onesb", [128, 128], BF16)
        maskS = sb("maskS", [128, 512])
        maskST = sb("maskST", [128, 512])
        maskI = sb("maskI", [128, 512])
        resetm = sb("resetm", [128, TP])
        sel = sb("sel", [16, 16 * 128])
        i2 = sb("i2", [128, 64])
        prm = {}
        relay_t = sb("relay", [128, 8])
        S.relay = relay_t[:]
        H32 = sb("H32", [128, 16, 64]); t_H = [T('H%d' % i) for i in range(16)]
        Hbf = sb("Hbf", [128, 16, 64], BF16)
        Hbd = sb("Hbd", [128, 16, 128], BF16)
        pwb = sb("pwb", [128, 4 * 512], BF16); t_pwb = T('pwb')
        w2b = sb("w2b", [LORA, RW], BF16)
        a2b = sb("a2b", [LORA, RW], BF16)
        ARN = 17408
        arena_t = sb("arena", [128, ARN])
        AR = Arena(arena_t, ARN)
        pst = st.enter_context(nc.psum_tensor("ps", [128, 8 * 512], F32))
        ps = [pst[:, i * 512:(i + 1) * 512] for i in range(8)]
        psb = [pst[:, i * 512:(i + 1) * 512].bitcast(BF16) for i in range(8)]
        t_ps = [T('ps%d' % i, excl=True) for i in range(8)]
        psrr = [0]
        block = st.enter_context(nc.Block())

        def bank():
            i = psrr[0]
            psrr[0] = (i + 1) % 8
            return i

        dq = [0]

        def hwq():
            dq[0] ^= 1
            if os.environ.get('KQ'):
                return 'sync'
            return 'sync' if dq[0] else 'scalar'

        for nm, tl in [('ident', ident), ('bones', bones), ('maskS', maskS), ('maskST', maskST),
                       ('maskI', maskI), ('resetm', resetm), ('sel', sel), ('i2', i2)]:
            S.dma('sync', tl[:], cin[nm][:, :], writes=[t_c])
        KB = int(os.environ.get('KB', '99'))
        if KB >= 2:
          S.op('vector', lambda e: e.tensor_copy(out=identb[:], in_=ident[:]), reads=[t_c], writes=[t_c])
        if KB >= 2:
          S.op('vector', lambda e: e.tensor_scalar(out=bones64[:], in0=bones[:], scalar1=1.0 / 64, scalar2=None,
                                                 op0=ALU.mult), reads=[t_c], writes=[t_c])
        if KB >= 2:
          S.op('vector', lambda e: e.memset(onesb[:], 1.0), writes=[t_c])
        if KB >= 3:
          S.dma('gpsimd', w2b[:], w2[:, :], writes=[t_c])
          S.dma('gpsimd', a2b[:], a2[:, :], writes=[t_c])

        def ptile(name, src, n, ntile, kp=128):
            if KB < (5 if kp != 128 else 4):
                prm[name] = None
                return
            tl = sb("p_" + name, [kp, ntile])
            stg_ = sb("ps_" + name, [ntile, kp])
            t_stg_ = T()
            S.dma('sync', stg_[:], src.rearrange("(c p) -> c p", p=kp), writes=[t_stg_])
            bi = bank()
            S.op('tensor', lambda e, bi=bi, stg_=stg_: e.transpose(ps[bi][0:kp, 0:ntile], stg_[:, :], ident[0:ntile, 0:ntile]),
                 reads=[t_stg_, t_c], writes=[t_ps[bi]])
            S.op('vector', lambda e, bi=bi, tl=tl: e.tensor_copy(out=tl[:, :], in_=ps[bi][0:kp, 0:ntile]),
                 reads=[t_ps[bi]], writes=[t_c])
            prm[name] = tl

        ptile('mu_r', mu[0:RW], RW, 16)
        ptile('mu_k', mu[RW:2 * RW], RW, 16)
        ptile('mu_v', mu[2 * RW:3 * RW], RW, 16)
        for nm, src in [('w0', w0), ('a0', a0), ('k_k', k_k), ('k_a', k_a), ('r_k', r_k), ('gln_w', gln_w),
                        ('gln_b', gln_b), ('pscale', pool_scale)]:
            ptile(nm, src, RW, 16)
        ptile('b_gate', b_gate, 3 * D, 96)
        ptile('mul', mu[3 * RW:3 * RW + 2 * LORA], 2 * LORA, 2, kp=LORA)
        mul = prm['mul']
        omka = sb("p_omka", [128, 16])
        if KB >= 6:
          S.op('vector', lambda e: e.tensor_scalar(out=omka[:], in0=prm['k_a'][:], scalar1=-1.0, scalar2=1.0,
                                                 op0=ALU.mult, op1=ALU.add), reads=[t_c], writes=[t_c])
        if KB >= 7:
          S.op('vector', lambda e: e.memset(H32[:], 0.0), writes=t_H)
          S.op('vector', lambda e: e.memset(Hbf[:], 0.0), writes=t_H)
          S.op('vector', lambda e: e.memset(Hbd[:], 0.0), writes=t_H)

        wrr = [0]

        def wload(src_ap, K, ncols):
            i = wrr[0]
            wrr[0] = (i + 1) % NWB
            if K % 128 == 0:
                KC = K // 128
                v = wch[i][:, 0:KC * ncols].rearrange("p (a b) -> p a b", b=ncols)
                S.dma('gpsimd', v, src_ap.rearrange("(c p) n -> p c n", p=128), writes=[t_wch[i]])
                return v, t_wch[i], 128, KC
            raise AssertionError

        def proj(wv, wt, KC, c0, c1, rhs3, rhs_ts, cols, bi=None):
            if bi is None:
                bi = bank()
            n = cols.stop - cols.start
            for kc in range(KC):
                S.op('tensor', lambda e, kc=kc: e.matmul(ps[bi][0:c1 - c0, 0:n], wv[:, kc, c0:c1], rhs3[:, kc, cols],
                                                        start=(kc == 0), stop=(kc == KC - 1)),
                     reads=[wt] + list(rhs_ts), writes=[t_ps[bi]])
            return bi

        def inproj(col0, ncols=128, cols=slice(0, W)):
            wv, wt, kp, KC = wload(w_in[:, col0:col0 + ncols], D, ncols)
            return wv, wt, KC

        def load_xT(row0, nrows, col0, src):
            r = 0
            while r < nrows:
                nr = min(128, nrows - r)
                for hlf in range(2):
                    if AR.off + 2048 > AR.n:
                        AR.reset()
                    xr, t_xr = AR.alloc([128, 2048], F32, 'xrow')
                    S.dma(hwq(), xr[0:nr, :], src[row0 + r:row0 + r + nr, hlf * 2048:(hlf + 1) * 2048], writes=[t_xr])
                    for g4 in range(4):
                        bi = bank()
                        for j in range(4):
                            cc = g4 * 4 + j
                            S.op('tensor', lambda e, cc=cc, j=j, bi=bi, nr=nr, xr=xr: e.transpose(
                                ps[bi][:, j * 128:j * 128 + nr], xr[0:nr, cc * 128:(cc + 1) * 128], ident[0:nr, 0:nr]),
                                reads=[t_xr, t_c], writes=[t_ps[bi]])
                        kc0 = hlf * 16 + g4 * 4
                        S.op('vector' if g4 % 2 == 0 else 'scalar',
                             (lambda e, bi=bi, kc0=kc0, nr=nr, c=col0 + r: e.tensor_copy(
                                 out=xT[:, kc0:kc0 + 4, c:c + nr],
                                 in_=ps[bi][:, :].rearrange("p (a b) -> p a b", b=128)[:, :, 0:nr]))
                             if g4 % 2 == 0 else
                             (lambda e, bi=bi, kc0=kc0, nr=nr, c=col0 + r: e.activation(
                                 out=xT[:, kc0:kc0 + 4, c:c + nr],
                                 in_=ps[bi][:, :].rearrange("p (a b) -> p a b", b=128)[:, :, 0:nr], func=AF.Copy)),
                             reads=[t_ps[bi]], writes=[t_xT])
                r += nr

        AR.reset()
        memT = hT[:, :, :].rearrange("p a b -> p (a b)")[:, 0:32 * NMEM].rearrange("p (a b) -> p a b", b=NMEM)

        def load_memT():
            for r in range(2):
                for hlf in range(2):
                    xr, t_xr = AR.alloc([128, 2048], F32, 'mrow')
                    S.dma(hwq(), xr[:, :], memx[r * 128:(r + 1) * 128, hlf * 2048:(hlf + 1) * 2048], writes=[t_xr])
                    for g4 in range(4):
                        bi = bank()
                        for j in range(4):
                            cc = g4 * 4 + j
                            S.op('tensor', lambda e, cc=cc, j=j, bi=bi, xr=xr: e.transpose(
                                ps[bi][:, j * 128:(j + 1) * 128], xr[:, cc * 128:(cc + 1) * 128], ident[:, :]),
                                reads=[t_xr, t_c], writes=[t_ps[bi]])
                        kc0 = hlf * 16 + g4 * 4
                        S.op('vector', lambda e, bi=bi, kc0=kc0, r=r: e.tensor_copy(
                            out=memT[:, kc0:kc0 + 4, r * 128:(r + 1) * 128],
                            in_=ps[bi][:, :].rearrange("p (a b) -> p a b", b=128)), reads=[t_ps[bi]], writes=t_hT)
        KSTOP0 = int(os.environ.get('KSTOP', '99'))
        if KSTOP0 > 0:
            load_memT()
        for c in range(int(os.environ.get('KC', '24')) if KSTOP0 > 0 else 0):
            wv, wt, kp, KC = wload(w_mem_kv[:, c * 128:(c + 1) * 128], D, 128)
            bi = proj(wv, wt, KC, 0, 128, memT, t_hT, slice(0, NMEM))
            kv32, t_kv32 = AR.alloc([128, NMEM], F32, 'kv32')
            S.op('vector', lambda e, bi=bi, kv32=kv32: e.tensor_copy(out=kv32[:, :], in_=ps[bi][:, 0:NMEM]),
                 reads=[t_ps[bi]], writes=[t_kv32])
            if c < 12:
                S.op('scalar', lambda e, bi=bi, c=c: e.activation(out=KT[:, c, :], in_=ps[bi][:, 0:NMEM], func=AF.Copy),
                     reads=[t_ps[bi]], writes=[t_KT])
            b2 = bank()
            for mt in range(2):
                S.op('tensor', lambda e, mt=mt, b2=b2, kv32=kv32: e.transpose(
                    ps[b2][:, mt * 128:(mt + 1) * 128], kv32[:, mt * 128:(mt + 1) * 128], ident[:, :]),
                    reads=[t_kv32, t_c], writes=[t_ps[b2]])
            kvt, t_kvt = AR.alloc([128, 2, 128], F32, 'kvt')
            S.op('vector', lambda e, b2=b2, kvt=kvt: e.tensor_copy(
                out=kvt[:, :, :], in_=ps[b2][:, 0:256].rearrange("p (a b) -> p a b", b=128)),
                reads=[t_ps[b2]], writes=[t_kvt])
            if c >= 12:
                S.op('scalar', lambda e, b2=b2, c=c: e.activation(
                    out=Vb[:, :, (c - 12) * 128:(c - 11) * 128],
                    in_=ps[b2][:, 0:256].rearrange("p (a b) -> p a b", b=128), func=AF.Copy),
                    reads=[t_ps[b2]], writes=[t_Vb])
            dst = o_mk if c < 12 else o_mv
            cc = c % 12
            if not os.environ.get('KO'):
                S.dma(hwq(), dst[:, cc * 128:(cc + 1) * 128].rearrange("(a p) n -> p a n", p=128), kvt[:, :, :],
                      reads=[t_kvt], writes=[T()])
            if c % 6 == 5:
                AR.reset()
        AR.reset()

        t_vscr = [T('v%d' % i) for i in range(NPH * (TP // 128) + 1)]

        def run_pass(p):
            main = p >= NPH
            last = p == NPASS - 1
            cur_last[0] = last
            mp = p - NPH
            AR.reset()
            load_xT(p * TP, HALO + TP, 0, xin)
            load_xT(0, NS, HALO + TP, xs)
            AR.reset()
            chk(2)
            rwkv_stage(p, main, last)
            chk(3)
            if main:
                chk(4)
                branch_proj(wb_rwkv, RW, 1, first=True)
                chk(5)
                pool_stage(p, last)
                branch_proj(wb_pool, PW, 0, first=False)
                chk(6)
                mem_stage(p, last)
                branch_proj(wb_mem, MW, 2, first=False)
                chk(7)
                final_stage(p, last)
                chk(9)

        cur_last = [False]

        def dump(src3, ntile, tls, off):
            if not (KDBG and cur_last[0]):
                return
            stg, t_stg = AR.alloc([128, ntile, 32], F32, 'dbgstg')
            S.op('vector', lambda e: e.tensor_copy(out=stg[:, :, :], in_=src3[:, 0:ntile, HALO + TP - 16:HALO + TP + 16]),
                 reads=tls, writes=[t_stg])
            S.dma('sync', dbg[:, off * 32:(off + ntile) * 32], stg[:, :, :].rearrange("p a b -> p (a b)"), reads=[t_stg],
                  writes=[T()])

        def branch_proj(wb, Kb, gi, first):
            AR.reset()
            KCb = Kb // 128
            dump(bo, KCb, t_bo[0:KCb], {1: 0, 0: 16, 2: 32}[gi])
            for c in range(32):
                wv, wt, kp, KC = wload(wb[:, c * 128:(c + 1) * 128], Kb, 128)
                bp = proj(wv, wt, KC, 0, 128, bo, t_bo[0:KCb], slice(0, W))
                gv, gt, gKC = inproj(C_G + gi * D + c * 128)
                bg = proj(gv, gt, gKC, 0, 128, xT, [t_xT], slice(0, W))
                g, t_g = AR.alloc([128, W], F32, 'g')
                S.op('scalar', lambda e, bg=bg, g=g, gi=gi, c=c: e.activation(
                    out=g[:, :], in_=ps[bg][:, 0:W], func=AF.Sigmoid,
                    bias=prm['b_gate'][:, gi * 32 + c:gi * 32 + c + 1]), reads=[t_ps[bg], t_c], writes=[t_g])
                if first:
                    S.op('vector', lambda e, bp=bp, g=g, c=c: e.tensor_tensor(
                        out=hT[:, c, :], in0=ps[bp][:, 0:W], in1=g[:, :], op=ALU.mult),
                        reads=[t_ps[bp], t_g], writes=[t_hT[c]])
                else:
                    tmp, t_tmp = AR.alloc([128, W], F32, 'gtmp')
                    S.op('vector', lambda e, bp=bp, g=g, tmp=tmp: e.tensor_tensor(
                        out=tmp[:, :], in0=ps[bp][:, 0:W], in1=g[:, :], op=ALU.mult),
                        reads=[t_ps[bp], t_g], writes=[t_tmp])
                    S.op('gpsimd', lambda e, tmp=tmp, c=c: e.tensor_tensor(
                        out=hT[:, c, :], in0=hT[:, c, :], in1=tmp[:, :], op=ALU.add),
                        reads=[t_tmp, t_hT[c]], writes=[t_hT[c]])
                if c % 4 == 3:
                    AR.reset()

        def tail_out(src32, t_src, dst, col0):
            bi = bank()
            S.op('tensor', lambda e, bi=bi: e.transpose(ps[bi][0:32, 0:128], src32[:, HALO + TP - 16:HALO + TP + 16],
                                                        ident[:, :]), reads=[t_src, t_c], writes=[t_ps[bi]])
            tl, t_tl = AR.alloc([32, 128], F32, 'tail')
            S.op('scalar', lambda e, bi=bi, tl=tl: e.activation(out=tl[:, :], in_=ps[bi][0:32, 0:128], func=AF.Copy),
                 reads=[t_ps[bi]], writes=[t_tl])
            S.dma(hwq(), dst[:, col0:col0 + 128], tl[:, :], reads=[t_tl], writes=[T()])

        def rwkv_stage(p, main, last):
            AR.reset()
            lmix = []
            shl = None
            for li in range(2):
                wv, wt, KC = inproj(C_WD + li * LORA, LORA)
                bi = proj(wv, wt, KC, 0, LORA, xT, [t_xT], slice(0, W))
                raw, t_raw = AR.alloc([LORA, W], F32, 'lraw')
                S.op('vector', lambda e, bi=bi, raw=raw: e.tensor_copy(out=raw[:, :], in_=ps[bi][0:LORA, 0:W]),
                     reads=[t_ps[bi]], writes=[t_raw])
                if last:
                    b2 = bank()
                    S.op('tensor', lambda e, b2=b2, raw=raw: e.transpose(
                        ps[b2][0:32, 0:LORA], raw[:, HALO + TP - 16:HALO + TP + 16], ident[0:LORA, 0:LORA]),
                        reads=[t_raw, t_c], writes=[t_ps[b2]])
                    tl, t_tl = AR.alloc([32, LORA], F32, 'tailL')
                    S.op('scalar', lambda e, b2=b2, tl=tl: e.activation(out=tl[:, :], in_=ps[b2][0:32, 0:LORA], func=AF.Copy),
                         reads=[t_ps[b2]], writes=[t_tl])
                    S.dma(hwq(), o_tailsh[:, 3 * RW + li * LORA:3 * RW + (li + 1) * LORA], tl[:, :], reads=[t_tl], writes=[T()])
                mix, t_mix = AR.alloc([LORA, NOS], F32, 'lmix')
                d, t_d = AR.alloc([LORA, NOS], F32, 'ld')
                S.op('vector', lambda e, raw=raw, d=d: e.tensor_tensor(
                    out=d[:, 0:TP], in0=raw[:, HALO - 1:HALO + TP - 1], in1=raw[:, OWN], op=ALU.subtract),
                    reads=[t_raw], writes=[t_d])
                S.op('vector', lambda e, raw=raw, d=d, mix=mix, li=li: e.scalar_tensor_tensor(
                    out=mix[:, 0:TP], in0=d[:, 0:TP], scalar=mul[:, li:li + 1], in1=raw[:, OWN], op0=ALU.mult, op1=ALU.add),
                    reads=[t_raw, t_d, t_c], writes=[t_mix])
                if last:
                    if shl is None:
                        shl, t_shl = load_shiftT(3 * RW, 2 * LORA, LORA)
                    S.op('vector', lambda e, raw=raw, d=d, li=li, shl=shl: e.tensor_tensor(
                        out=d[:, TP:NOS], in0=shl[li][0:LORA, :], in1=raw[:, SMP], op=ALU.subtract),
                        reads=[t_raw, t_shl], writes=[t_d])
                    S.op('vector', lambda e, raw=raw, d=d, mix=mix, li=li: e.scalar_tensor_tensor(
                        out=mix[:, TP:NOS], in0=d[:, TP:NOS], scalar=mul[:, li:li + 1], in1=raw[:, SMP], op0=ALU.mult,
                        op1=ALU.add), reads=[t_raw, t_d, t_c], writes=[t_mix])
                lb, t_lb = AR.alloc([LORA, NOS], BF16, 'lbf')
                ncol = NOS if last else TP
                S.op('scalar', lambda e, mix=mix, lb=lb, li=li, ncol=ncol: e.activation(
                    out=lb[:, 0:ncol], in_=mix[:, 0:ncol], func=AF.Tanh if li == 0 else AF.Copy),
                    reads=[t_mix], writes=[t_lb])
                lmix.append((lb, t_lb))
            arena_base = AR.off
            base_tiles = list(AR.tiles)
            for hp in range(16):
                keep = AR.tiles[:len(base_tiles)]
                AR.reset()
                AR.off = arena_base
                AR.tiles = keep
                rwkv_pair(p, main, last, hp, lmix)
            AR.reset()

        def load_shiftT(c0, ncol, chunk):
            tok, t_tok = AR.alloc([NS, ncol], F32, 'shtok')
            S.dma(hwq(), tok[:, :], sshift[:, c0:c0 + ncol], writes=[t_tok])
            outs = []
            res, t_res = AR.alloc([128, (ncol // chunk) * NS], F32, 'shT')
            for i in range(ncol // chunk):
                bi = bank()
                S.op('tensor', lambda e, bi=bi, i=i: e.transpose(ps[bi][0:chunk, 0:NS], tok[:, i * chunk:(i + 1) * chunk],
                                                                ident[0:NS, 0:NS]), reads=[t_tok, t_c], writes=[t_ps[bi]])
                S.op('vector', lambda e, bi=bi, i=i: e.tensor_copy(out=res[0:chunk, i * NS:(i + 1) * NS],
                                                                   in_=ps[bi][0:chunk, 0:NS]),
                     reads=[t_ps[bi]], writes=[t_res])
                outs.append(res[:, i * NS:(i + 1) * NS])
            return outs, t_res

        def rwkv_pair(p, main, last, hp, lmix):
            ncol = NOS if last else TP
            CS = slice(0, ncol)
            pcol = lambda name: prm[name][:, hp:hp + 1]
            mixed = {}
            names = ['r', 'k', 'v'] if main else ['k', 'v']
            cbase = {'r': C_R, 'k': C_K, 'v': C_V}
            shs = None
            for nm in names:
                wv, wt, KC = inproj(cbase[nm] + hp * 128)
                bi = proj(wv, wt, KC, 0, 128, xT, [t_xT], slice(0, W))
                raw, t_raw = AR.alloc([128, W], F32, 'raw' + nm)
                S.op('scalar', lambda e, bi=bi, raw=raw: e.activation(out=raw[:, :], in_=ps[bi][:, 0:W], func=AF.Copy),
                     reads=[t_ps[bi]], writes=[t_raw])
                if last:
                    tail_out(raw, t_raw, o_tailsh, (cbase[nm] - C_R) + hp * 128)
                d, t_d = AR.alloc([128, NOS], F32, 'd' + nm)
                mx, t_mx = AR.alloc([128, NOS], F32, 'm' + nm)
                S.op('vector', lambda e, raw=raw, d=d: e.tensor_tensor(
                    out=d[:, 0:TP], in0=raw[:, HALO - 1:HALO + TP - 1], in1=raw[:, OWN], op=ALU.subtract),
                    reads=[t_raw], writes=[t_d])
                S.op('vector', lambda e, raw=raw, d=d, mx=mx, nm=nm: e.scalar_tensor_tensor(
                    out=mx[:, 0:TP], in0=d[:, 0:TP], scalar=pcol('mu_' + nm), in1=raw[:, OWN], op0=ALU.mult, op1=ALU.add),
                    reads=[t_raw, t_d, t_c], writes=[t_mx])
                if last:
                    sh1, t_sh1 = load_shiftT((cbase[nm] - C_R) + hp * 128, 128, 128)
                    S.op('vector', lambda e, raw=raw, d=d, sh1=sh1: e.tensor_tensor(
                        out=d[:, TP:NOS], in0=sh1[0][:, :], in1=raw[:, SMP], op=ALU.subtract),
                        reads=[t_raw, t_sh1], writes=[t_d])
                    S.op('vector', lambda e, raw=raw, d=d, mx=mx, nm=nm: e.scalar_tensor_tensor(
                        out=mx[:, TP:NOS], in0=d[:, TP:NOS], scalar=pcol('mu_' + nm), in1=raw[:, SMP], op0=ALU.mult,
                        op1=ALU.add), reads=[t_raw, t_d, t_c], writes=[t_mx])
                mixed[nm] = (mx, t_mx)
            k, t_k = mixed['k']
            v, t_v = mixed['v']
            lw, t_lw = AR.alloc([128, NOS], F32, 'lw')
            av, t_av = AR.alloc([128, NOS], F32, 'a')
            for li, (wl, dst, t_dst, bname) in enumerate([(w2b, lw, t_lw, 'w0'), (a2b, av, t_av, 'a0')]):
                bi = bank()
                lb, t_lb = lmix[li]
                S.op('tensor', lambda e, bi=bi, wl=wl, lb=lb: e.matmul(ps[bi][:, 0:ncol], wl[:, hp * 128:(hp + 1) * 128],
                                                                       lb[:, 0:ncol], start=True, stop=True),
                     reads=[t_c, t_lb], writes=[t_ps[bi]])
                S.op('scalar', lambda e, bi=bi, dst=dst, bname=bname: e.activation(
                    out=dst[:, CS], in_=ps[bi][:, 0:ncol], func=AF.Sigmoid, bias=pcol(bname)),
                    reads=[t_ps[bi], t_c], writes=[t_dst])
            S.op('vector', lambda e: e.tensor_scalar(out=lw[:, CS], in0=lw[:, CS], scalar1=-0.6065306597126334,
                                                     scalar2=None, op0=ALU.mult), reads=[t_lw], writes=[t_lw])
            kk, t_kk = AR.alloc([128, NOS], F32, 'kk')
            sq, t_sq = AR.alloc([128, NOS], F32, 'sq')
            S.op('vector', lambda e: e.tensor_scalar(out=kk[:, CS], in0=k[:, CS], scalar1=pcol('k_k'), scalar2=None,
                                                     op0=ALU.mult), reads=[t_k, t_c], writes=[t_kk])
            S.op('gpsimd', lambda e: e.tensor_tensor(out=sq[:, CS], in0=kk[:, CS], in1=kk[:, CS], op=ALU.mult),
                 reads=[t_kk], writes=[t_sq])
            bi = bank()
            S.op('tensor', lambda e, bi=bi: e.matmul(ps[bi][:, 0:ncol], bones[:, :], sq[:, CS], start=True, stop=True),
                 reads=[t_c, t_sq], writes=[t_ps[bi]])
            S.op('vector', lambda e, bi=bi: e.tensor_scalar(out=sq[:, CS], in0=ps[bi][:, 0:ncol], scalar1=1e-24,
                                                            scalar2=None, op0=ALU.max), reads=[t_ps[bi]], writes=[t_sq])
            S.op('scalar', lambda e: e.activation(out=sq[:, CS], in_=sq[:, CS], func=AF.Sqrt), reads=[t_sq], writes=[t_sq])
            S.op('vector', lambda e: e.reciprocal(out=sq[:, CS], in_=sq[:, CS]), reads=[t_sq], writes=[t_sq])
            S.op('vector', lambda e: e.tensor_tensor(out=kk[:, CS], in0=kk[:, CS], in1=sq[:, CS], op=ALU.mult),
                 reads=[t_kk, t_sq], writes=[t_kk])
            k2, t_k2 = AR.alloc([128, NOS], F32, 'k2')
            bb, t_bb = AR.alloc([128, NOS], F32, 'b')
            S.op('vector', lambda e: e.tensor_scalar(out=k2[:, CS], in0=av[:, CS], scalar1=pcol('k_a'),
                                                     scalar2=omka[:, hp:hp + 1], op0=ALU.mult, op1=ALU.add),
                 reads=[t_av, t_c], writes=[t_k2])
            S.op('vector', lambda e: e.tensor_tensor(out=k2[:, CS], in0=k2[:, CS], in1=k[:, CS], op=ALU.mult),
                 reads=[t_k2, t_k], writes=[t_k2])
            S.op('gpsimd', lambda e: e.tensor_tensor(out=bb[:, CS], in0=kk[:, CS], in1=av[:, CS], op=ALU.mult),
                 reads=[t_kk, t_av], writes=[t_bb])
            r = t_r = None
            if main:
                r, t_r = mixed['r']
            y, t_y = AR.alloc([128, NOS], F32, 'y')
            mk_ = AR.mark()
            scan(p, main, hp, r, t_r, k2, t_k2, v, t_v, kk, t_kk, bb, t_bb, lw, t_lw, y, t_y)
            AR.release(mk_)
            if last:
                sample_step(hp, r, t_r, k2, t_k2, v, t_v, kk, t_kk, bb, t_bb, lw, t_lw, y, t_y)
                AR.release(mk_)
            if not main:
                return
            wv, wt, KC = inproj(C_ZR + hp * 128)
            bz = proj(wv, wt, KC, 0, 128, xT, [t_xT], slice(0, W))
            sz, t_sz = AR.alloc([128, NOS], F32, 'sz')
            S.op('scalar', lambda e, bz=bz: e.activation(out=sz[:, 0:NOS], in_=ps[bz][:, HALO:HALO + NOS], func=AF.Silu),
                 reads=[t_ps[bz]], writes=[t_sz])
            bon, t_bon = AR.alloc([128, NOS], F32, 'bon')
            S.op('vector', lambda e: e.scalar_tensor_tensor(out=bon[:, CS], in0=r[:, CS], scalar=pcol('r_k'),
                                                            in1=k2[:, CS], op0=ALU.mult, op1=ALU.mult),
                 reads=[t_r, t_k2, t_c], writes=[t_bon])
            bi = bank()
            S.op('tensor', lambda e, bi=bi: e.matmul(ps[bi][:, 0:ncol], bones[:, :], bon[:, CS], start=True, stop=True),
                 reads=[t_c, t_bon], writes=[t_ps[bi]])
            S.op('vector', lambda e, bi=bi: e.tensor_tensor(out=bon[:, CS], in0=ps[bi][:, 0:ncol], in1=v[:, CS],
                                                            op=ALU.mult), reads=[t_ps[bi], t_v], writes=[t_bon])
            bi = bank()
            S.op('tensor', lambda e, bi=bi: e.matmul(ps[bi][:, 0:ncol], bones64[:, :], y[:, CS], start=True, stop=True),
                 reads=[t_c, t_y], writes=[t_ps[bi]])
            S.op('vector', lambda e, bi=bi: e.tensor_tensor(out=y[:, CS], in0=y[:, CS], in1=ps[bi][:, 0:ncol],
                                                            op=ALU.subtract), reads=[t_ps[bi], t_y], writes=[t_y])
            S.op('gpsimd', lambda e: e.tensor_tensor(out=sq[:, CS], in0=y[:, CS], in1=y[:, CS], op=ALU.mult),
                 reads=[t_y], writes=[t_sq])
            bi = bank()
            S.op('tensor', lambda e, bi=bi: e.matmul(ps[bi][:, 0:ncol], bones64[:, :], sq[:, CS], start=True, stop=True),
                 reads=[t_c, t_sq], writes=[t_ps[bi]])
            S.op('vector', lambda e, bi=bi: e.tensor_scalar(out=sq[:, CS], in0=ps[bi][:, 0:ncol], scalar1=GN_EPS,
                                                            scalar2=None, op0=ALU.add), reads=[t_ps[bi]], writes=[t_sq])
            S.op('scalar', lambda e: e.activation(out=sq[:, CS], in_=sq[:, CS], func=AF.Sqrt), reads=[t_sq], writes=[t_sq])
            S.op('vector', lambda e: e.reciprocal(out=sq[:, CS], in_=sq[:, CS]), reads=[t_sq], writes=[t_sq])
            S.op('vector', lambda e: e.tensor_tensor(out=y[:, CS], in0=y[:, CS], in1=sq[:, CS], op=ALU.mult),
                 reads=[t_y, t_sq], writes=[t_y])
            S.op('vector', lambda e: e.tensor_scalar(out=y[:, CS], in0=y[:, CS], scalar1=pcol('gln_w'),
                                                     scalar2=pcol('gln_b'), op0=ALU.mult, op1=ALU.add),
                 reads=[t_y, t_c], writes=[t_y])
            S.op('vector', lambda e: e.tensor_tensor(out=y[:, CS], in0=y[:, CS], in1=bon[:, CS], op=ALU.add),
                 reads=[t_y, t_bon], writes=[t_y])
            S.op('vector', lambda e: e.tensor_tensor(out=bo[:, hp, HALO:HALO + ncol], in0=y[:, CS], in1=sz[:, CS],
                                                     op=ALU.mult), reads=[t_y, t_sz], writes=[t_bo[hp]])

        def scan(p, main, hp, r, t_r, k2, t_k2, v, t_v, kk, t_kk, bb, t_bb, lw, t_lw, y, t_y):
            TS = slice(0, TP)
            cl, t_cl = AR.alloc([128, TP], F32, 'cl')
            e1, t_e1 = AR.alloc([128, TP], F32, 'e1')
            e2, t_e2 = AR.alloc([128, TP], F32, 'e2')
            e3, t_e3 = AR.alloc([128, TP], F32, 'e3')
            S.op('vector', lambda e: e.tensor_tensor_scan(out=cl[:, :], data0=resetm[:, :], data1=lw[:, TS], initial=0.0,
                                                          op0=ALU.mult, op1=ALU.add), reads=[t_c, t_lw], writes=[t_cl])
            S.op('scalar', lambda e: e.activation(out=e1[:, :], in_=cl[:, :], func=AF.Exp), reads=[t_cl], writes=[t_e1])
            S.op('scalar', lambda e: e.activation(out=e2[:, :], in_=cl[:, :], func=AF.Exp, scale=-1.0), reads=[t_cl],
                 writes=[t_e2])
            S.op('vector', lambda e: e.tensor_tensor(out=e3[:, :], in0=cl[:, :], in1=lw[:, TS], op=ALU.subtract),
                 reads=[t_cl, t_lw], writes=[t_e3])
            S.op('scalar', lambda e: e.activation(out=e3[:, :], in_=e3[:, :], func=AF.Exp), reads=[t_e3], writes=[t_e3])
            def bdtile(name):
                tl, tt = AR.alloc([128, NCH, 128], BF16, name)
                S.op('gpsimd', lambda e, tl=tl: e.memset(tl[:, :, :], 0.0), writes=[tt])
                return tl, tt
            at_bd, t_at = bdtile('at_bd')
            bt_bd, t_bt = bdtile('bt_bd')
            kt_bd, t_kt = bdtile('kt_bd')
            v_bd, t_vbd = bdtile('v_bd')
            for h in range(2):
                PS_ = slice(h * 64, (h + 1) * 64)
                CS_ = slice(h * 64, (h + 1) * 64)
                def v3(x, PS_=PS_):
                    return x[PS_, 0:TP].rearrange("p (q t) -> p q t", t=64)
                S.op('vector', lambda e, PS_=PS_, CS_=CS_, v3=v3: e.scalar_tensor_tensor(
                    out=at_bd[PS_, :, CS_], in0=v3(kk), scalar=-1.0, in1=v3(e3), op0=ALU.mult, op1=ALU.mult),
                    reads=[t_kk, t_e3], writes=[t_at])
                S.op('vector', lambda e, PS_=PS_, CS_=CS_, v3=v3: e.tensor_tensor(
                    out=bt_bd[PS_, :, CS_], in0=v3(bb), in1=v3(e2), op=ALU.mult), reads=[t_bb, t_e2], writes=[t_bt])
                S.op('vector', lambda e, PS_=PS_, CS_=CS_, v3=v3: e.tensor_tensor(
                    out=kt_bd[PS_, :, CS_], in0=v3(k2), in1=v3(e2), op=ALU.mult), reads=[t_k2, t_e2], writes=[t_kt])
                S.op('scalar', lambda e, PS_=PS_, CS_=CS_, v3=v3: e.activation(
                    out=v_bd[PS_, :, CS_], in_=v3(v), func=AF.Copy), reads=[t_v], writes=[t_vbd])
            rt_c = t_rt = None
            if main:
                rt_c, t_rt = AR.alloc([128, NCH, 64], BF16, 'rt_c')
                S.op('vector', lambda e: e.tensor_tensor(out=rt_c[:, :, :].rearrange("p q t -> p (q t)"), in0=r[:, TS],
                                                         in1=e1[:, :], op=ALU.mult), reads=[t_r, t_e1], writes=[t_rt])

            def per_chunk_mm(width, mmfn, reads):
                per = 512 // width
                groups = []
                q = 0
                while q < NCH:
                    nq = min(per, NCH - q)
                    bi = bank()
                    for j in range(nq):
                        mmfn(q + j, ps[bi][:, j * width:(j + 1) * width], bi)
                    groups.append((bi, q, nq))
                    q += nq
                return groups

            def mm1(lhsT_fn, rhs_fn, reads):
                def f(q, out, bi):
                    S.op('tensor', lambda e, q=q, out=out: e.matmul(out, lhsT_fn(q), rhs_fn(q), start=True, stop=True),
                         reads=reads, writes=[t_ps[bi]])
                return f

            def evac(groups, width, fn, reads, writes, eng='vector'):
                for (bi, q0, nq) in groups:
                    src = ps[bi][:, 0:nq * width].rearrange("p (q w) -> p q w", w=width)
                    S.op(eng, lambda e, src=src, q0=q0, nq=nq: fn(e, src, q0, nq), reads=[t_ps[bi]] + reads, writes=writes)

            L = []
            Nm = []
            for i in range(2):
                a_, ta_ = AR.alloc([128, NCH, 128], BF16, 'L%d' % i)
                b_, tb_ = AR.alloc([128, NCH, 128], BF16, 'N%d' % i)
                L.append((a_, ta_))
                Nm.append((b_, tb_))
            AkT, t_AkT = AR.alloc([128, NCH, 128], BF16, 'AkT')
            mS3 = maskS[:, :].rearrange("p (q w) -> p q w", w=128)
            mST3 = maskST[:, :].rearrange("p (q w) -> p q w", w=128)
            mI3 = maskI[:, :].rearrange("p (q w) -> p q w", w=64)
            g = per_chunk_mm(128, mm1(lambda q: bt_bd[:, q, :], lambda q: at_bd[:, q, :], [t_bt, t_at]), None)
            evac(g, 128, lambda e, src, q0, nq: e.tensor_tensor(out=L[0][0][:, q0:q0 + nq, :], in0=src, in1=mS3[:, 0:nq, :],
                                                                op=ALU.mult), [t_c], [L[0][1]])
            g = per_chunk_mm(128, mm1(lambda q: at_bd[:, q, :], lambda q: bt_bd[:, q, :], [t_bt, t_at]), None)
            evac(g, 128, lambda e, src, q0, nq: e.tensor_tensor(out=Nm[0][0][:, q0:q0 + nq, :], in0=src, in1=mST3[:, 0:nq, :],
                                                                op=ALU.mult), [t_c], [Nm[0][1]])
            g = per_chunk_mm(128, mm1(lambda q: kt_bd[:, q, :], lambda q: at_bd[:, q, :], [t_kt, t_at]), None)
            evac(g, 128, lambda e, src, q0, nq: e.tensor_tensor(out=AkT[:, q0:q0 + nq, :], in0=src, in1=mS3[:, 0:nq, :],
                                                                op=ALU.mult), [t_c], [t_AkT])
            if main:
                ArbT, t_ArbT = AR.alloc([128, NCH, 64], BF16, 'ArbT')
                ArkT, t_ArkT = AR.alloc([128, NCH, 64], BF16, 'ArkT')
                g = per_chunk_mm(64, mm1(lambda q: bt_bd[:, q, :], lambda q: rt_c[:, q, :], [t_bt, t_rt]), None)
                evac(g, 64, lambda e, src, q0, nq: e.tensor_tensor(out=ArbT[:, q0:q0 + nq, :], in0=src, in1=mI3[:, 0:nq, :],
                                                                   op=ALU.mult), [t_c], [t_ArbT])
                g = per_chunk_mm(64, mm1(lambda q: kt_bd[:, q, :], lambda q: rt_c[:, q, :], [t_kt, t_rt]), None)
                evac(g, 64, lambda e, src, q0, nq: e.tensor_tensor(out=ArkT[:, q0:q0 + nq, :], in0=src, in1=mI3[:, 0:nq, :],
                                                                   op=ALU.mult), [t_c], [t_ArkT])
            def tr_groups(src_bd, t_src):
                groups = []
                q = 0
                while q < NCH:
                    nq = min(4, NCH - q)
                    bi = bank()
                    for j in range(nq):
                        S.op('tensor', lambda e, q=q, j=j, bi=bi: e.transpose(psb[bi][:, j * 128:(j + 1) * 128],
                                                                             src_bd[:, q + j, :], identb[:, :]),
                             reads=[t_src, t_c], writes=[t_ps[bi]])
                    groups.append((bi, q, nq))
                    q += nq
                return groups

            def evac_b(groups, fn, reads, writes, eng='vector'):
                for (bi, q0, nq) in groups:
                    src = psb[bi][:, 0:nq * 128].rearrange("p (q w) -> p q w", w=128)
                    S.op(eng, lambda e, src=src, q0=q0, nq=nq: fn(e, src, q0, nq), reads=[t_ps[bi]] + reads, writes=writes)

            X32, t_X32 = AR.alloc([128, NCH, 128], F32, 'X32')
            Xbf, t_Xbf = AR.alloc([128, NCH, 128], BF16, 'Xbf')
            btk, t_btk = AR.alloc([128, NCH, 128], BF16, 'btk')
            ktk, t_ktk = AR.alloc([128, NCH, 128], BF16, 'ktk')
            vtk_c, t_vtkc = AR.alloc([128, NCH, 64], BF16, 'vtk_c')
            g = tr_groups(at_bd, t_at)
            for h in range(2):
                PS_ = slice(h * 64, (h + 1) * 64)
                evac_b(g, lambda e, src, q0, nq, PS_=PS_: e.tensor_copy(out=X32[PS_, q0:q0 + nq, 0:64],
                                                                       in_=src[PS_, :, PS_]), [], [t_X32])
            g = tr_groups(bt_bd, t_bt)
            evac_b(g, lambda e, src, q0, nq: e.tensor_copy(out=btk[:, q0:q0 + nq, :], in_=src), [], [t_btk])
            g = tr_groups(kt_bd, t_kt)
            evac_b(g, lambda e, src, q0, nq: e.activation(out=ktk[:, q0:q0 + nq, :], in_=src, func=AF.Copy), [], [t_ktk],
                   eng='scalar')
            g = tr_groups(v_bd, t_vbd)
            vtk_bd = t_vtkbd = None
            if main:
                vtk_bd, t_vtkbd = AR.alloc([128, NCH, 128], BF16, 'vtk_bd')
                evac_b(g, lambda e, src, q0, nq: e.activation(out=vtk_bd[:, q0:q0 + nq, :], in_=src, func=AF.Copy), [],
                       [t_vtkbd], eng='scalar')
            for h in range(2):
                PS_ = slice(h * 64, (h + 1) * 64)
                evac_b(g, lambda e, src, q0, nq, PS_=PS_: e.tensor_copy(out=vtk_c[PS_, q0:q0 + nq, :],
                                                                       in_=src[PS_, :, PS_]), [], [t_vtkc])
            g = per_chunk_mm(64, mm1(lambda q: AkT[:, q, :], lambda q: vtk_c[:, q, :], [t_AkT, t_vtkc]), None)
            evac(g, 64, lambda e, src, q0, nq: e.tensor_copy(out=X32[:, q0:q0 + nq, 64:128], in_=src), [], [t_X32])
            S.op('scalar', lambda e: e.activation(out=Xbf[:, :, :], in_=X32[:, :, :], func=AF.Copy), reads=[t_X32],
                 writes=[t_Xbf])
            for lv in range(6):
                Lc, t_Lc = L[lv % 2]
                Nc, t_Nc = Nm[lv % 2]
                g = per_chunk_mm(128, mm1(lambda q, Lc=Lc: Lc[:, q, :], lambda q: Xbf[:, q, :], [t_Lc, t_Xbf]), None)
                evac(g, 128, lambda e, src, q0, nq: e.tensor_tensor(out=X32[:, q0:q0 + nq, :], in0=src,
                                                                    in1=X32[:, q0:q0 + nq, :], op=ALU.add), [t_X32], [t_X32])
                if lv < 5:
                    Ln, t_Ln = L[(lv + 1) % 2]
                    Nn, t_Nn = Nm[(lv + 1) % 2]
                    g1 = per_chunk_mm(128, mm1(lambda q, Nc=Nc: Nc[:, q, :], lambda q, Lc=Lc: Lc[:, q, :], [t_Lc, t_Nc]), None)
                    g2 = per_chunk_mm(128, mm1(lambda q, Lc=Lc: Lc[:, q, :], lambda q, Nc=Nc: Nc[:, q, :], [t_Lc, t_Nc]), None)
                    evac(g1, 128, lambda e, src, q0, nq, Ln=Ln: e.activation(out=Ln[:, q0:q0 + nq, :], in_=src, func=AF.Copy),
                         [], [t_Ln], eng='scalar')
                    evac(g2, 128, lambda e, src, q0, nq, Nn=Nn: e.tensor_copy(out=Nn[:, q0:q0 + nq, :], in_=src), [], [t_Nn],
                         eng='gpsimd' if False else 'vector')
                S.op('scalar', lambda e: e.activation(out=Xbf[:, :, :], in_=X32[:, :, :], func=AF.Copy), reads=[t_X32],
                     writes=[t_Xbf])
            Ah_bd, t_Ah = bdtile('Ah_bd')
            for h in range(2):
                PS_ = slice(h * 64, (h + 1) * 64)
                S.op('vector', lambda e, PS_=PS_: e.tensor_copy(out=Ah_bd[PS_, :, PS_], in_=Xbf[PS_, :, 0:64]),
                     reads=[t_Xbf], writes=[t_Ah])
            U0_bd = t_U0 = None
            if main:
                U0_bd, t_U0 = bdtile('U0_bd')
                for h in range(2):
                    PS_ = slice(h * 64, (h + 1) * 64)
                    S.op('vector', lambda e, PS_=PS_: e.tensor_copy(out=U0_bd[PS_, :, PS_], in_=Xbf[PS_, :, 64:128]),
                         reads=[t_Xbf], writes=[t_U0])
            TpT, t_TpT = AR.alloc([128, NCH, 128], BF16, 'TpT')
            g = per_chunk_mm(128, mm1(lambda q: Ah_bd[:, q, :], lambda q: btk[:, q, :], [t_Ah, t_btk]), None)
            evac(g, 128, lambda e, src, q0, nq: e.tensor_copy(out=TpT[:, q0:q0 + nq, :], in_=src), [], [t_TpT])
            G0p, t_G0 = AR.alloc([128, NCH, 64], F32, 'G0p')
            pc3 = e1[:, :].rearrange("p (q t) -> p q t", t=64)[:, :, 63:64]

            def g0mm(q, out, bi):
                S.op('tensor', lambda e, q=q, out=out: e.matmul(out, btk[:, q, :], Xbf[:, q, 64:128], start=True, stop=False),
                     reads=[t_btk, t_Xbf], writes=[t_ps[bi]])
                S.op('tensor', lambda e, q=q, out=out: e.matmul(out, ktk[:, q, :], vtk_c[:, q, :], start=False, stop=True),
                     reads=[t_ktk, t_vtkc], writes=[t_ps[bi]])
            g = per_chunk_mm(64, g0mm, None)
            evac(g, 64, lambda e, src, q0, nq: e.tensor_tensor(out=G0p[:, q0:q0 + nq, :], in0=src,
                                                               in1=pc3[:, q0:q0 + nq, :].to_broadcast([128, nq, 64]),
                                                               op=ALU.mult), [t_e1], [t_G0])
            RhT = t_RhT = None
            if main:
                RhT, t_RhT = AR.alloc([128, NCH, 64], BF16, 'RhT')
                g = per_chunk_mm(64, mm1(lambda q: Ah_bd[:, q, :], lambda q: ArbT[:, q, :], [t_Ah, t_ArbT]), None)
                evac(g, 64, lambda e, src, q0, nq: e.tensor_tensor(out=RhT[:, q0:q0 + nq, :], in0=src,
                                                                   in1=rt_c[:, q0:q0 + nq, :], op=ALU.add), [t_rt], [t_RhT])
            tmp, t_tmp = AR.alloc([128, 64], F32, 'chtmp')
            if main:
                for h in range(2):
                    PS_ = slice(h * 64, (h + 1) * 64)
                    S.op('vector', lambda e, PS_=PS_: e.tensor_copy(out=Hbd[PS_, hp, PS_], in_=H32[PS_, hp, :]),
                         reads=[t_H[hp]], writes=[t_H[hp]])
            for q in range(NCH):
                if main:
                    by = bank()
                    S.op('tensor', lambda e, q=q, by=by: e.matmul(ps[by][:, 0:64], U0_bd[:, q, :], ArbT[:, q, :],
                                                                  start=True, stop=False),
                         reads=[t_U0, t_ArbT], writes=[t_ps[by]])
                    S.op('tensor', lambda e, q=q, by=by: e.matmul(ps[by][:, 0:64], vtk_bd[:, q, :], ArkT[:, q, :],
                                                                  start=False, stop=False),
                         reads=[t_vtkbd, t_ArkT], writes=[t_ps[by]])
                    S.op('tensor', lambda e, q=q, by=by: e.matmul(ps[by][:, 0:64], Hbd[:, hp, :], RhT[:, q, :],
                                                                  start=False, stop=True),
                         reads=[t_H[hp], t_RhT], writes=[t_ps[by]])
                    S.op('scalar', lambda e, q=q, by=by: e.activation(out=y[:, q * 64:(q + 1) * 64], in_=ps[by][:, 0:64],
                                                                      func=AF.Copy), reads=[t_ps[by]], writes=[t_y])
                bc = bank()
                S.op('tensor', lambda e, q=q, bc=bc: e.matmul(ps[bc][:, 0:64], TpT[:, q, :], Hbf[:, hp, :],
                                                              start=True, stop=True),
                     reads=[t_TpT, t_H[hp]], writes=[t_ps[bc]])
                S.op('vector', lambda e, bc=bc: e.tensor_tensor(out=tmp[:, :], in0=ps[bc][:, 0:64], in1=H32[:, hp, :],
                                                                op=ALU.add), reads=[t_ps[bc], t_H[hp]], writes=[t_tmp])
                S.op('vector', lambda e, q=q: e.scalar_tensor_tensor(out=H32[:, hp, :], in0=tmp[:, :],
                                                                     scalar=e1[:, q * 64 + 63:q * 64 + 64], in1=G0p[:, q, :],
                                                                     op0=ALU.mult, op1=ALU.add),
                     reads=[t_tmp, t_e1, t_G0], writes=[t_H[hp]])
                S.op('scalar', lambda e: e.activation(out=Hbf[:, hp, :], in_=H32[:, hp, :], func=AF.Copy),
                     reads=[t_H[hp]], writes=[t_H[hp]])
                if main:
                    for h in range(2):
                        PS_ = slice(h * 64, (h + 1) * 64)
                        S.op('gpsimd', lambda e, PS_=PS_: e.tensor_copy(out=Hbd[PS_, hp, PS_], in_=H32[PS_, hp, :]),
                             reads=[t_H[hp]], writes=[t_H[hp]])

        def sample_step(hp, r, t_r, k2, t_k2, v, t_v, kk, t_kk, bb, t_bb, lw, t_lw, y, t_y):
            SC = slice(TP, NOS)
            Sst, t_S = AR.alloc([128, NS, 64], F32, 'Sst')
            with nc.allow_non_contiguous_dma(reason="state"):
                pass
            S.dma(hwq(), Sst[:, :, :], srwkv[:, 2 * hp:2 * hp + 2, :, :].rearrange("n h i j -> (h i) n j"), writes=[t_S])
            dec, t_dec = AR.alloc([128, NS], F32, 'dec')
            S.op('scalar', lambda e: e.activation(out=dec[:, :], in_=lw[:, SC], func=AF.Exp), reads=[t_lw], writes=[t_dec])
            nkk, t_nkk = AR.alloc([128, NS], F32, 'nkk')
            S.op('vector', lambda e: e.tensor_scalar(out=nkk[:, :], in0=kk[:, SC], scalar1=-1.0, scalar2=None, op0=ALU.mult),
                 reads=[t_kk], writes=[t_nkk])
            bc = {}
            dgc = [None]
            for nm, (src, t_src) in {'w': (dec[:, :], t_dec), 'nkk': (nkk[:, :], t_nkk), 'b': (bb[:, SC], t_bb),
                                     'k2': (k2[:, SC], t_k2), 'r': (r[:, SC], t_r)}.items():
                if dgc[0] is None:
                    dgc[0] = AR.alloc([128, NS, 64], F32, 'dg')
                dg, t_dg = dgc[0]
                S.op('vector', lambda e, src=src, dg=dg: e.tensor_tensor(
                    out=dg[:, :, :], in0=src.unsqueeze(2).to_broadcast([128, NS, 64]),
                    in1=i2[:, :].unsqueeze(1).to_broadcast([128, NS, 64]), op=ALU.mult), reads=[t_src, t_c], writes=[t_dg])
                xb, t_xb = AR.alloc([128, NS, 64], F32, 'xb' + nm)
                for hf in range(2):
                    bi = bank()
                    S.op('tensor', lambda e, bi=bi, dg=dg, hf=hf: e.matmul(
                        ps[bi][:, 0:512], bones[:, :], dg[:, hf * 8:(hf + 1) * 8, :].rearrange("p a b -> p (a b)"),
                        start=True, stop=True), reads=[t_c, t_dg], writes=[t_ps[bi]])
                    S.op('scalar', lambda e, bi=bi, xb=xb, hf=hf: e.activation(
                        out=xb[:, hf * 8:(hf + 1) * 8, :].rearrange("p a b -> p (a b)"), in_=ps[bi][:, 0:512], func=AF.Copy),
                        reads=[t_ps[bi]], writes=[t_xb])
                bc[nm] = (xb, t_xb)
            t1, t_t1 = AR.alloc([128, NS, 64], F32, 't1')
            sa, t_sa = AR.alloc([128, NS], F32, 'sa')
            S.op('vector', lambda e: e.tensor_tensor(out=t1[:, :, :], in0=Sst[:, :, :], in1=bc['nkk'][0][:, :, :], op=ALU.mult),
                 reads=[t_S, bc['nkk'][1]], writes=[t_t1])
            S.op('vector', lambda e: e.tensor_reduce(out=sa[:, :], in_=t1[:, :, :], axis=AX.X, op=ALU.add),
                 reads=[t_t1], writes=[t_sa])
            S.op('vector', lambda e: e.tensor_tensor(out=Sst[:, :, :], in0=Sst[:, :, :], in1=bc['w'][0][:, :, :], op=ALU.mult),
                 reads=[t_S, bc['w'][1]], writes=[t_S])
            S.op('vector', lambda e: e.tensor_tensor(out=t1[:, :, :], in0=bc['b'][0][:, :, :],
                                                     in1=sa[:, :].unsqueeze(2).to_broadcast([128, NS, 64]), op=ALU.mult),
                 reads=[t_sa, bc['b'][1]], writes=[t_t1])
            S.op('vector', lambda e: e.tensor_tensor(out=Sst[:, :, :], in0=Sst[:, :, :], in1=t1[:, :, :], op=ALU.add),
                 reads=[t_S, t_t1], writes=[t_S])
            S.op('vector', lambda e: e.tensor_tensor(out=t1[:, :, :], in0=bc['k2'][0][:, :, :],
                                                     in1=v[:, SC].unsqueeze(2).to_broadcast([128, NS, 64]), op=ALU.mult),
                 reads=[t_v, bc['k2'][1]], writes=[t_t1])
            S.op('vector', lambda e: e.tensor_tensor(out=Sst[:, :, :], in0=Sst[:, :, :], in1=t1[:, :, :], op=ALU.add),
                 reads=[t_S, t_t1], writes=[t_S])
            S.op('vector', lambda e: e.tensor_tensor(out=t1[:, :, :], in0=Sst[:, :, :], in1=bc['r'][0][:, :, :], op=ALU.mult),
                 reads=[t_S, bc['r'][1]], writes=[t_t1])
            S.op('vector', lambda e: e.tensor_reduce(out=y[:, SC], in_=t1[:, :, :], axis=AX.X, op=ALU.add),
                 reads=[t_t1], writes=[t_y])
            S.dma(hwq(), o_rwkvs[:, 2 * hp:2 * hp + 2, :, :].rearrange("n h i j -> (h i) n j"), Sst[:, :, :],
                  reads=[t_S], writes=[T()])

        def pool_stage(p, last):
            AR.reset()
            mp = p - NPH
            posb, t_posb = AR.alloc([128, TP], F32, 'posb')
            S.dma('sync', posb[:, :], pos[0:1, mp * TP:(mp + 1) * TP].partition_broadcast(128).rearrange("p o n -> p (o n)"),
                  writes=[t_posb])
            rc, t_rc = AR.alloc([128, 4, TP], F32, 'rc')
            for gi in range(4):
                win = float(2 ** (gi + 1))
                S.op('vector', lambda e, gi=gi, win=win: e.tensor_scalar(out=rc[:, gi, :], in0=posb[:, :], scalar1=1.0,
                                                                        scalar2=win, op0=ALU.add, op1=ALU.min),
                     reads=[t_posb], writes=[t_rc])
            S.op('vector', lambda e: e.reciprocal(out=rc[:, :, :], in_=rc[:, :, :]), reads=[t_rc], writes=[t_rc])
            base_off = AR.off
            base_tiles = list(AR.tiles)
            for gi in range(4):
                keep = AR.tiles[:len(base_tiles)]
                AR.reset()
                AR.off = base_off
                AR.tiles = keep
                win = 2 ** (gi + 1)
                pooled, t_pl = AR.alloc([128, 4, W], BF16, 'pooled')
                for ct in range(4):
                    tix = gi * 4 + ct
                    wv, wt, KC = inproj(C_U + tix * 128)
                    bi = proj(wv, wt, KC, 0, 128, xT, [t_xT], slice(0, W))
                    u, t_u = AR.alloc([128, W], F32, 'u')
                    S.op('scalar', lambda e, bi=bi, u=u: e.activation(out=u[:, :], in_=ps[bi][:, 0:W], func=AF.Copy),
                         reads=[t_ps[bi]], writes=[t_u])
                    if last:
                        tail_out(u, t_u, o_tailu, tix * 128)
                    s_prev, t_sp = u, t_u
                    step = 1
                    while step < win:
                        s_new, t_sn = AR.alloc([128, W], F32, 's')
                        lo = 2 * step - 1
                        S.op('vector' if step % 4 == 1 else 'gpsimd', lambda e, s_prev=s_prev, s_new=s_new, step=step, lo=lo:
                             e.tensor_tensor(out=s_new[:, lo:HALO + TP], in0=s_prev[:, lo:HALO + TP],
                                             in1=s_prev[:, lo - step:HALO + TP - step], op=ALU.add),
                             reads=[t_sp], writes=[t_sn])
                        s_prev, t_sp = s_new, t_sn
                        step *= 2
                    tmpw, t_tw = AR.alloc([128, TP], F32, 'tmpw')
                    S.op('vector', lambda e, s_prev=s_prev, tmpw=tmpw, gi=gi: e.tensor_tensor(
                        out=tmpw[:, :], in0=s_prev[:, OWN], in1=rc[:, gi, :], op=ALU.mult), reads=[t_sp, t_rc], writes=[t_tw])
                    S.op('vector', lambda e, tmpw=tmpw, u=u, ct=ct, pooled=pooled: e.tensor_tensor(
                        out=pooled[:, ct, OWN], in0=tmpw[:, :], in1=u[:, OWN], op=ALU.subtract),
                        reads=[t_tw, t_u], writes=[t_pl])
                    if last:
                        stk, t_stk = AR.alloc([128, 2, 128], F32, 'pstk')
                        sv_ = spool.rearrange("n t c -> (n t) c")
                        S.dma(hwq(), stk[:, 0, :], sv_[0:128, tix * 128:(tix + 1) * 128], writes=[t_stk])
                        S.dma(hwq(), stk[0:112, 1, :], sv_[128:240, tix * 128:(tix + 1) * 128], writes=[t_stk])
                        pT, t_pT = AR.alloc([128, 256], F32, 'pT')
                        bi = bank()
                        S.op('tensor', lambda e, bi=bi, stk=stk: e.transpose(ps[bi][:, 0:128], stk[:, 0, :], ident[:, :]),
                             reads=[t_stk, t_c], writes=[t_ps[bi]])
                        S.op('tensor', lambda e, bi=bi, stk=stk: e.transpose(ps[bi][:, 128:240], stk[0:112, 1, :],
                                                                             ident[0:112, 0:112]),
                             reads=[t_stk, t_c], writes=[t_ps[bi]])
                        S.op('vector', lambda e, bi=bi, pT=pT: e.tensor_copy(out=pT[:, 0:240], in_=ps[bi][:, 0:240]),
                             reads=[t_ps[bi]], writes=[t_pT])
                        ws, t_ws = AR.alloc([128, NS], F32, 'ws')
                        pT3 = pT[:, 0:240].rearrange("p (n t) -> p n t", t=15)
                        S.op('vector', lambda e, pT3=pT3, ws=ws, win=win: e.tensor_reduce(
                            out=ws[:, :], in_=pT3[:, :, 15 - (win - 1):15], axis=AX.X, op=ALU.add), reads=[t_pT], writes=[t_ws])
                        S.op('vector', lambda e, ws=ws, u=u: e.tensor_tensor(out=ws[:, :], in0=ws[:, :], in1=u[:, SMP], op=ALU.add),
                             reads=[t_ws, t_u], writes=[t_ws])
                        S.op('vector', lambda e, ws=ws, u=u, ct=ct, pooled=pooled, win=win: e.scalar_tensor_tensor(
                            out=pooled[:, ct, SMP], in0=ws[:, :], scalar=1.0 / win, in1=u[:, SMP], op0=ALU.mult,
                            op1=ALU.subtract), reads=[t_ws, t_u], writes=[t_pl])
                pwv = pwb[:, :].rearrange("p (a b) -> p a b", b=512)
                pwt, pKC = t_pwb, 4
                S.dma('gpsimd', pwv, pool_w[gi].rearrange("(c p) n -> p c n", p=128), writes=[t_pwb])
                for dc in range(4):
                    tix = gi * 4 + dc
                    wv, wt, KC = inproj(C_ZP + tix * 128)
                    bz = proj(wv, wt, KC, 0, 128, xT, [t_xT], slice(0, W))
                    sz, t_sz = AR.alloc([128, NOS], F32, 'szp')
                    S.op('scalar', lambda e, bz=bz, sz=sz: e.activation(out=sz[:, :], in_=ps[bz][:, HALO:HALO + NOS], func=AF.Silu),
                         reads=[t_ps[bz]], writes=[t_sz])
                    bm = proj(pwv, pwt, pKC, dc * 128, (dc + 1) * 128, pooled, [t_pl], OS)
                    S.op('vector', lambda e, bm=bm, sz=sz, tix=tix: e.scalar_tensor_tensor(
                        out=bo[:, tix, OS], in0=ps[bm][:, 0:NOS], scalar=prm['pscale'][:, tix:tix + 1], in1=sz[:, :],
                        op0=ALU.mult, op1=ALU.mult), reads=[t_ps[bm], t_sz, t_c], writes=[t_bo[tix]])
            if last:
                S.dma('sync', o_pools[:, :, :], spool[:, 1:15, :], writes=[T()])

        def mem_stage(p, last):
            AR.reset()
            qT, t_qT = AR.alloc([128, 12, W], BF16, 'qT')
            for c in range(12):
                wv, wt, KC = inproj(C_Q + c * 128)
                bi = proj(wv, wt, KC, 0, 128, xT, [t_xT], slice(0, W))
                S.op('scalar', lambda e, bi=bi, c=c: e.activation(out=qT[:, c, :], in_=ps[bi][:, 0:W], func=AF.Copy,
                                                                  scale=float(MHD) ** -0.5), reads=[t_ps[bi]], writes=[t_qT])
            szs = t_szs = None
            if last:
                szs, t_szs = AR.alloc([128, 12, NS], F32, 'szs')
            base_off = AR.off
            base_tiles = list(AR.tiles)
            for h in range(4):
                keep = AR.tiles[:len(base_tiles)]
                AR.reset()
                AR.off = base_off
                AR.tiles = keep
                eT, t_eT = AR.alloc([128, 2, W], BF16, 'eT')
                for mt in range(2):
                    bi = bank()
                    for dt in range(3):
                        S.op('tensor', lambda e, bi=bi, mt=mt, dt=dt, h=h: e.matmul(
                            ps[bi][:, 0:W], KT[:, h * 3 + dt, mt * 128:(mt + 1) * 128], qT[:, h * 3 + dt, :],
                            start=(dt == 0), stop=(dt == 2)), reads=[t_KT, t_qT], writes=[t_ps[bi]])
                    S.op('scalar', lambda e, bi=bi, mt=mt, eT=eT: e.activation(out=eT[:, mt, :], in_=ps[bi][:, 0:W], func=AF.Exp),
                         reads=[t_ps[bi]], writes=[t_eT])
                bd_ = bank()
                for mt in range(2):
                    S.op('tensor', lambda e, bd_=bd_, mt=mt, eT=eT: e.matmul(ps[bd_][:, 0:W], onesb[:, :], eT[:, mt, :],
                                                                             start=(mt == 0), stop=(mt == 1)),
                         reads=[t_c, t_eT], writes=[t_ps[bd_]])
                rden, t_rden = AR.alloc([128, W], F32, 'rden')
                S.op('vector', lambda e, bd_=bd_, rden=rden: e.reciprocal(out=rden[:, :], in_=ps[bd_][:, 0:W]),
                     reads=[t_ps[bd_]], writes=[t_rden])
                for dt in range(3):
                    c = h * 3 + dt
                    wv, wt, KC = inproj(C_ZM + c * 128)
                    bz = proj(wv, wt, KC, 0, 128, xT, [t_xT], slice(0, W))
                    sz, t_sz = AR.alloc([128, W], F32, 'szm')
                    S.op('scalar', lambda e, bz=bz, sz=sz: e.activation(out=sz[:, :], in_=ps[bz][:, 0:W], func=AF.Silu),
                         reads=[t_ps[bz]], writes=[t_sz])
                    if last:
                        S.op('gpsimd', lambda e, sz=sz, c=c: e.tensor_copy(out=szs[:, c, :], in_=sz[:, SMP]),
                             reads=[t_sz], writes=[t_szs])
                    bo_ = bank()
                    for mt in range(2):
                        S.op('tensor', lambda e, bo_=bo_, mt=mt, c=c, eT=eT: e.matmul(
                            ps[bo_][:, 0:W], Vb[:, mt, c * 128:(c + 1) * 128], eT[:, mt, :], start=(mt == 0), stop=(mt == 1)),
                            reads=[t_Vb, t_eT], writes=[t_ps[bo_]])
                    tmp, t_tmp = AR.alloc([128, W], F32, 'otmp')
                    S.op('vector', lambda e, bo_=bo_, tmp=tmp, rden=rden: e.tensor_tensor(
                        out=tmp[:, :], in0=ps[bo_][:, 0:W], in1=rden[:, :], op=ALU.mult), reads=[t_ps[bo_], t_rden], writes=[t_tmp])
                    S.op('vector', lambda e, tmp=tmp, sz=sz, c=c: e.tensor_tensor(
                        out=bo[:, c, :], in0=tmp[:, :], in1=sz[:, :], op=ALU.mult), reads=[t_tmp, t_sz], writes=[t_bo[c]])
            if last:
                keep = AR.tiles[:len(base_tiles)]
                AR.reset()
                AR.off = base_off
                AR.tiles = keep
                sample_attn(qT, t_qT, szs, t_szs)

        def sample_attn(qT, t_qT, szs, t_szs):
            qtok, t_qtok = AR.alloc([NS, MW], F32, 'qtok')
            for g in range(3):
                bi = bank()
                for j in range(4):
                    c = g * 4 + j
                    S.op('tensor', lambda e, bi=bi, j=j, c=c: e.transpose(psb[bi][0:NS, j * 128:(j + 1) * 128], qT[:, c, SMP],
                                                                         identb[:, :]), reads=[t_qT, t_c], writes=[t_ps[bi]])
                S.op('vector', lambda e, bi=bi, g=g: e.tensor_copy(out=qtok[:, g * 512:(g + 1) * 512], in_=psb[bi][0:NS, 0:512]),
                     reads=[t_ps[bi]], writes=[t_qtok])
            E, t_E = AR.alloc([128, NS, 2, 4], F32, 'E')
            oT, t_oT = AR.alloc([128, 12, NS], F32, 'oT')
            base_off = AR.off
            base_tiles = list(AR.tiles)
            for n in range(NS):
                keep = AR.tiles[:len(base_tiles)]
                AR.reset()
                AR.off = base_off
                AR.tiles = keep
                Kn, t_Kn = AR.alloc([128, 2, MW], F32, 'Kn')
                S.dma(hwq(), Kn[:, :, :], ck[n].rearrange("(a p) d -> p a d", p=128), writes=[t_Kn])
                Vn, t_Vn = AR.alloc([128, 2, MW], F32, 'Vn')
                S.dma(hwq(), Vn[:, :, :], cv[n].rearrange("(a p) d -> p a d", p=128), writes=[t_Vn])
                bq = [bank(), bank(), bank()]
                for g in range(3):
                    S.op('tensor', lambda e, g=g, n=n: e.matmul(pst[:, g * 512:(g + 1) * 512], sel[:, n * 128:(n + 1) * 128],
                                                                qtok[:, g * 512:(g + 1) * 512], start=True, stop=True),
                         reads=[t_c, t_qtok], writes=[t_ps[g]])
                S.op('vector', lambda e, Kn=Kn: e.tensor_tensor(
                    out=Kn[:, :, :], in0=Kn[:, :, :], in1=pst[:, 0:MW].unsqueeze(1).to_broadcast([128, 2, MW]), op=ALU.mult),
                    reads=[t_Kn, t_ps[0], t_ps[1], t_ps[2]], writes=[t_Kn])
                sc, t_sc = AR.alloc([128, 2, 4], F32, 'sc')
                S.op('vector', lambda e, Kn=Kn, sc=sc: e.tensor_reduce(
                    out=sc[:, :, :], in_=Kn[:, :, :].rearrange("p a (h d) -> p a h d", d=MHD), axis=AX.X, op=ALU.add),
                    reads=[t_Kn], writes=[t_sc])
                S.op('scalar', lambda e, sc=sc, n=n: e.activation(out=E[:, n, :, :], in_=sc[:, :, :], func=AF.Exp),
                     reads=[t_sc], writes=[t_E])
                bo_ = bank()
                for c in range(12):
                    h = c // 3
                    for mt in range(2):
                        S.op('tensor', lambda e, bo_=bo_, c=c, h=h, mt=mt, n=n, Vn=Vn: e.matmul(
                            ps[bo_][:, c:c + 1], Vn[:, mt, c * 128:(c + 1) * 128], E[:, n, mt, h:h + 1],
                            start=(mt == 0), stop=(mt == 1)), reads=[t_Vn, t_E], writes=[t_ps[bo_]])
                S.op('vector', lambda e, bo_=bo_, n=n: e.tensor_copy(out=oT[:, :, n], in_=ps[bo_][:, 0:12]),
                     reads=[t_ps[bo_]], writes=[t_oT])
            bd_ = bank()
            for mt in range(2):
                S.op('tensor', lambda e, bd_=bd_, mt=mt: e.matmul(
                    ps[bd_][:, 0:NS * 4].rearrange("p (n h) -> p n h", h=4), bones[:, :], E[:, :, mt, :],
                    start=(mt == 0), stop=False), reads=[t_c, t_E], writes=[t_ps[bd_]])
            S.op('tensor', lambda e, bd_=bd_: e.matmul(
                ps[bd_][:, 0:NS * 4].rearrange("p (n h) -> p n h", h=4), obones[:, :], E[:, :, 0, :], start=False, stop=False),
                reads=[t_c, t_E], writes=[t_ps[bd_]])
            S.op('tensor', lambda e, bd_=bd_: e.matmul(
                ps[bd_][:, 0:NS * 4].rearrange("p (n h) -> p n h", h=4), obones[:, :], E[:, :, 1, :], start=False, stop=True),
                reads=[t_c, t_E], writes=[t_ps[bd_]])
            rd, t_rd = AR.alloc([128, NS, 4], F32, 'rd')
            S.op('vector', lambda e, bd_=bd_: e.reciprocal(out=rd[:, :, :], in_=ps[bd_][:, 0:NS * 4].rearrange("p (n h) -> p n h", h=4)),
                 reads=[t_ps[bd_]], writes=[t_rd])
            for c in range(12):
                h = c // 3
                S.op('vector', lambda e, c=c, h=h: e.tensor_tensor(out=oT[:, c, :], in0=oT[:, c, :], in1=rd[:, :, h], op=ALU.mult),
                     reads=[t_oT, t_rd], writes=[t_oT])
                S.op('vector', lambda e, c=c: e.tensor_tensor(out=bo[:, c, SMP], in0=oT[:, c, :], in1=szs[:, c, :], op=ALU.mult),
                     reads=[t_oT, t_szs], writes=[t_bo[c]])

        obones = sb("obones", [128, 128])
        if int(os.environ.get('KB', '99')) >= 8:
          S.op('vector', lambda e: e.tensor_scalar(out=obones[:], in0=bones[:], scalar1=-1.0, scalar2=1.0, op0=ALU.mult,
                                                 op1=ALU.add), reads=[t_c], writes=[t_c])

        def final_stage(p, last):
            AR.reset()
            dump(hT, 32, t_hT, 44)
            AR.reset()
            mp = p - NPH
            ntt = TP // 128
            vt = [t_vscr[mp * ntt + i] for i in range(ntt)]
            for g4 in range(8):
                subT, t_sub = AR.alloc([128, 4, W], F32, 'subT')
                for j in range(4):
                    c = g4 * 4 + j
                    wv, wt, kp, KC = wload(w_out[:, c * 128:(c + 1) * 128], D, 128)
                    bi = proj(wv, wt, KC, 0, 128, hT, t_hT, slice(0, W))
                    S.op('scalar' if j % 2 else 'vector',
                         (lambda e, bi=bi, j=j, subT=subT: e.activation(out=subT[:, j, :], in_=ps[bi][:, 0:W], func=AF.Copy))
                         if j % 2 else
                         (lambda e, bi=bi, j=j, subT=subT: e.tensor_copy(out=subT[:, j, :], in_=ps[bi][:, 0:W])),
                         reads=[t_ps[bi]], writes=[t_sub])
                tiles = [(HALO + i * 128, 128, y_own[mp * TP + i * 128: mp * TP + (i + 1) * 128, :], vt[i]) for i in range(ntt)]
                if last:
                    tiles.append((HALO + TP, NS, y_s[:, :], t_vscr[-1]))
                for (c0, nr, dst, tv) in tiles:
                    bi = bank()
                    for j in range(4):
                        S.op('tensor', lambda e, bi=bi, j=j, c0=c0, nr=nr, subT=subT: e.transpose(
                            ps[bi][0:nr, j * 128:(j + 1) * 128], subT[:, j, c0:c0 + nr], ident[:, :]),
                            reads=[t_sub, t_c], writes=[t_ps[bi]])
                    stg, t_stg = AR.alloc([128, 512], F32, 'stg')
                    S.op('vector', lambda e, bi=bi, nr=nr, stg=stg: e.tensor_copy(out=stg[0:nr, :], in_=ps[bi][0:nr, 0:512]),
                         reads=[t_ps[bi]], writes=[t_stg])
                    S.dma(hwq(), dst[:, g4 * 512:(g4 + 1) * 512], stg[0:nr, :], reads=[t_stg], writes=[tv])
                if g4 % 2 == 1:
                    AR.reset()
            chk(8)
            AR.reset()
            gb, t_gb = AR.alloc([128, 2, D], F32, 'gb')
            S.dma('sync', gb[:, 0, :], ln_g[0:1, :].partition_broadcast(128).rearrange("p o n -> p (o n)"), writes=[t_gb])
            S.dma('scalar', gb[:, 1, :], ln_b[0:1, :].partition_broadcast(128).rearrange("p o n -> p (o n)"), writes=[t_gb])
            base_off = AR.off
            base_tiles = list(AR.tiles)
            tiles = [(128, y_own[mp * TP + i * 128: mp * TP + (i + 1) * 128, :], vt[i],
                      xin[HALO + p * TP + i * 128: HALO + p * TP + (i + 1) * 128, :]) for i in range(ntt)]
            if last:
                tiles.append((NS, y_s[:, :], t_vscr[-1], xs[:, :]))
            for (nr, dst, tv, xsrc) in tiles:
                keep = AR.tiles[:len(base_tiles)]
                AR.reset()
                AR.off = base_off
                AR.tiles = keep
                vrow, t_vr = AR.alloc([128, D], F32, 'vrow')
                S.dma('sync', vrow[0:nr, :], dst, reads=[tv], writes=[t_vr])
                for hlf in range(2):
                    xrow, t_xr = AR.alloc([128, 2048], F32, 'xrow2')
                    S.dma('scalar', xrow[0:nr, :], xsrc[:, hlf * 2048:(hlf + 1) * 2048], writes=[t_xr])
                    S.op('vector', lambda e, nr=nr, vrow=vrow, xrow=xrow, hlf=hlf: e.scalar_tensor_tensor(
                        out=vrow[0:nr, hlf * 2048:(hlf + 1) * 2048], in0=xrow[0:nr, :], scalar=ALPHA,
                        in1=vrow[0:nr, hlf * 2048:(hlf + 1) * 2048], op0=ALU.mult, op1=ALU.add),
                        reads=[t_xr, t_vr], writes=[t_vr])
                stt, t_stt = AR.alloc([128, 8, 6], F32, 'stt')
                for j in range(8):
                    S.op('vector', lambda e, nr=nr, vrow=vrow, stt=stt, j=j: e.bn_stats(out=stt[0:nr, j, :],
                                                                                      in_=vrow[0:nr, j * 512:(j + 1) * 512]),
                         reads=[t_vr], writes=[t_stt])
                mv, t_mv = AR.alloc([128, 4], F32, 'mv')
                S.op('vector', lambda e, nr=nr, stt=stt, mv=mv: e.bn_aggr(out=mv[0:nr, 0:2],
                                                                         in_=stt[0:nr, :, :].rearrange("p a b -> p (a b)")),
                     reads=[t_stt], writes=[t_mv])
                S.op('vector', lambda e, nr=nr, mv=mv: e.tensor_scalar(out=mv[0:nr, 2:3], in0=mv[0:nr, 1:2], scalar1=LN_EPS,
                                                                       scalar2=None, op0=ALU.add), reads=[t_mv], writes=[t_mv])
                S.op('scalar', lambda e, nr=nr, mv=mv: e.activation(out=mv[0:nr, 2:3], in_=mv[0:nr, 2:3], func=AF.Sqrt),
                     reads=[t_mv], writes=[t_mv])
                S.op('vector', lambda e, nr=nr, mv=mv: e.reciprocal(out=mv[0:nr, 2:3], in_=mv[0:nr, 2:3]),
                     reads=[t_mv], writes=[t_mv])
                S.op('vector', lambda e, nr=nr, vrow=vrow, mv=mv: e.tensor_scalar(
                    out=vrow[0:nr, :], in0=vrow[0:nr, :], scalar1=mv[0:nr, 0:1], scalar2=mv[0:nr, 2:3], op0=ALU.subtract,
                    op1=ALU.mult), reads=[t_vr, t_mv], writes=[t_vr])
                S.op('gpsimd', lambda e, nr=nr, vrow=vrow: e.tensor_tensor(out=vrow[0:nr, :], in0=vrow[0:nr, :],
                                                                           in1=gb[0:nr, 0, :], op=ALU.mult),
                     reads=[t_vr, t_gb], writes=[t_vr])
                S.op('vector', lambda e, nr=nr, vrow=vrow: e.tensor_tensor(out=vrow[0:nr, :], in0=vrow[0:nr, :],
                                                                           in1=gb[0:nr, 1, :], op=ALU.add),
                     reads=[t_vr, t_gb], writes=[t_vr])
                S.dma('sync', dst, vrow[0:nr, :], reads=[t_vr], writes=[tv])

        KSTOP = int(os.environ.get('KSTOP', '99'))

        class StopBuild(Exception):
            pass

        def chk(level):
            if KSTOP <= level:
                raise StopBuild()
        stopped = False
        try:
            chk(1)
            for p in range(NPASS):
                run_pass(p)
        except StopBuild:
            stopped = True
        AR.reset()
        for hp in range(16 if not stopped else 0):
            bi = bank()
            S.op('tensor', lambda e, bi=bi, hp=hp: e.transpose(ps[bi][0:64, 0:128], H32[:, hp, :], ident[:, :]),
                 reads=[t_H[hp], t_c], writes=[t_ps[bi]])
            ho, t_ho = AR.alloc([64, 128], F32, 'ho')
            S.op('vector', lambda e, bi=bi, ho=ho: e.tensor_copy(out=ho[:, :], in_=ps[bi][0:64, 0:128]),
                 reads=[t_ps[bi]], writes=[t_ho])
            if not os.environ.get('KNOD'):
                S.dma('sync', o_rwkvp[2 * hp:2 * hp + 2, :, :].rearrange("h i j -> i h j"),
                      ho[:, :].rearrange("p (h j) -> p h j", j=64), reads=[t_ho], writes=[T()])
        S.finish()
        print("ninstr", S.ninstr, {e: len(S.prog[e]) for e in ENGS})
        S.emit(block)
    return nc


_CACHE = {}


def kernel(**inp):
    x_prompt = np.asarray(inp['x_prompt'], np.float32)
    B, SEQ, _ = x_prompt.shape
    T_OWN = SEQ // 2
    TP = min(256, T_OWN)
    key = (T_OWN, TP)
    if key not in _CACHE:
        _CACHE[key] = build(T_OWN, TP)
    nc = _CACHE[key]
    consts = make_consts(TP)
    f = lambda a: np.ascontiguousarray(np.asarray(a, np.float32))
    shared = {}
    for nm in ['w_in', 'b_gate', 'pool_w', 'pool_scale', 'rwkv_mu', 'rwkv_w0', 'rwkv_w2', 'rwkv_a0', 'rwkv_a2', 'rwkv_k_k',
               'rwkv_k_a', 'rwkv_r_k', 'rwkv_ln_w', 'rwkv_ln_b', 'w_mem_kv', 'w_branch_pool', 'w_branch_rwkv',
               'w_branch_mem', 'w_out']:
        a = f(inp[nm])[0]
        if nm == 'rwkv_r_k':
            a = a.reshape(-1)
        shared[nm] = np.ascontiguousarray(a)
    shared['ln_g'] = f(inp['ln_g'])[0].reshape(1, D)
    shared['ln_b'] = f(inp['ln_b'])[0].reshape(1, D)
    for k_, v_ in consts.items():
        shared['c_' + k_] = v_
    in_maps = []
    for c in range(8):
        s, hf = c // 2, c % 2
        xin = np.zeros((HALO + 2 * T_OWN, D), np.float32)
        if hf == 0:
            xin[HALO + T_OWN:] = x_prompt[s, 0:T_OWN]
        else:
            xin[HALO:] = x_prompt[s]
        m = dict(shared)
        m['xin'] = xin
        sl = slice(c * NS, (c + 1) * NS)
        m['xs'] = f(inp['x_sample'])[sl, 0]
        m['pos'] = (np.arange(T_OWN, dtype=np.float32) + hf * T_OWN).reshape(1, T_OWN)
        m['memx'] = f(inp['mem_prompt'])[s]
        m['ck'] = f(inp['cache_mem_k'])[0, sl].reshape(NS, NMEM, MW)
        m['cv'] = f(inp['cache_mem_v'])[0, sl].reshape(NS, NMEM, MW)
        m['spool'] = f(inp['state_pool'])[0, sl]
        m['sshift'] = f(inp['state_shift'])[0, sl, 0]
        m['srwkv'] = f(inp['state_rwkv'])[0, sl]
        in_maps.append({k_: np.ascontiguousarray(v_) for k_, v_ in m.items()})
    kc = os.environ.get('KCORES')
    if kc is not None:
        sel_ = [int(x) for x in kc.split(',')]
        res = run_bass_kernel_spmd(nc, [in_maps[i] for i in sel_], core_ids=list(range(len(sel_))), trace=bool(os.environ.get('KTRACE')))
        print('exec_time_ns', getattr(res, 'exec_time_ns', None))
        return {sel_[i]: res.results[i] for i in range(len(sel_))}
    res = run_bass_kernel_spmd(nc, in_maps, core_ids=list(range(8)))
    R = res.results
    DEC = 8 * NS
    yp = np.zeros((B, SEQ, D), np.float32)
    ys = np.zeros((DEC, 1, D), np.float32)
    mk = np.zeros((1, B, NMEM, 4, MHD), np.float32)
    mv = np.zeros((1, B, NMEM, 4, MHD), np.float32)
    pp = np.zeros((1, B, 15, PW), np.float32)
    shp = np.zeros((1, B, 1, SHW), np.float32)
    sp = np.zeros((1, B, 32, 64, 64), np.float32)
    psm = np.zeros((1, DEC, 15, PW), np.float32)
    shs = np.zeros((1, DEC, 1, SHW), np.float32)
    ss = np.zeros((1, DEC, 32, 64, 64), np.float32)
    for c in range(8):
        s, hf = c // 2, c % 2
        r = R[c]
        yp[s, hf * T_OWN:(hf + 1) * T_OWN] = r['y_own']
        sl = slice(c * NS, (c + 1) * NS)
        ys[sl, 0] = r['y_s']
        psm[0, sl, 0:14] = r['o_pools']
        psm[0, sl, 14] = r['o_tailu'][16:32]
        shs[0, sl, 0] = r['o_tailsh'][16:32]
        ss[0, sl] = r['o_rwkvs']
        if hf == 0:
            mk[0, s] = r['o_mk'].reshape(NMEM, 4, MHD)
            mv[0, s] = r['o_mv'].reshape(NMEM, 4, MHD)
        else:
            pp[0, s] = r['o_tailu'][1:16]
            shp[0, s, 0] = r['o_tailsh'][15]
            sp[0, s] = r['o_rwkvp']
    return (yp, ys, mk, mv, pp, shp, sp, psm, shs, ss)
```

```python
import contextlib
import os
import numpy as np
import concourse.bass as bass
import concourse.mybir as mybir
from concourse.bass_utils import run_bass_kernel_spmd

F32 = mybir.dt.float32
BF16 = mybir.dt.bfloat16
AF = mybir.ActivationFunctionType
ALU = mybir.AluOpType
AX = mybir.AxisListType

ENGS = ['tensor', 'vector', 'scalar', 'gpsimd', 'sync']

D = 4096
PW = 2048
RW = 2048
LORA = 96
SHW = 3 * RW + 2 * LORA
MW = 1536
MHD = 384
NMEM = 256
INC = 27840
C_U, C_ZP, C_R, C_K, C_V, C_WD, C_AD, C_ZR, C_Q, C_ZM, C_G = 0, 2048, 4096, 6144, 8192, 10240, 10336, 10432, 12480, 14016, 15552
GN_EPS = 64e-5
LN_EPS = 1e-5
ALPHA = 2.0 ** 0.25
NS = 16
HALO = 16


class T:
    __slots__ = ('name', 'last_w', 'readers', 'excl')

    def __init__(self, name='', excl=False):
        self.name = name
        self.last_w = None
        self.readers = []
        self.excl = excl


class Sched:
    def __init__(self, nc, n_dma_sems=40, strict=('vector', 'scalar', 'gpsimd')):
        self.nc = nc
        self.prog = {e: [] for e in ENGS}
        self.cnt = {e: 0 for e in ENGS}
        self.waited = {e: {} for e in ENGS}
        self.strict = set(strict)
        self.n_dma_sems = n_dma_sems
        self.dma_cnt = [0] * n_dma_sems
        self.dma_rr = 0
        self.sems = {}
        self.ninstr = 0
        self.relay = None

    def alloc_sems(self, stack):
        for e in ENGS:
            self.sems[e] = stack.enter_context(self.nc.semaphore('s_' + e))
        for i in range(self.n_dma_sems):
            self.sems[('d', i)] = stack.enter_context(self.nc.semaphore('d%d' % i))

    def _wait(self, eng, key, val):
        if val <= 0:
            return
        w = self.waited[eng]
        if w.get(key, 0) >= val:
            return
        w[key] = val
        sem = self.sems[key]
        self.prog[eng].append(lambda e, sem=sem, val=val: e.wait_ge(sem, val))
        self.ninstr += 1

    def _deps(self, eng, reads, writes):
        deps = []
        for t in reads:
            if t.last_w is not None:
                deps.append(t.last_w)
            if t.excl:
                deps.extend(r for r in t.readers if r[0] != eng)
        for t in writes:
            if t.last_w is not None:
                deps.append(t.last_w)
            deps.extend(t.readers)
        for key, val in deps:
            if key == eng and eng not in self.strict:
                continue
            if eng == 'gpsimd' and key == 'tensor' and self.relay is not None:
                if self.waited[eng].get(key, 0) >= val:
                    continue
                self.waited[eng][key] = val
                self._wait('vector', key, val)
                self.cnt['vector'] += 1
                v2 = self.cnt['vector']
                rl = self.relay
                sem = self.sems['vector']
                self.prog['vector'].append(lambda e, rl=rl, sem=sem: e.memset(rl, 0.0).then_inc(sem, 1))
                self.ninstr += 1
                self._wait(eng, 'vector', v2)
                continue
            self._wait(eng, key, val)

    def op(self, eng, fn, reads=(), writes=()):
        self._deps(eng, reads, writes)
        self.cnt[eng] += 1
        val = self.cnt[eng]
        sem = self.sems[eng]
        self.prog[eng].append(lambda e, fn=fn, sem=sem: fn(e).then_inc(sem, 1))
        self.ninstr += 1
        for t in writes:
            t.last_w = (eng, val)
            t.readers = []
        for t in reads:
            if not any(t is w for w in writes):
                t.readers.append((eng, val))
                if len(t.readers) > 24:
                    self._compress(t)

    @staticmethod
    def _compress(t):
        m = {}
        for k, v in t.readers:
            if m.get(k, 0) < v:
                m[k] = v
        t.readers = list(m.items())

    def dma(self, eng, out, in_, reads=(), writes=(), **kw):
        i = self.dma_rr
        self.dma_rr = (self.dma_rr + 1) % self.n_dma_sems
        key = ('d', i)
        self._wait(eng, key, self.dma_cnt[i])
        self._deps(eng, reads, writes)
        self.dma_cnt[i] += 16
        val = self.dma_cnt[i]
        sem = self.sems[key]
        self.prog[eng].append(
            lambda e, out=out, in_=in_, sem=sem, kw=kw: e.dma_start(out=out, in_=in_, **kw).then_inc(sem, 16))
        self.ninstr += 1
        for t in writes:
            t.last_w = (key, val)
            t.readers = []
        for t in reads:
            t.readers.append((key, val))
            if len(t.readers) > 24:
                self._compress(t)

    def finish(self, eng='sync'):
        for q in ('scalar', 'gpsimd'):
            for i in range(self.n_dma_sems):
                self._wait(q, ('d', i), self.dma_cnt[i])
        for i in range(self.n_dma_sems):
            self._wait(eng, ('d', i), self.dma_cnt[i])
        for e in ENGS:
            if e != eng:
                self._wait(eng, e, self.cnt[e])

    def emit(self, block):
        for e in ENGS:
            lst = self.prog[e]

            def body(engine, lst=lst):
                for th in lst:
                    th(engine)
            getattr(block, e)(body)


class Arena:
    def __init__(self, tensor, nelem):
        self.t = tensor
        self.n = nelem
        self.off = 0
        self.tiles = []
        self.pending = []

    def reset(self):
        m = {}
        for k, v in self.pending:
            if m.get(k, 0) < v:
                m[k] = v
        for t in self.tiles:
            acc = list(t.readers)
            if t.last_w is not None:
                acc.append(t.last_w)
            for k, v in acc:
                if m.get(k, 0) < v:
                    m[k] = v
        self.pending = list(m.items())
        self.tiles = []
        self.off = 0

    def mark(self):
        return (self.off, len(self.tiles))

    def release(self, mk):
        off, nt = mk
        m = {}
        for k, v in self.pending:
            if m.get(k, 0) < v:
                m[k] = v
        for t in self.tiles[nt:]:
            acc = list(t.readers)
            if t.last_w is not None:
                acc.append(t.last_w)
            for k, v in acc:
                if m.get(k, 0) < v:
                    m[k] = v
        self.pending = list(m.items())
        self.tiles = self.tiles[:nt]
        self.off = off

    def alloc(self, shape, dtype=F32, name=''):
        n = 1
        for s in shape[1:]:
            n *= s
        n32 = n if dtype == F32 else (n + 1) // 2
        n32 = (n32 + 7) // 8 * 8
        assert self.off + n32 <= self.n, "arena overflow %s %d+%d>%d" % (name, self.off, n32, self.n)
        v = self.t[:, self.off:self.off + n32]
        self.off += n32
        if dtype != F32:
            v = v.bitcast(dtype)
        v = v[0:shape[0], 0:n]
        if len(shape) == 3:
            v = v.rearrange("p (a b) -> p a b", b=shape[2])
        elif len(shape) == 4:
            v = v.rearrange("p (a b c) -> p a b c", b=shape[2], c=shape[3])
        t = T(name)
        t.readers = list(self.pending)
        self.tiles.append(t)
        return v, t


def make_consts(TP):
    c = {}
    c['ident'] = np.eye(128, dtype=np.float32)
    p = np.arange(128)
    hh = p // 64
    ss = p % 64
    bd = (hh[:, None] == hh[None, :]).astype(np.float32)
    c['bones'] = bd.copy()
    mS = bd * (ss[:, None] < ss[None, :])
    mST = bd * (ss[:, None] > ss[None, :])
    c['maskS'] = np.tile(mS, (1, 4)).astype(np.float32)
    c['maskST'] = np.tile(mST, (1, 4)).astype(np.float32)
    mI = (ss[:, None] <= np.arange(64)[None, :]).astype(np.float32)
    c['maskI'] = np.tile(mI, (1, 8)).astype(np.float32)
    rm = np.ones((128, TP), np.float32)
    rm[:, ::64] = 0.0
    c['resetm'] = rm
    sel = np.zeros((16, 16, 128), np.float32)
    for n in range(16):
        sel[n, n, :] = 1.0
    c['sel'] = sel.reshape(16, 16 * 128)
    c['i2'] = np.tile(np.eye(64, dtype=np.float32), (2, 1))
    return c


def build(T_OWN, TP, dbg=False):
    NPH = T_OWN // TP
    NPASS = 2 * NPH
    W = HALO + TP + NS
    NCH = TP // 64
    OWN = slice(HALO, HALO + TP)
    SMP = slice(HALO + TP, HALO + TP + NS)
    OS = slice(HALO, HALO + TP + NS)
    NOS = TP + NS
    nc = bass.Bass("TRN2", target_bir_lowering=False)

    def din(name, shape):
        return nc.dram_tensor(name, list(shape), F32, kind="ExternalInput").ap()

    def dout(name, shape):
        return nc.dram_tensor(name, list(shape), F32, kind="ExternalOutput").ap()

    xin = din("xin", [HALO + 2 * T_OWN, D])
    xs = din("xs", [NS, D])
    pos = din("pos", [1, T_OWN])
    memx = din("memx", [NMEM, D])
    ck = din("ck", [NS, NMEM, MW])
    cv = din("cv", [NS, NMEM, MW])
    spool = din("spool", [NS, 15, PW])
    sshift = din("sshift", [NS, SHW])
    srwkv = din("srwkv", [NS, 32, 64, 64])
    w_in = din("w_in", [D, INC])
    b_gate = din("b_gate", [3 * D])
    pool_w = din("pool_w", [4, 512, 512])
    pool_scale = din("pool_scale", [PW])
    mu = din("rwkv_mu", [SHW])
    w0 = din("rwkv_w0", [RW])
    w2 = din("rwkv_w2", [LORA, RW])
    a0 = din("rwkv_a0", [RW])
    a2 = din("rwkv_a2", [LORA, RW])
    k_k = din("rwkv_k_k", [RW])
    k_a = din("rwkv_k_a", [RW])
    r_k = din("rwkv_r_k", [RW])
    gln_w = din("rwkv_ln_w", [RW])
    gln_b = din("rwkv_ln_b", [RW])
    w_mem_kv = din("w_mem_kv", [D, 2 * MW])
    wb_pool = din("w_branch_pool", [PW, D])
    wb_rwkv = din("w_branch_rwkv", [RW, D])
    wb_mem = din("w_branch_mem", [MW, D])
    w_out = din("w_out", [D, D])
    ln_g = din("ln_g", [1, D])
    ln_b = din("ln_b", [1, D])
    cshapes = {k: v.shape for k, v in make_consts(TP).items()}
    cin = {k: din("c_" + k, cshapes[k]) for k in cshapes}

    y_own = dout("y_own", [T_OWN, D])
    y_s = dout("y_s", [NS, D])
    o_mk = dout("o_mk", [NMEM, MW])
    o_mv = dout("o_mv", [NMEM, MW])
    o_tailu = dout("o_tailu", [32, PW])
    o_tailsh = dout("o_tailsh", [32, SHW])
    o_pools = dout("o_pools", [NS, 14, PW])
    o_rwkvp = dout("o_rwkvp", [32, 64, 64])
    o_rwkvs = dout("o_rwkvs", [NS, 32, 64, 64])
    KDBG = bool(os.environ.get('KDBG'))
    dbg = dout("dbg", [128, 76 * 32]) if KDBG else None

    st = contextlib.ExitStack()
    with st:
        S = Sched(nc, strict=tuple(x for x in os.environ.get('KSTRICT', 'vector,scalar,gpsimd').split(',') if x))
        S.alloc_sems(st)

        def sb(name, shape, dt=F32):
            return st.enter_context(nc.sbuf_tensor(name, list(shape), dt))

        xT = sb("xT", [128, 32, W], BF16); t_xT = T('xT')
        hT = sb("hT", [128, 32, W], BF16); t_hT = [T('hT%d' % i) for i in range(32)]
        bo = sb("bo", [128, 16, W], BF16); t_bo = [T('bo%d' % i) for i in range(16)]
        NWB = int(os.environ.get('KNWB', '4'))
        wch = [sb("wch%d" % i, [128, 32 * 128], BF16) for i in range(NWB)]
        t_wch = [T('wch%d' % i) for i in range(NWB)]
        KT = sb("KT", [128, 12, NMEM], BF16); t_KT = T('KT')
        Vb = sb("Vb", [128, 2, MW], BF16); t_Vb = T('Vb')
        ident = sb("ident", [128, 128]); t_c = T('consts')
        identb = sb("identb", [128, 128], BF16)
        bones = sb("bones", [128, 128])
        bones64 = sb("bones64", [128, 128])
        onesb = sb("onesb", [128, 128], BF16)
        maskS = sb("maskS", [128, 512])
        maskST = sb("maskST", [128, 512])
        maskI = sb("maskI", [128, 512])
        resetm = sb("resetm", [128, TP])
        sel = sb("sel", [16, 16 * 128])
        i2 = sb("i2", [128, 64])
        prm = {}
        relay_t = sb("relay", [128, 8])
        S.relay = None
        H32 = sb("H32", [128, 16, 64]); t_H = [T('H%d' % i) for i in range(16)]
        Hbf = sb("Hbf", [128, 16, 64], BF16)
        Hbd = sb("Hbd", [128, 16, 128], BF16)
        pwb = sb("pwb", [128, 4 * 512], BF16); t_pwb = T('pwb')
        w2b = sb("w2b", [LORA, RW], BF16)
        a2b = sb("a2b", [LORA, RW], BF16)
        ARN = 17408
        arena_t = sb("arena", [128, ARN])
        AR = Arena(arena_t, ARN)
        pst = st.enter_context(nc.psum_tensor("ps", [128, 8 * 512], F32))
        ps = [pst[:, i * 512:(i + 1) * 512] for i in range(8)]
        psb = [pst[:, i * 512:(i + 1) * 512].bitcast(BF16) for i in range(8)]
        t_ps = [T('ps%d' % i, excl=True) for i in range(8)]
        psrr = [0]
        block = st.enter_context(nc.Block())

        def bank():
            i = psrr[0]
            psrr[0] = (i + 1) % 8
            return i

        dq = [0]

        def hwq():
            dq[0] ^= 1
            if os.environ.get('KQ'):
                return 'sync'
            return 'sync' if dq[0] else 'scalar'

        for nm, tl in [('ident', ident), ('bones', bones), ('maskS', maskS), ('maskST', maskST),
                       ('maskI', maskI), ('resetm', resetm), ('sel', sel), ('i2', i2)]:
            S.dma('sync', tl[:], cin[nm][:, :], writes=[t_c])
        KB = int(os.environ.get('KB', '99'))
        if KB >= 2:
          S.op('vector', lambda e: e.tensor_copy(out=identb[:], in_=ident[:]), reads=[t_c], writes=[t_c])
        if KB >= 2:
          S.op('vector', lambda e: e.tensor_scalar(out=bones64[:], in0=bones[:], scalar1=1.0 / 64, scalar2=None,
                                                 op0=ALU.mult), reads=[t_c], writes=[t_c])
        if KB >= 2:
          S.op('vector', lambda e: e.memset(onesb[:], 1.0), writes=[t_c])
        if KB >= 3:
          S.dma('gpsimd', w2b[:], w2[:, :], writes=[t_c])
          S.dma('gpsimd', a2b[:], a2[:, :], writes=[t_c])

        def ptile(name, src, n, ntile, kp=128):
            if KB < (5 if kp != 128 else 4):
                prm[name] = None
                return
            tl = sb("p_" + name, [kp, ntile])
            stg_ = sb("ps_" + name, [ntile, kp])
            t_stg_ = T()
            S.dma('sync', stg_[:], src.rearrange("(c p) -> c p", p=kp), writes=[t_stg_])
            bi = bank()
            S.op('tensor', lambda e, bi=bi, stg_=stg_: e.transpose(ps[bi][0:kp, 0:ntile], stg_[:, :], ident[0:ntile, 0:ntile]),
                 reads=[t_stg_, t_c], writes=[t_ps[bi]])
            S.op('vector', lambda e, bi=bi, tl=tl: e.tensor_copy(out=tl[:, :], in_=ps[bi][0:kp, 0:ntile]),
                 reads=[t_ps[bi]], writes=[t_c])
            prm[name] = tl

        ptile('mu_r', mu[0:RW], RW, 16)
        ptile('mu_k', mu[RW:2 * RW], RW, 16)
        ptile('mu_v', mu[2 * RW:3 * RW], RW, 16)
        for nm, src in [('w0', w0), ('a0', a0), ('k_k', k_k), ('k_a', k_a), ('r_k', r_k), ('gln_w', gln_w),
                        ('gln_b', gln_b), ('pscale', pool_scale)]:
            ptile(nm, src, RW, 16)
        ptile('b_gate', b_gate, 3 * D, 96)
        ptile('mul', mu[3 * RW:3 * RW + 2 * LORA], 2 * LORA, 2, kp=LORA)
        mul = prm['mul']
        omka = sb("p_omka", [128, 16])
        if KB >= 6:
          S.op('vector', lambda e: e.tensor_scalar(out=omka[:], in0=prm['k_a'][:], scalar1=-1.0, scalar2=1.0,
                                                 op0=ALU.mult, op1=ALU.add), reads=[t_c], writes=[t_c])
        if KB >= 7:
          S.op('vector', lambda e: e.memset(H32[:], 0.0), writes=t_H)
          S.op('vector', lambda e: e.memset(Hbf[:], 0.0), writes=t_H)
          S.op('vector', lambda e: e.memset(Hbd[:], 0.0), writes=t_H)

        wrr = [0]

        def wload(src_ap, K, ncols):
            i = wrr[0]
            wrr[0] = (i + 1) % NWB
            if K % 128 == 0:
                KC = K // 128
                v = wch[i][:, 0:KC * ncols].rearrange("p (a b) -> p a b", b=ncols)
                S.dma('gpsimd', v, src_ap.rearrange("(c p) n -> p c n", p=128), writes=[t_wch[i]])
                return v, t_wch[i], 128, KC
            raise AssertionError

        def proj(wv, wt, KC, c0, c1, rhs3, rhs_ts, cols, bi=None):
            if bi is None:
                bi = bank()
            n = cols.stop - cols.start
            for kc in range(KC):
                S.op('tensor', lambda e, kc=kc: e.matmul(ps[bi][0:c1 - c0, 0:n], wv[:, kc, c0:c1], rhs3[:, kc, cols],
                                                        start=(kc == 0), stop=(kc == KC - 1)),
                     reads=[wt] + list(rhs_ts), writes=[t_ps[bi]])
            return bi

        def inproj(col0, ncols=128, cols=slice(0, W)):
            wv, wt, kp, KC = wload(w_in[:, col0:col0 + ncols], D, ncols)
            return wv, wt, KC

        def load_xT(row0, nrows, col0, src):
            r = 0
            while r < nrows:
                nr = min(128, nrows - r)
                for hlf in range(2):
                    if AR.off + 2048 > AR.n:
                        AR.reset()
                    xr, t_xr = AR.alloc([128, 2048], F32, 'xrow')
                    S.dma(hwq(), xr[0:nr, :], src[row0 + r:row0 + r + nr, hlf * 2048:(hlf + 1) * 2048], writes=[t_xr])
                    for g4 in range(4):
                        bi = bank()
                        for j in range(4):
                            cc = g4 * 4 + j
                            S.op('tensor', lambda e, cc=cc, j=j, bi=bi, nr=nr, xr=xr: e.transpose(
                                ps[bi][:, j * 128:j * 128 + nr], xr[0:nr, cc * 128:(cc + 1) * 128], ident[0:nr, 0:nr]),
                                reads=[t_xr, t_c], writes=[t_ps[bi]])
                        kc0 = hlf * 16 + g4 * 4
                        S.op('vector' if g4 % 2 == 0 else 'scalar',
                             (lambda e, bi=bi, kc0=kc0, nr=nr, c=col0 + r: e.tensor_copy(
                                 out=xT[:, kc0:kc0 + 4, c:c + nr],
                                 in_=ps[bi][:, :].rearrange("p (a b) -> p a b", b=128)[:, :, 0:nr]))
                             if g4 % 2 == 0 else
                             (lambda e, bi=bi, kc0=kc0, nr=nr, c=col0 + r: e.activation(
                                 out=xT[:, kc0:kc0 + 4, c:c + nr],
                                 in_=ps[bi][:, :].rearrange("p (a b) -> p a b", b=128)[:, :, 0:nr], func=AF.Copy)),
                             reads=[t_ps[bi]], writes=[t_xT])
                r += nr

        AR.reset()
        memT = hT[:, :, :].rearrange("p a b -> p (a b)")[:, 0:32 * NMEM].rearrange("p (a b) -> p a b", b=NMEM)

        def load_memT():
            for r in range(2):
                for hlf in range(2):
                    xr, t_xr = AR.alloc([128, 2048], F32, 'mrow')
                    S.dma(hwq(), xr[:, :], memx[r * 128:(r + 1) * 128, hlf * 2048:(hlf + 1) * 2048], writes=[t_xr])
                    for g4 in range(4):
                        bi = bank()
                        for j in range(4):
                            cc = g4 * 4 + j
                            S.op('tensor', lambda e, cc=cc, j=j, bi=bi, xr=xr: e.transpose(
                                ps[bi][:, j * 128:(j + 1) * 128], xr[:, cc * 128:(cc + 1) * 128], ident[:, :]),
                                reads=[t_xr, t_c], writes=[t_ps[bi]])
                        kc0 = hlf * 16 + g4 * 4
                        S.op('vector', lambda e, bi=bi, kc0=kc0, r=r: e.tensor_copy(
                            out=memT[:, kc0:kc0 + 4, r * 128:(r + 1) * 128],
                            in_=ps[bi][:, :].rearrange("p (a b) -> p a b", b=128)), reads=[t_ps[bi]], writes=t_hT)
        KSTOP0 = int(os.environ.get('KSTOP', '99'))
        if KSTOP0 > 0:
            load_memT()
        for c in range(int(os.environ.get('KC', '24')) if KSTOP0 > 0 else 0):
            wv, wt, kp, KC = wload(w_mem_kv[:, c * 128:(c + 1) * 128], D, 128)
            bi = proj(wv, wt, KC, 0, 128, memT, t_hT, slice(0, NMEM))
            kv32, t_kv32 = AR.alloc([128, NMEM], F32, 'kv32')
            S.op('vector', lambda e, bi=bi, kv32=kv32: e.tensor_copy(out=kv32[:, :], in_=ps[bi][:, 0:NMEM]),
                 reads=[t_ps[bi]], writes=[t_kv32])
            if c < 12:
                S.op('scalar', lambda e, bi=bi, c=c: e.activation(out=KT[:, c, :], in_=ps[bi][:, 0:NMEM], func=AF.Copy),
                     reads=[t_ps[bi]], writes=[t_KT])
            b2 = bank()
            for mt in range(2):
                S.op('tensor', lambda e, mt=mt, b2=b2, kv32=kv32: e.transpose(
                    ps[b2][:, mt * 128:(mt + 1) * 128], kv32[:, mt * 128:(mt + 1) * 128], ident[:, :]),
                    reads=[t_kv32, t_c], writes=[t_ps[b2]])
            kvt, t_kvt = AR.alloc([128, 2, 128], F32, 'kvt')
            S.op('vector', lambda e, b2=b2, kvt=kvt: e.tensor_copy(
                out=kvt[:, :, :], in_=ps[b2][:, 0:256].rearrange("p (a b) -> p a b", b=128)),
                reads=[t_ps[b2]], writes=[t_kvt])
            if c >= 12:
                S.op('scalar', lambda e, b2=b2, c=c: e.activation(
                    out=Vb[:, :, (c - 12) * 128:(c - 11) * 128],
                    in_=ps[b2][:, 0:256].rearrange("p (a b) -> p a b", b=128), func=AF.Copy),
                    reads=[t_ps[b2]], writes=[t_Vb])
            dst = o_mk if c < 12 else o_mv
            cc = c % 12
            if not os.environ.get('KO'):
                S.dma(hwq(), dst[:, cc * 128:(cc + 1) * 128].rearrange("(a p) n -> p a n", p=128), kvt[:, :, :],
                      reads=[t_kvt], writes=[T()])
            if c % 6 == 5:
                AR.reset()
        AR.reset()

        t_vscr = [T('v%d' % i) for i in range(NPH * (TP // 128) + 1)]

        def run_pass(p):
            main = p >= NPH
            last = p == NPASS - 1
            cur_last[0] = last
            mp = p - NPH
            AR.reset()
            load_xT(p * TP, HALO + TP, 0, xin)
            load_xT(0, NS, HALO + TP, xs)
            AR.reset()
            chk(2)
            rwkv_stage(p, main, last)
            chk(3)
            if main:
                chk(4)
                branch_proj(wb_rwkv, RW, 1, first=True)
                chk(5)
                pool_stage(p, last)
                branch_proj(wb_pool, PW, 0, first=False)
                chk(6)
                mem_stage(p, last)
                branch_proj(wb_mem, MW, 2, first=False)
                chk(7)
                final_stage(p, last)
                chk(9)

        cur_last = [False]

        def dump(src3, ntile, tls, off):
            if not (KDBG and cur_last[0]):
                return
            stg, t_stg = AR.alloc([128, ntile, 32], F32, 'dbgstg')
            S.op('vector', lambda e: e.tensor_copy(out=stg[:, :, :], in_=src3[:, 0:ntile, HALO + TP - 16:HALO + TP + 16]),
                 reads=tls, writes=[t_stg])
            S.dma('sync', dbg[:, off * 32:(off + ntile) * 32], stg[:, :, :].rearrange("p a b -> p (a b)"), reads=[t_stg],
                  writes=[T()])

        def branch_proj(wb, Kb, gi, first):
            AR.reset()
            KCb = Kb // 128
            dump(bo, KCb, t_bo[0:KCb], {1: 0, 0: 16, 2: 32}[gi])
            for c in range(32):
                wv, wt, kp, KC = wload(wb[:, c * 128:(c + 1) * 128], Kb, 128)
                bp = proj(wv, wt, KC, 0, 128, bo, t_bo[0:KCb], slice(0, W))
                gv, gt, gKC = inproj(C_G + gi * D + c * 128)
                bg = proj(gv, gt, gKC, 0, 128, xT, [t_xT], slice(0, W))
                g, t_g = AR.alloc([128, W], F32, 'g')
                S.op('scalar', lambda e, bg=bg, g=g, gi=gi, c=c: e.activation(
                    out=g[:, :], in_=ps[bg][:, 0:W], func=AF.Sigmoid,
                    bias=prm['b_gate'][:, gi * 32 + c:gi * 32 + c + 1]), reads=[t_ps[bg], t_c], writes=[t_g])
                if first:
                    S.op('vector', lambda e, bp=bp, g=g, c=c: e.tensor_tensor(
                        out=hT[:, c, :], in0=ps[bp][:, 0:W], in1=g[:, :], op=ALU.mult),
                        reads=[t_ps[bp], t_g], writes=[t_hT[c]])
                else:
                    tmp, t_tmp = AR.alloc([128, W], F32, 'gtmp')
                    S.op('vector', lambda e, bp=bp, g=g, tmp=tmp: e.tensor_tensor(
                        out=tmp[:, :], in0=ps[bp][:, 0:W], in1=g[:, :], op=ALU.mult),
                        reads=[t_ps[bp], t_g], writes=[t_tmp])
                    S.op('vector', lambda e, tmp=tmp, c=c: e.tensor_tensor(
                        out=hT[:, c, :], in0=hT[:, c, :], in1=tmp[:, :], op=ALU.add),
                        reads=[t_tmp, t_hT[c]], writes=[t_hT[c]])
                if c % 4 == 3:
                    AR.reset()

        def tail_out(src32, t_src, dst, col0):
            bi = bank()
            S.op('tensor', lambda e, bi=bi: e.transpose(ps[bi][0:32, 0:128], src32[:, HALO + TP - 16:HALO + TP + 16],
                                                        ident[:, :]), reads=[t_src, t_c], writes=[t_ps[bi]])
            tl, t_tl = AR.alloc([32, 128], F32, 'tail')
            S.op('scalar', lambda e, bi=bi, tl=tl: e.activation(out=tl[:, :], in_=ps[bi][0:32, 0:128], func=AF.Copy),
                 reads=[t_ps[bi]], writes=[t_tl])
            S.dma(hwq(), dst[:, col0:col0 + 128], tl[:, :], reads=[t_tl], writes=[T()])

        def rwkv_stage(p, main, last):
            AR.reset()
            lmix = []
            shl = None
            for li in range(2):
                wv, wt, KC = inproj(C_WD + li * LORA, LORA)
                bi = proj(wv, wt, KC, 0, LORA, xT, [t_xT], slice(0, W))
                raw, t_raw = AR.alloc([LORA, W], F32, 'lraw')
                S.op('vector', lambda e, bi=bi, raw=raw: e.tensor_copy(out=raw[:, :], in_=ps[bi][0:LORA, 0:W]),
                     reads=[t_ps[bi]], writes=[t_raw])
                if last:
                    b2 = bank()
                    S.op('tensor', lambda e, b2=b2, raw=raw: e.transpose(
                        ps[b2][0:32, 0:LORA], raw[:, HALO + TP - 16:HALO + TP + 16], ident[0:LORA, 0:LORA]),
                        reads=[t_raw, t_c], writes=[t_ps[b2]])
                    tl, t_tl = AR.alloc([32, LORA], F32, 'tailL')
                    S.op('scalar', lambda e, b2=b2, tl=tl: e.activation(out=tl[:, :], in_=ps[b2][0:32, 0:LORA], func=AF.Copy),
                         reads=[t_ps[b2]], writes=[t_tl])
                    S.dma(hwq(), o_tailsh[:, 3 * RW + li * LORA:3 * RW + (li + 1) * LORA], tl[:, :], reads=[t_tl], writes=[T()])
                mix, t_mix = AR.alloc([LORA, NOS], F32, 'lmix')
                d, t_d = AR.alloc([LORA, NOS], F32, 'ld')
                S.op('vector', lambda e, raw=raw, d=d: e.tensor_tensor(
                    out=d[:, 0:TP], in0=raw[:, HALO - 1:HALO + TP - 1], in1=raw[:, OWN], op=ALU.subtract),
                    reads=[t_raw], writes=[t_d])
                S.op('vector', lambda e, raw=raw, d=d, mix=mix, li=li: e.scalar_tensor_tensor(
                    out=mix[:, 0:TP], in0=d[:, 0:TP], scalar=mul[:, li:li + 1], in1=raw[:, OWN], op0=ALU.mult, op1=ALU.add),
                    reads=[t_raw, t_d, t_c], writes=[t_mix])
                if last:
                    if shl is None:
                        shl, t_shl = load_shiftT(3 * RW, 2 * LORA, LORA)
                    S.op('vector', lambda e, raw=raw, d=d, li=li, shl=shl: e.tensor_tensor(
                        out=d[:, TP:NOS], in0=shl[li][0:LORA, :], in1=raw[:, SMP], op=ALU.subtract),
                        reads=[t_raw, t_shl], writes=[t_d])
                    S.op('vector', lambda e, raw=raw, d=d, mix=mix, li=li: e.scalar_tensor_tensor(
                        out=mix[:, TP:NOS], in0=d[:, TP:NOS], scalar=mul[:, li:li + 1], in1=raw[:, SMP], op0=ALU.mult,
                        op1=ALU.add), reads=[t_raw, t_d, t_c], writes=[t_mix])
                lb, t_lb = AR.alloc([LORA, NOS], BF16, 'lbf')
                ncol = NOS if last else TP
                S.op('scalar', lambda e, mix=mix, lb=lb, li=li, ncol=ncol: e.activation(
                    out=lb[:, 0:ncol], in_=mix[:, 0:ncol], func=AF.Tanh if li == 0 else AF.Copy),
                    reads=[t_mix], writes=[t_lb])
                lmix.append((lb, t_lb))
            arena_base = AR.off
            base_tiles = list(AR.tiles)
            for hp in range(16):
                keep = AR.tiles[:len(base_tiles)]
                AR.reset()
                AR.off = arena_base
                AR.tiles = keep
                rwkv_pair(p, main, last, hp, lmix)
            AR.reset()

        def load_shiftT(c0, ncol, chunk):
            tok, t_tok = AR.alloc([NS, ncol], F32, 'shtok')
            S.dma(hwq(), tok[:, :], sshift[:, c0:c0 + ncol], writes=[t_tok])
            outs = []
            res, t_res = AR.alloc([128, (ncol // chunk) * NS], F32, 'shT')
            for i in range(ncol // chunk):
                bi = bank()
                S.op('tensor', lambda e, bi=bi, i=i: e.transpose(ps[bi][0:chunk, 0:NS], tok[:, i * chunk:(i + 1) * chunk],
                                                                ident[0:NS, 0:NS]), reads=[t_tok, t_c], writes=[t_ps[bi]])
                S.op('vector', lambda e, bi=bi, i=i: e.tensor_copy(out=res[0:chunk, i * NS:(i + 1) * NS],
                                                                   in_=ps[bi][0:chunk, 0:NS]),
                     reads=[t_ps[bi]], writes=[t_res])
                outs.append(res[:, i * NS:(i + 1) * NS])
            return outs, t_res

        def rwkv_pair(p, main, last, hp, lmix):
            ncol = NOS if last else TP
            CS = slice(0, ncol)
            pcol = lambda name: prm[name][:, hp:hp + 1]
            mixed = {}
            names = ['r', 'k', 'v'] if main else ['k', 'v']
            cbase = {'r': C_R, 'k': C_K, 'v': C_V}
            shs = None
            for nm in names:
                wv, wt, KC = inproj(cbase[nm] + hp * 128)
                bi = proj(wv, wt, KC, 0, 128, xT, [t_xT], slice(0, W))
                raw, t_raw = AR.alloc([128, W], F32, 'raw' + nm)
                S.op('scalar', lambda e, bi=bi, raw=raw: e.activation(out=raw[:, :], in_=ps[bi][:, 0:W], func=AF.Copy),
                     reads=[t_ps[bi]], writes=[t_raw])
                if last:
                    tail_out(raw, t_raw, o_tailsh, (cbase[nm] - C_R) + hp * 128)
                d, t_d = AR.alloc([128, NOS], F32, 'd' + nm)
                mx, t_mx = AR.alloc([128, NOS], F32, 'm' + nm)
                S.op('vector', lambda e, raw=raw, d=d: e.tensor_tensor(
                    out=d[:, 0:TP], in0=raw[:, HALO - 1:HALO + TP - 1], in1=raw[:, OWN], op=ALU.subtract),
                    reads=[t_raw], writes=[t_d])
                S.op('vector', lambda e, raw=raw, d=d, mx=mx, nm=nm: e.scalar_tensor_tensor(
                    out=mx[:, 0:TP], in0=d[:, 0:TP], scalar=pcol('mu_' + nm), in1=raw[:, OWN], op0=ALU.mult, op1=ALU.add),
                    reads=[t_raw, t_d, t_c], writes=[t_mx])
                if last:
                    sh1, t_sh1 = load_shiftT((cbase[nm] - C_R) + hp * 128, 128, 128)
                    S.op('vector', lambda e, raw=raw, d=d, sh1=sh1: e.tensor_tensor(
                        out=d[:, TP:NOS], in0=sh1[0][:, :], in1=raw[:, SMP], op=ALU.subtract),
                        reads=[t_raw, t_sh1], writes=[t_d])
                    S.op('vector', lambda e, raw=raw, d=d, mx=mx, nm=nm: e.scalar_tensor_tensor(
                        out=mx[:, TP:NOS], in0=d[:, TP:NOS], scalar=pcol('mu_' + nm), in1=raw[:, SMP], op0=ALU.mult,
                        op1=ALU.add), reads=[t_raw, t_d, t_c], writes=[t_mx])
                mixed[nm] = (mx, t_mx)
            k, t_k = mixed['k']
            v, t_v = mixed['v']
            lw, t_lw = AR.alloc([128, NOS], F32, 'lw')
            av, t_av = AR.alloc([128, NOS], F32, 'a')
            for li, (wl, dst, t_dst, bname) in enumerate([(w2b, lw, t_lw, 'w0'), (a2b, av, t_av, 'a0')]):
                bi = bank()
                lb, t_lb = lmix[li]
                S.op('tensor', lambda e, bi=bi, wl=wl, lb=lb: e.matmul(ps[bi][:, 0:ncol], wl[:, hp * 128:(hp + 1) * 128],
                                                                       lb[:, 0:ncol], start=True, stop=True),
                     reads=[t_c, t_lb], writes=[t_ps[bi]])
                S.op('scalar', lambda e, bi=bi, dst=dst, bname=bname: e.activation(
                    out=dst[:, CS], in_=ps[bi][:, 0:ncol], func=AF.Sigmoid, bias=pcol(bname)),
                    reads=[t_ps[bi], t_c], writes=[t_dst])
            S.op('vector', lambda e: e.tensor_scalar(out=lw[:, CS], in0=lw[:, CS], scalar1=-0.6065306597126334,
                                                     scalar2=None, op0=ALU.mult), reads=[t_lw], writes=[t_lw])
            kk, t_kk = AR.alloc([128, NOS], F32, 'kk')
            sq, t_sq = AR.alloc([128, NOS], F32, 'sq')
            S.op('vector', lambda e: e.tensor_scalar(out=kk[:, CS], in0=k[:, CS], scalar1=pcol('k_k'), scalar2=None,
                                                     op0=ALU.mult), reads=[t_k, t_c], writes=[t_kk])
            S.op('vector', lambda e: e.tensor_tensor(out=sq[:, CS], in0=kk[:, CS], in1=kk[:, CS], op=ALU.mult),
                 reads=[t_kk], writes=[t_sq])
            bi = bank()
            S.op('tensor', lambda e, bi=bi: e.matmul(ps[bi][:, 0:ncol], bones[:, :], sq[:, CS], start=True, stop=True),
                 reads=[t_c, t_sq], writes=[t_ps[bi]])
            S.op('vector', lambda e, bi=bi: e.tensor_scalar(out=sq[:, CS], in0=ps[bi][:, 0:ncol], scalar1=1e-24,
                                                            scalar2=None, op0=ALU.max), reads=[t_ps[bi]], writes=[t_sq])
            S.op('scalar', lambda e: e.activation(out=sq[:, CS], in_=sq[:, CS], func=AF.Sqrt), reads=[t_sq], writes=[t_sq])
            S.op('vector', lambda e: e.reciprocal(out=sq[:, CS], in_=sq[:, CS]), reads=[t_sq], writes=[t_sq])
            S.op('vector', lambda e: e.tensor_tensor(out=kk[:, CS], in0=kk[:, CS], in1=sq[:, CS], op=ALU.mult),
                 reads=[t_kk, t_sq], writes=[t_kk])
            k2, t_k2 = AR.alloc([128, NOS], F32, 'k2')
            bb, t_bb = AR.alloc([128, NOS], F32, 'b')
            S.op('vector', lambda e: e.tensor_scalar(out=k2[:, CS], in0=av[:, CS], scalar1=pcol('k_a'),
                                                     scalar2=omka[:, hp:hp + 1], op0=ALU.mult, op1=ALU.add),
                 reads=[t_av, t_c], writes=[t_k2])
            S.op('vector', lambda e: e.tensor_tensor(out=k2[:, CS], in0=k2[:, CS], in1=k[:, CS], op=ALU.mult),
                 reads=[t_k2, t_k], writes=[t_k2])
            S.op('vector', lambda e: e.tensor_tensor(out=bb[:, CS], in0=kk[:, CS], in1=av[:, CS], op=ALU.mult),
                 reads=[t_kk, t_av], writes=[t_bb])
            r = t_r = None
            if main:
                r, t_r = mixed['r']
            y, t_y = AR.alloc([128, NOS], F32, 'y')
            mk_ = AR.mark()
            scan(p, main, hp, r, t_r, k2, t_k2, v, t_v, kk, t_kk, bb, t_bb, lw, t_lw, y, t_y)
            AR.release(mk_)
            if last:
                sample_step(hp, r, t_r, k2, t_k2, v, t_v, kk, t_kk, bb, t_bb, lw, t_lw, y, t_y)
                AR.release(mk_)
            if not main:
                return
            wv, wt, KC = inproj(C_ZR + hp * 128)
            bz = proj(wv, wt, KC, 0, 128, xT, [t_xT], slice(0, W))
            sz, t_sz = AR.alloc([128, NOS], F32, 'sz')
            S.op('scalar', lambda e, bz=bz: e.activation(out=sz[:, 0:NOS], in_=ps[bz][:, HALO:HALO + NOS], func=AF.Silu),
                 reads=[t_ps[bz]], writes=[t_sz])
            bon, t_bon = AR.alloc([128, NOS], F32, 'bon')
            S.op('vector', lambda e: e.scalar_tensor_tensor(out=bon[:, CS], in0=r[:, CS], scalar=pcol('r_k'),
                                                            in1=k2[:, CS], op0=ALU.mult, op1=ALU.mult),
                 reads=[t_r, t_k2, t_c], writes=[t_bon])
            bi = bank()
            S.op('tensor', lambda e, bi=bi: e.matmul(ps[bi][:, 0:ncol], bones[:, :], bon[:, CS], start=True, stop=True),
                 reads=[t_c, t_bon], writes=[t_ps[bi]])
            S.op('vector', lambda e, bi=bi: e.tensor_tensor(out=bon[:, CS], in0=ps[bi][:, 0:ncol], in1=v[:, CS],
                                                            op=ALU.mult), reads=[t_ps[bi], t_v], writes=[t_bon])
            bi = bank()
            S.op('tensor', lambda e, bi=bi: e.matmul(ps[bi][:, 0:ncol], bones64[:, :], y[:, CS], start=True, stop=True),
                 reads=[t_c, t_y], writes=[t_ps[bi]])
            S.op('vector', lambda e, bi=bi: e.tensor_tensor(out=y[:, CS], in0=y[:, CS], in1=ps[bi][:, 0:ncol],
                                                            op=ALU.subtract), reads=[t_ps[bi], t_y], writes=[t_y])
            S.op('vector', lambda e: e.tensor_tensor(out=sq[:, CS], in0=y[:, CS], in1=y[:, CS], op=ALU.mult),
                 reads=[t_y], writes=[t_sq])
            bi = bank()
            S.op('tensor', lambda e, bi=bi: e.matmul(ps[bi][:, 0:ncol], bones64[:, :], sq[:, CS], start=True, stop=True),
                 reads=[t_c, t_sq], writes=[t_ps[bi]])
            S.op('vector', lambda e, bi=bi: e.tensor_scalar(out=sq[:, CS], in0=ps[bi][:, 0:ncol], scalar1=GN_EPS,
                                                            scalar2=None, op0=ALU.add), reads=[t_ps[bi]], writes=[t_sq])
            S.op('scalar', lambda e: e.activation(out=sq[:, CS], in_=sq[:, CS], func=AF.Sqrt), reads=[t_sq], writes=[t_sq])
            S.op('vector', lambda e: e.reciprocal(out=sq[:, CS], in_=sq[:, CS]), reads=[t_sq], writes=[t_sq])
            S.op('vector', lambda e: e.tensor_tensor(out=y[:, CS], in0=y[:, CS], in1=sq[:, CS], op=ALU.mult),
                 reads=[t_y, t_sq], writes=[t_y])
            S.op('vector', lambda e: e.tensor_scalar(out=y[:, CS], in0=y[:, CS], scalar1=pcol('gln_w'),
                                                     scalar2=pcol('gln_b'), op0=ALU.mult, op1=ALU.add),
                 reads=[t_y, t_c], writes=[t_y])
            S.op('vector', lambda e: e.tensor_tensor(out=y[:, CS], in0=y[:, CS], in1=bon[:, CS], op=ALU.add),
                 reads=[t_y, t_bon], writes=[t_y])
            S.op('vector', lambda e: e.tensor_tensor(out=bo[:, hp, HALO:HALO + ncol], in0=y[:, CS], in1=sz[:, CS],
                                                     op=ALU.mult), reads=[t_y, t_sz], writes=[t_bo[hp]])

        def scan(p, main, hp, r, t_r, k2, t_k2, v, t_v, kk, t_kk, bb, t_bb, lw, t_lw, y, t_y):
            TS = slice(0, TP)
            cl, t_cl = AR.alloc([128, TP], F32, 'cl')
            e1, t_e1 = AR.alloc([128, TP], F32, 'e1')
            e2, t_e2 = AR.alloc([128, TP], F32, 'e2')
            e3, t_e3 = AR.alloc([128, TP], F32, 'e3')
            S.op('vector', lambda e: e.tensor_tensor_scan(out=cl[:, :], data0=resetm[:, :], data1=lw[:, TS], initial=0.0,
                                                          op0=ALU.mult, op1=ALU.add), reads=[t_c, t_lw], writes=[t_cl])
            S.op('scalar', lambda e: e.activation(out=e1[:, :], in_=cl[:, :], func=AF.Exp), reads=[t_cl], writes=[t_e1])
            S.op('scalar', lambda e: e.activation(out=e2[:, :], in_=cl[:, :], func=AF.Exp, scale=-1.0), reads=[t_cl],
                 writes=[t_e2])
            S.op('vector', lambda e: e.tensor_tensor(out=e3[:, :], in0=cl[:, :], in1=lw[:, TS], op=ALU.subtract),
                 reads=[t_cl, t_lw], writes=[t_e3])
            S.op('scalar', lambda e: e.activation(out=e3[:, :], in_=e3[:, :], func=AF.Exp), reads=[t_e3], writes=[t_e3])
            def bdtile(name):
                tl, tt = AR.alloc([128, NCH, 128], BF16, name)
                S.op('vector', lambda e, tl=tl: e.memset(tl[:, :, :], 0.0), writes=[tt])
                return tl, tt
            at_bd, t_at = bdtile('at_bd')
            bt_bd, t_bt = bdtile('bt_bd')
            kt_bd, t_kt = bdtile('kt_bd')
            v_bd, t_vbd = bdtile('v_bd')
            for h in range(2):
                PS_ = slice(h * 64, (h + 1) * 64)
                CS_ = slice(h * 64, (h + 1) * 64)
                def v3(x, PS_=PS_):
                    return x[PS_, 0:TP].rearrange("p (q t) -> p q t", t=64)
                S.op('vector', lambda e, PS_=PS_, CS_=CS_, v3=v3: e.scalar_tensor_tensor(
                    out=at_bd[PS_, :, CS_], in0=v3(kk), scalar=-1.0, in1=v3(e3), op0=ALU.mult, op1=ALU.mult),
                    reads=[t_kk, t_e3], writes=[t_at])
                S.op('vector', lambda e, PS_=PS_, CS_=CS_, v3=v3: e.tensor_tensor(
                    out=bt_bd[PS_, :, CS_], in0=v3(bb), in1=v3(e2), op=ALU.mult), reads=[t_bb, t_e2], writes=[t_bt])
                S.op('vector', lambda e, PS_=PS_, CS_=CS_, v3=v3: e.tensor_tensor(
                    out=kt_bd[PS_, :, CS_], in0=v3(k2), in1=v3(e2), op=ALU.mult), reads=[t_k2, t_e2], writes=[t_kt])
                S.op('scalar', lambda e, PS_=PS_, CS_=CS_, v3=v3: e.activation(
                    out=v_bd[PS_, :, CS_], in_=v3(v), func=AF.Copy), reads=[t_v], writes=[t_vbd])
            rt_c = t_rt = None
            if main:
                rt_c, t_rt = AR.alloc([128, NCH, 64], BF16, 'rt_c')
                S.op('vector', lambda e: e.tensor_tensor(out=rt_c[:, :, :].rearrange("p q t -> p (q t)"), in0=r[:, TS],
                                                         in1=e1[:, :], op=ALU.mult), reads=[t_r, t_e1], writes=[t_rt])

            def per_chunk_mm(width, mmfn, reads):
                per = 512 // width
                groups = []
                q = 0
                while q < NCH:
                    nq = min(per, NCH - q)
                    bi = bank()
                    for j in range(nq):
                        mmfn(q + j, ps[bi][:, j * width:(j + 1) * width], bi)
                    groups.append((bi, q, nq))
                    q += nq
                return groups

            def mm1(lhsT_fn, rhs_fn, reads):
                def f(q, out, bi):
                    S.op('tensor', lambda e, q=q, out=out: e.matmul(out, lhsT_fn(q), rhs_fn(q), start=True, stop=True),
                         reads=reads, writes=[t_ps[bi]])
                return f

            def evac(groups, width, fn, reads, writes, eng='vector'):
                for (bi, q0, nq) in groups:
                    src = ps[bi][:, 0:nq * width].rearrange("p (q w) -> p q w", w=width)
                    S.op(eng, lambda e, src=src, q0=q0, nq=nq: fn(e, src, q0, nq), reads=[t_ps[bi]] + reads, writes=writes)

            L = []
            Nm = []
            for i in range(2):
                a_, ta_ = AR.alloc([128, NCH, 128], BF16, 'L%d' % i)
                b_, tb_ = AR.alloc([128, NCH, 128], BF16, 'N%d' % i)
                L.append((a_, ta_))
                Nm.append((b_, tb_))
            AkT, t_AkT = AR.alloc([128, NCH, 128], BF16, 'AkT')
            mS3 = maskS[:, :].rearrange("p (q w) -> p q w", w=128)
            mST3 = maskST[:, :].rearrange("p (q w) -> p q w", w=128)
            mI3 = maskI[:, :].rearrange("p (q w) -> p q w", w=64)
            g = per_chunk_mm(128, mm1(lambda q: bt_bd[:, q, :], lambda q: at_bd[:, q, :], [t_bt, t_at]), None)
            evac(g, 128, lambda e, src, q0, nq: e.tensor_tensor(out=L[0][0][:, q0:q0 + nq, :], in0=src, in1=mS3[:, 0:nq, :],
                                                                op=ALU.mult), [t_c], [L[0][1]])
            g = per_chunk_mm(128, mm1(lambda q: at_bd[:, q, :], lambda q: bt_bd[:, q, :], [t_bt, t_at]), None)
            evac(g, 128, lambda e, src, q0, nq: e.tensor_tensor(out=Nm[0][0][:, q0:q0 + nq, :], in0=src, in1=mST3[:, 0:nq, :],
                                                                op=ALU.mult), [t_c], [Nm[0][1]])
            g = per_chunk_mm(128, mm1(lambda q: kt_bd[:, q, :], lambda q: at_bd[:, q, :], [t_kt, t_at]), None)
            evac(g, 128, lambda e, src, q0, nq: e.tensor_tensor(out=AkT[:, q0:q0 + nq, :], in0=src, in1=mS3[:, 0:nq, :],
                                                                op=ALU.mult), [t_c], [t_AkT])
            if main:
                ArbT, t_ArbT = AR.alloc([128, NCH, 64], BF16, 'ArbT')
                ArkT, t_ArkT = AR.alloc([128, NCH, 64], BF16, 'ArkT')
                g = per_chunk_mm(64, mm1(lambda q: bt_bd[:, q, :], lambda q: rt_c[:, q, :], [t_bt, t_rt]), None)
                evac(g, 64, lambda e, src, q0, nq: e.tensor_tensor(out=ArbT[:, q0:q0 + nq, :], in0=src, in1=mI3[:, 0:nq, :],
                                                                   op=ALU.mult), [t_c], [t_ArbT])
                g = per_chunk_mm(64, mm1(lambda q: kt_bd[:, q, :], lambda q: rt_c[:, q, :], [t_kt, t_rt]), None)
                evac(g, 64, lambda e, src, q0, nq: e.tensor_tensor(out=ArkT[:, q0:q0 + nq, :], in0=src, in1=mI3[:, 0:nq, :],
                                                                   op=ALU.mult), [t_c], [t_ArkT])
            def tr_groups(src_bd, t_src):
                groups = []
                q = 0
                while q < NCH:
                    nq = min(4, NCH - q)
                    bi = bank()
                    for j in range(nq):
                        S.op('tensor', lambda e, q=q, j=j, bi=bi: e.transpose(psb[bi][:, j * 128:(j + 1) * 128],
                                                                             src_bd[:, q + j, :], identb[:, :]),
                             reads=[t_src, t_c], writes=[t_ps[bi]])
                    groups.append((bi, q, nq))
                    q += nq
                return groups

            def evac_b(groups, fn, reads, writes, eng='vector'):
                for (bi, q0, nq) in groups:
                    src = psb[bi][:, 0:nq * 128].rearrange("p (q w) -> p q w", w=128)
                    S.op(eng, lambda e, src=src, q0=q0, nq=nq: fn(e, src, q0, nq), reads=[t_ps[bi]] + reads, writes=writes)

            X32, t_X32 = AR.alloc([128, NCH, 128], F32, 'X32')
            Xbf, t_Xbf = AR.alloc([128, NCH, 128], BF16, 'Xbf')
            btk, t_btk = AR.alloc([128, NCH, 128], BF16, 'btk')
            ktk, t_ktk = AR.alloc([128, NCH, 128], BF16, 'ktk')
            vtk_c, t_vtkc = AR.alloc([128, NCH, 64], BF16, 'vtk_c')
            g = tr_groups(at_bd, t_at)
            for h in range(2):
                PS_ = slice(h * 64, (h + 1) * 64)
                evac_b(g, lambda e, src, q0, nq, PS_=PS_: e.tensor_copy(out=X32[PS_, q0:q0 + nq, 0:64],
                                                                       in_=src[PS_, :, PS_]), [], [t_X32])
            g = tr_groups(bt_bd, t_bt)
            evac_b(g, lambda e, src, q0, nq: e.tensor_copy(out=btk[:, q0:q0 + nq, :], in_=src), [], [t_btk])
            g = tr_groups(kt_bd, t_kt)
            evac_b(g, lambda e, src, q0, nq: e.activation(out=ktk[:, q0:q0 + nq, :], in_=src, func=AF.Copy), [], [t_ktk],
                   eng='scalar')
            g = tr_groups(v_bd, t_vbd)
            vtk_bd = t_vtkbd = None
            if main:
                vtk_bd, t_vtkbd = AR.alloc([128, NCH, 128], BF16, 'vtk_bd')
                evac_b(g, lambda e, src, q0, nq: e.activation(out=vtk_bd[:, q0:q0 + nq, :], in_=src, func=AF.Copy), [],
                       [t_vtkbd], eng='scalar')
            for h in range(2):
                PS_ = slice(h * 64, (h + 1) * 64)
                evac_b(g, lambda e, src, q0, nq, PS_=PS_: e.tensor_copy(out=vtk_c[PS_, q0:q0 + nq, :],
                                                                       in_=src[PS_, :, PS_]), [], [t_vtkc])
            g = per_chunk_mm(64, mm1(lambda q: AkT[:, q, :], lambda q: vtk_c[:, q, :], [t_AkT, t_vtkc]), None)
            evac(g, 64, lambda e, src, q0, nq: e.tensor_copy(out=X32[:, q0:q0 + nq, 64:128], in_=src), [], [t_X32])
            S.op('scalar', lambda e: e.activation(out=Xbf[:, :, :], in_=X32[:, :, :], func=AF.Copy), reads=[t_X32],
                 writes=[t_Xbf])
            for lv in range(6):
                Lc, t_Lc = L[lv % 2]
                Nc, t_Nc = Nm[lv % 2]
                g = per_chunk_mm(128, mm1(lambda q, Lc=Lc: Lc[:, q, :], lambda q: Xbf[:, q, :], [t_Lc, t_Xbf]), None)
                evac(g, 128, lambda e, src, q0, nq: e.tensor_tensor(out=X32[:, q0:q0 + nq, :], in0=src,
                                                                    in1=X32[:, q0:q0 + nq, :], op=ALU.add), [t_X32], [t_X32])
                if lv < 5:
                    Ln, t_Ln = L[(lv + 1) % 2]
                    Nn, t_Nn = Nm[(lv + 1) % 2]
                    g1 = per_chunk_mm(128, mm1(lambda q, Nc=Nc: Nc[:, q, :], lambda q, Lc=Lc: Lc[:, q, :], [t_Lc, t_Nc]), None)
                    g2 = per_chunk_mm(128, mm1(lambda q, Lc=Lc: Lc[:, q, :], lambda q, Nc=Nc: Nc[:, q, :], [t_Lc, t_Nc]), None)
                    evac(g1, 128, lambda e, src, q0, nq, Ln=Ln: e.activation(out=Ln[:, q0:q0 + nq, :], in_=src, func=AF.Copy),
                         [], [t_Ln], eng='scalar')
                    evac(g2, 128, lambda e, src, q0, nq, Nn=Nn: e.tensor_copy(out=Nn[:, q0:q0 + nq, :], in_=src), [], [t_Nn],
                         eng='gpsimd' if False else 'vector')
                S.op('scalar', lambda e: e.activation(out=Xbf[:, :, :], in_=X32[:, :, :], func=AF.Copy), reads=[t_X32],
                     writes=[t_Xbf])
            Ah_bd, t_Ah = bdtile('Ah_bd')
            for h in range(2):
                PS_ = slice(h * 64, (h + 1) * 64)
                S.op('vector', lambda e, PS_=PS_: e.tensor_copy(out=Ah_bd[PS_, :, PS_], in_=Xbf[PS_, :, 0:64]),
                     reads=[t_Xbf], writes=[t_Ah])
            U0_bd = t_U0 = None
            if main:
                U0_bd, t_U0 = bdtile('U0_bd')
                for h in range(2):
                    PS_ = slice(h * 64, (h + 1) * 64)
                    S.op('vector', lambda e, PS_=PS_: e.tensor_copy(out=U0_bd[PS_, :, PS_], in_=Xbf[PS_, :, 64:128]),
                         reads=[t_Xbf], writes=[t_U0])
            TpT, t_TpT = AR.alloc([128, NCH, 128], BF16, 'TpT')
            g = per_chunk_mm(128, mm1(lambda q: Ah_bd[:, q, :], lambda q: btk[:, q, :], [t_Ah, t_btk]), None)
            evac(g, 128, lambda e, src, q0, nq: e.tensor_copy(out=TpT[:, q0:q0 + nq, :], in_=src), [], [t_TpT])
            G0p, t_G0 = AR.alloc([128, NCH, 64], F32, 'G0p')
            pc3 = e1[:, :].rearrange("p (q t) -> p q t", t=64)[:, :, 63:64]

            def g0mm(q, out, bi):
                S.op('tensor', lambda e, q=q, out=out: e.matmul(out, btk[:, q, :], Xbf[:, q, 64:128], start=True, stop=False),
                     reads=[t_btk, t_Xbf], writes=[t_ps[bi]])
                S.op('tensor', lambda e, q=q, out=out: e.matmul(out, ktk[:, q, :], vtk_c[:, q, :], start=False, stop=True),
                     reads=[t_ktk, t_vtkc], writes=[t_ps[bi]])
            g = per_chunk_mm(64, g0mm, None)
            evac(g, 64, lambda e, src, q0, nq: e.tensor_tensor(out=G0p[:, q0:q0 + nq, :], in0=src,
                                                               in1=pc3[:, q0:q0 + nq, :].to_broadcast([128, nq, 64]),
                                                               op=ALU.mult), [t_e1], [t_G0])
            RhT = t_RhT = None
            if main:
                RhT, t_RhT = AR.alloc([128, NCH, 64], BF16, 'RhT')
                g = per_chunk_mm(64, mm1(lambda q: Ah_bd[:, q, :], lambda q: ArbT[:, q, :], [t_Ah, t_ArbT]), None)
                evac(g, 64, lambda e, src, q0, nq: e.tensor_tensor(out=RhT[:, q0:q0 + nq, :], in0=src,
                                                                   in1=rt_c[:, q0:q0 + nq, :], op=ALU.add), [t_rt], [t_RhT])
            tmp, t_tmp = AR.alloc([128, 64], F32, 'chtmp')
            if main:
                for h in range(2):
                    PS_ = slice(h * 64, (h + 1) * 64)
                    S.op('vector', lambda e, PS_=PS_: e.tensor_copy(out=Hbd[PS_, hp, PS_], in_=H32[PS_, hp, :]),
                         reads=[t_H[hp]], writes=[t_H[hp]])
            for q in range(NCH):
                if main:
                    by = bank()
                    S.op('tensor', lambda e, q=q, by=by: e.matmul(ps[by][:, 0:64], U0_bd[:, q, :], ArbT[:, q, :],
                                                                  start=True, stop=False),
                         reads=[t_U0, t_ArbT], writes=[t_ps[by]])
                    S.op('tensor', lambda e, q=q, by=by: e.matmul(ps[by][:, 0:64], vtk_bd[:, q, :], ArkT[:, q, :],
                                                                  start=False, stop=False),
                         reads=[t_vtkbd, t_ArkT], writes=[t_ps[by]])
                    S.op('tensor', lambda e, q=q, by=by: e.matmul(ps[by][:, 0:64], Hbd[:, hp, :], RhT[:, q, :],
                                                                  start=False, stop=True),
                         reads=[t_H[hp], t_RhT], writes=[t_ps[by]])
                    S.op('scalar', lambda e, q=q, by=by: e.activation(out=y[:, q * 64:(q + 1) * 64], in_=ps[by][:, 0:64],
                                                                      func=AF.Copy), reads=[t_ps[by]], writes=[t_y])
                bc = bank()
                S.op('tensor', lambda e, q=q, bc=bc: e.matmul(ps[bc][:, 0:64], TpT[:, q, :], Hbf[:, hp, :],
                                                              start=True, stop=True),
                     reads=[t_TpT, t_H[hp]], writes=[t_ps[bc]])
                S.op('vector', lambda e, bc=bc: e.tensor_tensor(out=tmp[:, :], in0=ps[bc][:, 0:64], in1=H32[:, hp, :],
                                                                op=ALU.add), reads=[t_ps[bc], t_H[hp]], writes=[t_tmp])
                S.op('vector', lambda e, q=q: e.scalar_tensor_tensor(out=H32[:, hp, :], in0=tmp[:, :],
                                                                     scalar=e1[:, q * 64 + 63:q * 64 + 64], in1=G0p[:, q, :],
                                                                     op0=ALU.mult, op1=ALU.add),
                     reads=[t_tmp, t_e1, t_G0], writes=[t_H[hp]])
                S.op('scalar', lambda e: e.activation(out=Hbf[:, hp, :], in_=H32[:, hp, :], func=AF.Copy),
                     reads=[t_H[hp]], writes=[t_H[hp]])
                if main:
                    for h in range(2):
                        PS_ = slice(h * 64, (h + 1) * 64)
                        S.op('vector', lambda e, PS_=PS_: e.tensor_copy(out=Hbd[PS_, hp, PS_], in_=H32[PS_, hp, :]),
                             reads=[t_H[hp]], writes=[t_H[hp]])

        def sample_step(hp, r, t_r, k2, t_k2, v, t_v, kk, t_kk, bb, t_bb, lw, t_lw, y, t_y):
            SC = slice(TP, NOS)
            Sst, t_S = AR.alloc([128, NS, 64], F32, 'Sst')
            with nc.allow_non_contiguous_dma(reason="state"):
                pass
            S.dma(hwq(), Sst[:, :, :], srwkv[:, 2 * hp:2 * hp + 2, :, :].rearrange("n h i j -> (h i) n j"), writes=[t_S])
            dec, t_dec = AR.alloc([128, NS], F32, 'dec')
            S.op('scalar', lambda e: e.activation(out=dec[:, :], in_=lw[:, SC], func=AF.Exp), reads=[t_lw], writes=[t_dec])
            nkk, t_nkk = AR.alloc([128, NS], F32, 'nkk')
            S.op('vector', lambda e: e.tensor_scalar(out=nkk[:, :], in0=kk[:, SC], scalar1=-1.0, scalar2=None, op0=ALU.mult),
                 reads=[t_kk], writes=[t_nkk])
            bc = {}
            dgc = [None]
            for nm, (src, t_src) in {'w': (dec[:, :], t_dec), 'nkk': (nkk[:, :], t_nkk), 'b': (bb[:, SC], t_bb),
                                     'k2': (k2[:, SC], t_k2), 'r': (r[:, SC], t_r)}.items():
                if dgc[0] is None:
                    dgc[0] = AR.alloc([128, NS, 64], F32, 'dg')
                dg, t_dg = dgc[0]
                S.op('vector', lambda e, src=src, dg=dg: e.tensor_tensor(
                    out=dg[:, :, :], in0=src.unsqueeze(2).to_broadcast([128, NS, 64]),
                    in1=i2[:, :].unsqueeze(1).to_broadcast([128, NS, 64]), op=ALU.mult), reads=[t_src, t_c], writes=[t_dg])
                xb, t_xb = AR.alloc([128, NS, 64], F32, 'xb' + nm)
                for hf in range(2):
                    bi = bank()
                    S.op('tensor', lambda e, bi=bi, dg=dg, hf=hf: e.matmul(
                        ps[bi][:, 0:512], bones[:, :], dg[:, hf * 8:(hf + 1) * 8, :].rearrange("p a b -> p (a b)"),
                        start=True, stop=True), reads=[t_c, t_dg], writes=[t_ps[bi]])
                    S.op('scalar', lambda e, bi=bi, xb=xb, hf=hf: e.activation(
                        out=xb[:, hf * 8:(hf + 1) * 8, :].rearrange("p a b -> p (a b)"), in_=ps[bi][:, 0:512], func=AF.Copy),
                        reads=[t_ps[bi]], writes=[t_xb])
                bc[nm] = (xb, t_xb)
            t1, t_t1 = AR.alloc([128, NS, 64], F32, 't1')
            sa, t_sa = AR.alloc([128, NS], F32, 'sa')
            S.op('vector', lambda e: e.tensor_tensor(out=t1[:, :, :], in0=Sst[:, :, :], in1=bc['nkk'][0][:, :, :], op=ALU.mult),
                 reads=[t_S, bc['nkk'][1]], writes=[t_t1])
            S.op('vector', lambda e: e.tensor_reduce(out=sa[:, :], in_=t1[:, :, :], axis=AX.X, op=ALU.add),
                 reads=[t_t1], writes=[t_sa])
            S.op('vector', lambda e: e.tensor_tensor(out=Sst[:, :, :], in0=Sst[:, :, :], in1=bc['w'][0][:, :, :], op=ALU.mult),
                 reads=[t_S, bc['w'][1]], writes=[t_S])
            S.op('vector', lambda e: e.tensor_tensor(out=t1[:, :, :], in0=bc['b'][0][:, :, :],
                                                     in1=sa[:, :].unsqueeze(2).to_broadcast([128, NS, 64]), op=ALU.mult),
                 reads=[t_sa, bc['b'][1]], writes=[t_t1])
            S.op('vector', lambda e: e.tensor_tensor(out=Sst[:, :, :], in0=Sst[:, :, :], in1=t1[:, :, :], op=ALU.add),
                 reads=[t_S, t_t1], writes=[t_S])
            S.op('vector', lambda e: e.tensor_tensor(out=t1[:, :, :], in0=bc['k2'][0][:, :, :],
                                                     in1=v[:, SC].unsqueeze(2).to_broadcast([128, NS, 64]), op=ALU.mult),
                 reads=[t_v, bc['k2'][1]], writes=[t_t1])
            S.op('vector', lambda e: e.tensor_tensor(out=Sst[:, :, :], in0=Sst[:, :, :], in1=t1[:, :, :], op=ALU.add),
                 reads=[t_S, t_t1], writes=[t_S])
            S.op('vector', lambda e: e.tensor_tensor(out=t1[:, :, :], in0=Sst[:, :, :], in1=bc['r'][0][:, :, :], op=ALU.mult),
                 reads=[t_S, bc['r'][1]], writes=[t_t1])
            S.op('vector', lambda e: e.tensor_reduce(out=y[:, SC], in_=t1[:, :, :], axis=AX.X, op=ALU.add),
                 reads=[t_t1], writes=[t_y])
            S.dma(hwq(), o_rwkvs[:, 2 * hp:2 * hp + 2, :, :].rearrange("n h i j -> (h i) n j"), Sst[:, :, :],
                  reads=[t_S], writes=[T()])

        def pool_stage(p, last):
            AR.reset()
            mp = p - NPH
            posb, t_posb = AR.alloc([128, TP], F32, 'posb')
            S.dma('sync', posb[:, :], pos[0:1, mp * TP:(mp + 1) * TP].partition_broadcast(128).rearrange("p o n -> p (o n)"),
                  writes=[t_posb])
            rc, t_rc = AR.alloc([128, 4, TP], F32, 'rc')
            for gi in range(4):
                win = float(2 ** (gi + 1))
                S.op('vector', lambda e, gi=gi, win=win: e.tensor_scalar(out=rc[:, gi, :], in0=posb[:, :], scalar1=1.0,
                                                                        scalar2=win, op0=ALU.add, op1=ALU.min),
                     reads=[t_posb], writes=[t_rc])
            S.op('vector', lambda e: e.reciprocal(out=rc[:, :, :], in_=rc[:, :, :]), reads=[t_rc], writes=[t_rc])
            base_off = AR.off
            base_tiles = list(AR.tiles)
            for gi in range(4):
                keep = AR.tiles[:len(base_tiles)]
                AR.reset()
                AR.off = base_off
                AR.tiles = keep
                win = 2 ** (gi + 1)
                pooled, t_pl = AR.alloc([128, 4, W], BF16, 'pooled')
                for ct in range(4):
                    tix = gi * 4 + ct
                    wv, wt, KC = inproj(C_U + tix * 128)
                    bi = proj(wv, wt, KC, 0, 128, xT, [t_xT], slice(0, W))
                    u, t_u = AR.alloc([128, W], F32, 'u')
                    S.op('scalar', lambda e, bi=bi, u=u: e.activation(out=u[:, :], in_=ps[bi][:, 0:W], func=AF.Copy),
                         reads=[t_ps[bi]], writes=[t_u])
                    if last:
                        tail_out(u, t_u, o_tailu, tix * 128)
                    s_prev, t_sp = u, t_u
                    step = 1
                    while step < win:
                        s_new, t_sn = AR.alloc([128, W], F32, 's')
                        lo = 2 * step - 1
                        S.op('vector', lambda e, s_prev=s_prev, s_new=s_new, step=step, lo=lo:
                             e.tensor_tensor(out=s_new[:, lo:HALO + TP], in0=s_prev[:, lo:HALO + TP],
                                             in1=s_prev[:, lo - step:HALO + TP - step], op=ALU.add),
                             reads=[t_sp], writes=[t_sn])
                        s_prev, t_sp = s_new, t_sn
                        step *= 2
                    tmpw, t_tw = AR.alloc([128, TP], F32, 'tmpw')
                    S.op('vector', lambda e, s_prev=s_prev, tmpw=tmpw, gi=gi: e.tensor_tensor(
                        out=tmpw[:, :], in0=s_prev[:, OWN], in1=rc[:, gi, :], op=ALU.mult), reads=[t_sp, t_rc], writes=[t_tw])
                    S.op('vector', lambda e, tmpw=tmpw, u=u, ct=ct, pooled=pooled: e.tensor_tensor(
                        out=pooled[:, ct, OWN], in0=tmpw[:, :], in1=u[:, OWN], op=ALU.subtract),
                        reads=[t_tw, t_u], writes=[t_pl])
                    if last:
                        stk, t_stk = AR.alloc([128, 2, 128], F32, 'pstk')
                        sv_ = spool.rearrange("n t c -> (n t) c")
                        S.dma(hwq(), stk[:, 0, :], sv_[0:128, tix * 128:(tix + 1) * 128], writes=[t_stk])
                        S.dma(hwq(), stk[0:112, 1, :], sv_[128:240, tix * 128:(tix + 1) * 128], writes=[t_stk])
                        pT, t_pT = AR.alloc([128, 256], F32, 'pT')
                        bi = bank()
                        S.op('tensor', lambda e, bi=bi, stk=stk: e.transpose(ps[bi][:, 0:128], stk[:, 0, :], ident[:, :]),
                             reads=[t_stk, t_c], writes=[t_ps[bi]])
                        S.op('tensor', lambda e, bi=bi, stk=stk: e.transpose(ps[bi][:, 128:240], stk[0:112, 1, :],
                                                                             ident[0:112, 0:112]),
                             reads=[t_stk, t_c], writes=[t_ps[bi]])
                        S.op('vector', lambda e, bi=bi, pT=pT: e.tensor_copy(out=pT[:, 0:240], in_=ps[bi][:, 0:240]),
                             reads=[t_ps[bi]], writes=[t_pT])
                        ws, t_ws = AR.alloc([128, NS], F32, 'ws')
                        pT3 = pT[:, 0:240].rearrange("p (n t) -> p n t", t=15)
                        S.op('vector', lambda e, pT3=pT3, ws=ws, win=win: e.tensor_reduce(
                            out=ws[:, :], in_=pT3[:, :, 15 - (win - 1):15], axis=AX.X, op=ALU.add), reads=[t_pT], writes=[t_ws])
                        S.op('vector', lambda e, ws=ws, u=u: e.tensor_tensor(out=ws[:, :], in0=ws[:, :], in1=u[:, SMP], op=ALU.add),
                             reads=[t_ws, t_u], writes=[t_ws])
                        S.op('vector', lambda e, ws=ws, u=u, ct=ct, pooled=pooled, win=win: e.scalar_tensor_tensor(
                            out=pooled[:, ct, SMP], in0=ws[:, :], scalar=1.0 / win, in1=u[:, SMP], op0=ALU.mult,
                            op1=ALU.subtract), reads=[t_ws, t_u], writes=[t_pl])
                pwv = pwb[:, :].rearrange("p (a b) -> p a b", b=512)
                pwt, pKC = t_pwb, 4
                S.dma('gpsimd', pwv, pool_w[gi].rearrange("(c p) n -> p c n", p=128), writes=[t_pwb])
                for dc in range(4):
                    tix = gi * 4 + dc
                    wv, wt, KC = inproj(C_ZP + tix * 128)
                    bz = proj(wv, wt, KC, 0, 128, xT, [t_xT], slice(0, W))
                    sz, t_sz = AR.alloc([128, NOS], F32, 'szp')
                    S.op('scalar', lambda e, bz=bz, sz=sz: e.activation(out=sz[:, :], in_=ps[bz][:, HALO:HALO + NOS], func=AF.Silu),
                         reads=[t_ps[bz]], writes=[t_sz])
                    bm = proj(pwv, pwt, pKC, dc * 128, (dc + 1) * 128, pooled, [t_pl], OS)
                    S.op('vector', lambda e, bm=bm, sz=sz, tix=tix: e.scalar_tensor_tensor(
                        out=bo[:, tix, OS], in0=ps[bm][:, 0:NOS], scalar=prm['pscale'][:, tix:tix + 1], in1=sz[:, :],
                        op0=ALU.mult, op1=ALU.mult), reads=[t_ps[bm], t_sz, t_c], writes=[t_bo[tix]])
            if last:
                S.dma('sync', o_pools[:, :, :], spool[:, 1:15, :], writes=[T()])

        def mem_stage(p, last):
            AR.reset()
            qT, t_qT = AR.alloc([128, 12, W], BF16, 'qT')
            for c in range(12):
                wv, wt, KC = inproj(C_Q + c * 128)
                bi = proj(wv, wt, KC, 0, 128, xT, [t_xT], slice(0, W))
                S.op('scalar', lambda e, bi=bi, c=c: e.activation(out=qT[:, c, :], in_=ps[bi][:, 0:W], func=AF.Copy,
                                                                  scale=float(MHD) ** -0.5), reads=[t_ps[bi]], writes=[t_qT])
            szs = t_szs = None
            if last:
                szs, t_szs = AR.alloc([128, 12, NS], F32, 'szs')
            base_off = AR.off
            base_tiles = list(AR.tiles)
            for h in range(4):
                keep = AR.tiles[:len(base_tiles)]
                AR.reset()
                AR.off = base_off
                AR.tiles = keep
                eT, t_eT = AR.alloc([128, 2, W], BF16, 'eT')
                for mt in range(2):
                    bi = bank()
                    for dt in range(3):
                        S.op('tensor', lambda e, bi=bi, mt=mt, dt=dt, h=h: e.matmul(
                            ps[bi][:, 0:W], KT[:, h * 3 + dt, mt * 128:(mt + 1) * 128], qT[:, h * 3 + dt, :],
                            start=(dt == 0), stop=(dt == 2)), reads=[t_KT, t_qT], writes=[t_ps[bi]])
                    S.op('scalar', lambda e, bi=bi, mt=mt, eT=eT: e.activation(out=eT[:, mt, :], in_=ps[bi][:, 0:W], func=AF.Exp),
                         reads=[t_ps[bi]], writes=[t_eT])
                bd_ = bank()
                for mt in range(2):
                    S.op('tensor', lambda e, bd_=bd_, mt=mt, eT=eT: e.matmul(ps[bd_][:, 0:W], onesb[:, :], eT[:, mt, :],
                                                                             start=(mt == 0), stop=(mt == 1)),
                         reads=[t_c, t_eT], writes=[t_ps[bd_]])
                rden, t_rden = AR.alloc([128, W], F32, 'rden')
                S.op('vector', lambda e, bd_=bd_, rden=rden: e.reciprocal(out=rden[:, :], in_=ps[bd_][:, 0:W]),
                     reads=[t_ps[bd_]], writes=[t_rden])
                for dt in range(3):
                    c = h * 3 + dt
                    wv, wt, KC = inproj(C_ZM + c * 128)
                    bz = proj(wv, wt, KC, 0, 128, xT, [t_xT], slice(0, W))
                    sz, t_sz = AR.alloc([128, W], F32, 'szm')
                    S.op('scalar', lambda e, bz=bz, sz=sz: e.activation(out=sz[:, :], in_=ps[bz][:, 0:W], func=AF.Silu),
                         reads=[t_ps[bz]], writes=[t_sz])
                    if last:
                        S.op('vector', lambda e, sz=sz, c=c: e.tensor_copy(out=szs[:, c, :], in_=sz[:, SMP]),
                             reads=[t_sz], writes=[t_szs])
                    bo_ = bank()
                    for mt in range(2):
                        S.op('tensor', lambda e, bo_=bo_, mt=mt, c=c, eT=eT: e.matmul(
                            ps[bo_][:, 0:W], Vb[:, mt, c * 128:(c + 1) * 128], eT[:, mt, :], start=(mt == 0), stop=(mt == 1)),
                            reads=[t_Vb, t_eT], writes=[t_ps[bo_]])
                    tmp, t_tmp = AR.alloc([128, W], F32, 'otmp')
                    S.op('vector', lambda e, bo_=bo_, tmp=tmp, rden=rden: e.tensor_tensor(
                        out=tmp[:, :], in0=ps[bo_][:, 0:W], in1=rden[:, :], op=ALU.mult), reads=[t_ps[bo_], t_rden], writes=[t_tmp])
                    S.op('vector', lambda e, tmp=tmp, sz=sz, c=c: e.tensor_tensor(
                        out=bo[:, c, :], in0=tmp[:, :], in1=sz[:, :], op=ALU.mult), reads=[t_tmp, t_sz], writes=[t_bo[c]])
            if last:
                keep = AR.tiles[:len(base_tiles)]
                AR.reset()
                AR.off = base_off
                AR.tiles = keep
                sample_attn(qT, t_qT, szs, t_szs)

        def sample_attn(qT, t_qT, szs, t_szs):
            qtok, t_qtok = AR.alloc([NS, MW], F32, 'qtok')
            for g in range(3):
                bi = bank()
                for j in range(4):
                    c = g * 4 + j
                    S.op('tensor', lambda e, bi=bi, j=j, c=c: e.transpose(psb[bi][0:NS, j * 128:(j + 1) * 128], qT[:, c, SMP],
                                                                         identb[:, :]), reads=[t_qT, t_c], writes=[t_ps[bi]])
                S.op('vector', lambda e, bi=bi, g=g: e.tensor_copy(out=qtok[:, g * 512:(g + 1) * 512], in_=psb[bi][0:NS, 0:512]),
                     reads=[t_ps[bi]], writes=[t_qtok])
            E, t_E = AR.alloc([128, NS, 2, 4], F32, 'E')
            oT, t_oT = AR.alloc([128, 12, NS], F32, 'oT')
            base_off = AR.off
            base_tiles = list(AR.tiles)
            for n in range(NS):
                keep = AR.tiles[:len(base_tiles)]
                AR.reset()
                AR.off = base_off
                AR.tiles = keep
                Kn, t_Kn = AR.alloc([128, 2, MW], F32, 'Kn')
                S.dma(hwq(), Kn[:, :, :], ck[n].rearrange("(a p) d -> p a d", p=128), writes=[t_Kn])
                Vn, t_Vn = AR.alloc([128, 2, MW], F32, 'Vn')
                S.dma(hwq(), Vn[:, :, :], cv[n].rearrange("(a p) d -> p a d", p=128), writes=[t_Vn])
                bq = [bank(), bank(), bank()]
                for g in range(3):
                    S.op('tensor', lambda e, g=g, n=n: e.matmul(pst[:, g * 512:(g + 1) * 512], sel[:, n * 128:(n + 1) * 128],
                                                                qtok[:, g * 512:(g + 1) * 512], start=True, stop=True),
                         reads=[t_c, t_qtok], writes=[t_ps[g]])
                S.op('vector', lambda e, Kn=Kn: e.tensor_tensor(
                    out=Kn[:, :, :], in0=Kn[:, :, :], in1=pst[:, 0:MW].unsqueeze(1).to_broadcast([128, 2, MW]), op=ALU.mult),
                    reads=[t_Kn, t_ps[0], t_ps[1], t_ps[2]], writes=[t_Kn])
                sc, t_sc = AR.alloc([128, 2, 4], F32, 'sc')
                S.op('vector', lambda e, Kn=Kn, sc=sc: e.tensor_reduce(
                    out=sc[:, :, :], in_=Kn[:, :, :].rearrange("p a (h d) -> p a h d", d=MHD), axis=AX.X, op=ALU.add),
                    reads=[t_Kn], writes=[t_sc])
                S.op('scalar', lambda e, sc=sc, n=n: e.activation(out=E[:, n, :, :], in_=sc[:, :, :], func=AF.Exp),
                     reads=[t_sc], writes=[t_E])
                bo_ = bank()
                for c in range(12):
                    h = c // 3
                    for mt in range(2):
                        S.op('tensor', lambda e, bo_=bo_, c=c, h=h, mt=mt, n=n, Vn=Vn: e.matmul(
                            ps[bo_][:, c:c + 1], Vn[:, mt, c * 128:(c + 1) * 128], E[:, n, mt, h:h + 1],
                            start=(mt == 0), stop=(mt == 1)), reads=[t_Vn, t_E], writes=[t_ps[bo_]])
                S.op('vector', lambda e, bo_=bo_, n=n: e.tensor_copy(out=oT[:, :, n], in_=ps[bo_][:, 0:12]),
                     reads=[t_ps[bo_]], writes=[t_oT])
            bd_ = bank()
            for mt in range(2):
                S.op('tensor', lambda e, bd_=bd_, mt=mt: e.matmul(
                    ps[bd_][:, 0:NS * 4].rearrange("p (n h) -> p n h", h=4), bones[:, :], E[:, :, mt, :],
                    start=(mt == 0), stop=False), reads=[t_c, t_E], writes=[t_ps[bd_]])
            S.op('tensor', lambda e, bd_=bd_: e.matmul(
                ps[bd_][:, 0:NS * 4].rearrange("p (n h) -> p n h", h=4), obones[:, :], E[:, :, 0, :], start=False, stop=False),
                reads=[t_c, t_E], writes=[t_ps[bd_]])
            S.op('tensor', lambda e, bd_=bd_: e.matmul(
                ps[bd_][:, 0:NS * 4].rearrange("p (n h) -> p n h", h=4), obones[:, :], E[:, :, 1, :], start=False, stop=True),
                reads=[t_c, t_E], writes=[t_ps[bd_]])
            rd, t_rd = AR.alloc([128, NS, 4], F32, 'rd')
            S.op('vector', lambda e, bd_=bd_: e.reciprocal(out=rd[:, :, :], in_=ps[bd_][:, 0:NS * 4].rearrange("p (n h) -> p n h", h=4)),
                 reads=[t_ps[bd_]], writes=[t_rd])
            for c in range(12):
                h = c // 3
                S.op('vector', lambda e, c=c, h=h: e.tensor_tensor(out=oT[:, c, :], in0=oT[:, c, :], in1=rd[:, :, h], op=ALU.mult),
                     reads=[t_oT, t_rd], writes=[t_oT])
                S.op('vector', lambda e, c=c: e.tensor_tensor(out=bo[:, c, SMP], in0=oT[:, c, :], in1=szs[:, c, :], op=ALU.mult),
                     reads=[t_oT, t_szs], writes=[t_bo[c]])

        obones = sb("obones", [128, 128])
        if int(os.environ.get('KB', '99')) >= 8:
          S.op('vector', lambda e: e.tensor_scalar(out=obones[:], in0=bones[:], scalar1=-1.0, scalar2=1.0, op0=ALU.mult,
                                                 op1=ALU.add), reads=[t_c], writes=[t_c])

        def final_stage(p, last):
            AR.reset()
            dump(hT, 32, t_hT, 44)
            AR.reset()
            mp = p - NPH
            ntt = TP // 128
            vt = [t_vscr[mp * ntt + i] for i in range(ntt)]
            for g4 in range(8):
                subT, t_sub = AR.alloc([128, 4, W], F32, 'subT')
                for j in range(4):
                    c = g4 * 4 + j
                    wv, wt, kp, KC = wload(w_out[:, c * 128:(c + 1) * 128], D, 128)
                    bi = proj(wv, wt, KC, 0, 128, hT, t_hT, slice(0, W))
                    S.op('scalar' if j % 2 else 'vector',
                         (lambda e, bi=bi, j=j, subT=subT: e.activation(out=subT[:, j, :], in_=ps[bi][:, 0:W], func=AF.Copy))
                         if j % 2 else
                         (lambda e, bi=bi, j=j, subT=subT: e.tensor_copy(out=subT[:, j, :], in_=ps[bi][:, 0:W])),
                         reads=[t_ps[bi]], writes=[t_sub])
                tiles = [(HALO + i * 128, 128, y_own[mp * TP + i * 128: mp * TP + (i + 1) * 128, :], vt[i]) for i in range(ntt)]
                if last:
                    tiles.append((HALO + TP, NS, y_s[:, :], t_vscr[-1]))
                for (c0, nr, dst, tv) in tiles:
                    bi = bank()
                    for j in range(4):
                        S.op('tensor', lambda e, bi=bi, j=j, c0=c0, nr=nr, subT=subT: e.transpose(
                            ps[bi][0:nr, j * 128:(j + 1) * 128], subT[:, j, c0:c0 + nr], ident[:, :]),
                            reads=[t_sub, t_c], writes=[t_ps[bi]])
                    stg, t_stg = AR.alloc([128, 512], F32, 'stg')
                    S.op('vector', lambda e, bi=bi, nr=nr, stg=stg: e.tensor_copy(out=stg[0:nr, :], in_=ps[bi][0:nr, 0:512]),
                         reads=[t_ps[bi]], writes=[t_stg])
                    S.dma(hwq(), dst[:, g4 * 512:(g4 + 1) * 512], stg[0:nr, :], reads=[t_stg], writes=[tv])
                if g4 % 2 == 1:
                    AR.reset()
            chk(8)
            AR.reset()
            gb, t_gb = AR.alloc([128, 2, D], F32, 'gb')
            S.dma('sync', gb[:, 0, :], ln_g[0:1, :].partition_broadcast(128).rearrange("p o n -> p (o n)"), writes=[t_gb])
            S.dma('scalar', gb[:, 1, :], ln_b[0:1, :].partition_broadcast(128).rearrange("p o n -> p (o n)"), writes=[t_gb])
            base_off = AR.off
            base_tiles = list(AR.tiles)
            tiles = [(128, y_own[mp * TP + i * 128: mp * TP + (i + 1) * 128, :], vt[i],
                      xin[HALO + p * TP + i * 128: HALO + p * TP + (i + 1) * 128, :]) for i in range(ntt)]
            if last:
                tiles.append((NS, y_s[:, :], t_vscr[-1], xs[:, :]))
            for (nr, dst, tv, xsrc) in tiles:
                keep = AR.tiles[:len(base_tiles)]
                AR.reset()
                AR.off = base_off
                AR.tiles = keep
                vrow, t_vr = AR.alloc([128, D], F32, 'vrow')
                S.dma('sync', vrow[0:nr, :], dst, reads=[tv], writes=[t_vr])
                for hlf in range(2):
                    xrow, t_xr = AR.alloc([128, 2048], F32, 'xrow2')
                    S.dma('scalar', xrow[0:nr, :], xsrc[:, hlf * 2048:(hlf + 1) * 2048], writes=[t_xr])
                    S.op('vector', lambda e, nr=nr, vrow=vrow, xrow=xrow, hlf=hlf: e.scalar_tensor_tensor(
                        out=vrow[0:nr, hlf * 2048:(hlf + 1) * 2048], in0=xrow[0:nr, :], scalar=ALPHA,
                        in1=vrow[0:nr, hlf * 2048:(hlf + 1) * 2048], op0=ALU.mult, op1=ALU.add),
                        reads=[t_xr, t_vr], writes=[t_vr])
                stt, t_stt = AR.alloc([128, 8, 6], F32, 'stt')
                for j in range(8):
                    S.op('vector', lambda e, nr=nr, vrow=vrow, stt=stt, j=j: e.bn_stats(out=stt[0:nr, j, :],
                                                                                      in_=vrow[0:nr, j * 512:(j + 1) * 512]),
                         reads=[t_vr], writes=[t_stt])
                mv, t_mv = AR.alloc([128, 4], F32, 'mv')
                S.op('vector', lambda e, nr=nr, stt=stt, mv=mv: e.bn_aggr(out=mv[0:nr, 0:2],
                                                                         in_=stt[0:nr, :, :].rearrange("p a b -> p (a b)")),
                     reads=[t_stt], writes=[t_mv])
                S.op('vector', lambda e, nr=nr, mv=mv: e.tensor_scalar(out=mv[0:nr, 2:3], in0=mv[0:nr, 1:2], scalar1=LN_EPS,
                                                                       scalar2=None, op0=ALU.add), reads=[t_mv], writes=[t_mv])
                S.op('scalar', lambda e, nr=nr, mv=mv: e.activation(out=mv[0:nr, 2:3], in_=mv[0:nr, 2:3], func=AF.Sqrt),
                     reads=[t_mv], writes=[t_mv])
                S.op('vector', lambda e, nr=nr, mv=mv: e.reciprocal(out=mv[0:nr, 2:3], in_=mv[0:nr, 2:3]),
                     reads=[t_mv], writes=[t_mv])
                S.op('vector', lambda e, nr=nr, vrow=vrow, mv=mv: e.tensor_scalar(
                    out=vrow[0:nr, :], in0=vrow[0:nr, :], scalar1=mv[0:nr, 0:1], scalar2=mv[0:nr, 2:3], op0=ALU.subtract,
                    op1=ALU.mult), reads=[t_vr, t_mv], writes=[t_vr])
                S.op('vector', lambda e, nr=nr, vrow=vrow: e.tensor_tensor(out=vrow[0:nr, :], in0=vrow[0:nr, :],
                                                                           in1=gb[0:nr, 0, :], op=ALU.mult),
                     reads=[t_vr, t_gb], writes=[t_vr])
                S.op('vector', lambda e, nr=nr, vrow=vrow: e.tensor_tensor(out=vrow[0:nr, :], in0=vrow[0:nr, :],
                                                                           in1=gb[0:nr, 1, :], op=ALU.add),
                     reads=[t_vr, t_gb], writes=[t_vr])
                S.dma('sync', dst, vrow[0:nr, :], reads=[t_vr], writes=[tv])

        KSTOP = int(os.environ.get('KSTOP', '99'))

        class StopBuild(Exception):
            pass

        def chk(level):
            if KSTOP <= level:
                raise StopBuild()
        stopped = False
        try:
            chk(1)
            for p in range(NPASS):
                run_pass(p)
        except StopBuild:
            stopped = True
        AR.reset()
        for hp in range(16 if not stopped else 0):
            bi = bank()
            S.op('tensor', lambda e, bi=bi, hp=hp: e.transpose(ps[bi][0:64, 0:128], H32[:, hp, :], ident[:, :]),
                 reads=[t_H[hp], t_c], writes=[t_ps[bi]])
            ho, t_ho = AR.alloc([64, 128], F32, 'ho')
            S.op('vector', lambda e, bi=bi, ho=ho: e.tensor_copy(out=ho[:, :], in_=ps[bi][0:64, 0:128]),
                 reads=[t_ps[bi]], writes=[t_ho])
            if not os.environ.get('KNOD'):
                S.dma('sync', o_rwkvp[2 * hp:2 * hp + 2, :, :].rearrange("h i j -> i h j"),
                      ho[:, :].rearrange("p (h j) -> p h j", j=64), reads=[t_ho], writes=[T()])
        S.finish()
        print("ninstr", S.ninstr, {e: len(S.prog[e]) for e in ENGS})
        S.emit(block)
    return nc


_CACHE = {}


def kernel(**inp):
    x_prompt = np.asarray(inp['x_prompt'], np.float32)
    B, SEQ, _ = x_prompt.shape
    T_OWN = SEQ // 2
    TP = min(256, T_OWN)
    key = (T_OWN, TP)
    if key not in _CACHE:
        _CACHE[key] = build(T_OWN, TP)
    nc = _CACHE[key]
    consts = make_consts(TP)
    f = lambda a: np.ascontiguousarray(np.asarray(a, np.float32))
    shared = {}
    for nm in ['w_in', 'b_gate', 'pool_w', 'pool_scale', 'rwkv_mu', 'rwkv_w0', 'rwkv_w2', 'rwkv_a0', 'rwkv_a2', 'rwkv_k_k',
               'rwkv_k_a', 'rwkv_r_k', 'rwkv_ln_w', 'rwkv_ln_b', 'w_mem_kv', 'w_branch_pool', 'w_branch_rwkv',
               'w_branch_mem', 'w_out']:
        a = f(inp[nm])[0]
        if nm == 'rwkv_r_k':
            a = a.reshape(-1)
        shared[nm] = np.ascontiguousarray(a)
    shared['ln_g'] = f(inp['ln_g'])[0].reshape(1, D)
    shared['ln_b'] = f(inp['ln_b'])[0].reshape(1, D)
    for k_, v_ in consts.items():
        shared['c_' + k_] = v_
    in_maps = []
    for c in range(8):
        s, hf = c // 2, c % 2
        xin = np.zeros((HALO + 2 * T_OWN, D), np.float32)
        if hf == 0:
            xin[HALO + T_OWN:] = x_prompt[s, 0:T_OWN]
        else:
            xin[HALO:] = x_prompt[s]
        m = dict(shared)
        m['xin'] = xin
        sl = slice(c * NS, (c + 1) * NS)
        m['xs'] = f(inp['x_sample'])[sl, 0]
        m['pos'] = (np.arange(T_OWN, dtype=np.float32) + hf * T_OWN).reshape(1, T_OWN)
        m['memx'] = f(inp['mem_prompt'])[s]
        m['ck'] = f(inp['cache_mem_k'])[0, sl].reshape(NS, NMEM, MW)
        m['cv'] = f(inp['cache_mem_v'])[0, sl].reshape(NS, NMEM, MW)
        m['spool'] = f(inp['state_pool'])[0, sl]
        m['sshift'] = f(inp['state_shift'])[0, sl, 0]
        m['srwkv'] = f(inp['state_rwkv'])[0, sl]
        in_maps.append({k_: np.ascontiguousarray(v_) for k_, v_ in m.items()})
    kc = os.environ.get('KCORES')
    if kc is not None:
        sel_ = [int(x) for x in kc.split(',')]
        res = run_bass_kernel_spmd(nc, [in_maps[i] for i in sel_], core_ids=list(range(len(sel_))), trace=bool(os.environ.get('KTRACE')))
        print('exec_time_ns', getattr(res, 'exec_time_ns', None))
        return {sel_[i]: res.results[i] for i in range(len(sel_))}
    res = run_bass_kernel_spmd(nc, in_maps, core_ids=list(range(8)))
    R = res.results
    DEC = 8 * NS
    yp = np.zeros((B, SEQ, D), np.float32)
    ys = np.zeros((DEC, 1, D), np.float32)
    mk = np.zeros((1, B, NMEM, 4, MHD), np.float32)
    mv = np.zeros((1, B, NMEM, 4, MHD), np.float32)
    pp = np.zeros((1, B, 15, PW), np.float32)
    shp = np.zeros((1, B, 1, SHW), np.float32)
    sp = np.zeros((1, B, 32, 64, 64), np.float32)
    psm = np.zeros((1, DEC, 15, PW), np.float32)
    shs = np.zeros((1, DEC, 1, SHW), np.float32)
    ss = np.zeros((1, DEC, 32, 64, 64), np.float32)
    for c in range(8):
        s, hf = c // 2, c % 2
        r = R[c]
        yp[s, hf * T_OWN:(hf + 1) * T_OWN] = r['y_own']
        sl = slice(c * NS, (c + 1) * NS)
        ys[sl, 0] = r['y_s']
        psm[0, sl, 0:14] = r['o_pools']
        psm[0, sl, 14] = r['o_tailu'][16:32]
        shs[0, sl, 0] = r['o_tailsh'][16:32]
        ss[0, sl] = r['o_rwkvs']
        if hf == 0:
            mk[0, s] = r['o_mk'].reshape(NMEM, 4, MHD)
            mv[0, s] = r['o_mv'].reshape(NMEM, 4, MHD)
        else:
            pp[0, s] = r['o_tailu'][1:16]
            shp[0, s, 0] = r['o_tailsh'][15]
            sp[0, s] = r['o_rwkvp']
    return (yp, ys, mk, mv, pp, shp, sp, psm, shs, ss)
```
